# Optimizing a Trainium2 kernel written in Bass

```python
import math
import jax, jax.numpy as jnp
from jax import lax
import numpy as np

D_MODEL = 2048
BATCH = 8
SEQ = 2048
DEPTH = 1

HEAD_DIM = 128
A_Q_HEADS = 8
A_KV_HEADS = 2
A_GROUP = A_Q_HEADS // A_KV_HEADS
A_HALF_WINDOW = 128
A_BLOCK = 128
B_PATTERNS = ((128, 1), (512, 4), (2048, 16))
B_N_GROUPS = len(B_PATTERNS)
B_HEADS_PER_GROUP = 4
B_HEADS = B_N_GROUPS * B_HEADS_PER_GROUP
B_BLOCK = 64
N_BUCKETS = 32
MAX_DISTANCE = 1024
N_BIAS_HEADS = A_Q_HEADS + B_HEADS
A_Q_W = A_Q_HEADS * HEAD_DIM
A_KV_W = A_KV_HEADS * HEAD_DIM
B_W = B_HEADS * HEAD_DIM
B_OUT_W = B_HEADS_PER_GROUP * HEAD_DIM
IN_PROJ_W = A_Q_W + 2 * A_KV_W + 3 * B_W + 2 * D_MODEL
D_FF = 11 * D_MODEL // 4
CONV_WIDTH = 3
PLE_DIM = 256
RMS_EPS = 1e-6
NEG_INF = -1e30

kernel_name = "hybrid_gated_window_dilated_encoder"


def rms_norm(x, g):
    xf = x.astype(jnp.float32)
    y = xf * lax.rsqrt(jnp.mean(xf * xf, axis=-1, keepdims=True) + RMS_EPS)
    return (y * g.astype(jnp.float32)).astype(x.dtype)


def t5_bucket(rel):
    half = N_BUCKETS // 2
    max_exact = half // 2
    n = jnp.abs(rel)
    side = jnp.where(rel > 0, half, 0)
    nf = jnp.maximum(n, 1).astype(jnp.float32)
    large = max_exact + (jnp.log(nf / max_exact) / math.log(MAX_DISTANCE / max_exact)
                         * (half - max_exact)).astype(jnp.int32)
    large = jnp.minimum(large, half - 1)
    return side + jnp.where(n < max_exact, n, large)


def band_bias(table, half_w, blk, dilation):
    rel = (jnp.arange(blk + 2 * half_w)[None, :] - half_w) - jnp.arange(blk)[:, None]
    b = table[t5_bucket(rel * dilation)]
    return jnp.transpose(b, (2, 0, 1))


def banded_attention(q, k, v, bias, half_w, blk, sink=None):
    n, hkv, g, L, hd = q.shape
    nb = -(-L // blk)
    lp = nb * blk
    kw = blk + 2 * half_w
    q = jnp.pad(q, ((0, 0), (0, 0), (0, 0), (0, lp - L), (0, 0)))
    pad_kv = ((0, 0), (0, 0), (half_w, lp - L + half_w), (0, 0))
    k = jnp.pad(k, pad_kv)
    v = jnp.pad(v, pad_kv)
    kidx = jnp.arange(nb)[:, None] * blk + jnp.arange(kw)[None, :]
    kb = jnp.take(k, kidx, axis=2)
    vb = jnp.take(v, kidx, axis=2)
    qb = q.reshape(n, hkv, g, nb, blk, hd)
    s = jnp.einsum('nhgbqd,nhbkd->nhgbqk', qb, kb).astype(jnp.float32) * (hd ** -0.5)
    s = s + bias[None, :, :, None].astype(jnp.float32)
    kpos = (kidx - half_w)[:, None, :]
    qpos = (jnp.arange(nb)[:, None] * blk + jnp.arange(blk)[None, :])[:, :, None]
    valid = (jnp.abs(kpos - qpos) <= half_w) & (kpos >= 0) & (kpos < L)
    s = jnp.where(valid, s, NEG_INF)
    m = jnp.max(s, axis=-1)
    if sink is not None:
        sk = sink.astype(jnp.float32)[None, :, :, None, None]
        m = jnp.maximum(m, sk)
    pexp = jnp.exp(s - m[..., None])
    denom = jnp.sum(pexp, axis=-1)
    if sink is not None:
        denom = denom + jnp.exp(sk - m)
    out = jnp.einsum('nhgbqk,nhbkd->nhgbqd', pexp.astype(v.dtype), vb).astype(jnp.float32)
    out = (out / denom[..., None]).astype(v.dtype)
    lse = m + jnp.log(denom)
    out = out.reshape(n, hkv, g, lp, hd)[:, :, :, :L]
    lse = lse.reshape(n, hkv, g, lp)[:, :, :, :L]
    return out, lse


def windowed_gqa(q, k, v, table, sink):
    b, s, _ = q.shape
    q = q.reshape(b, s, A_KV_HEADS, A_GROUP, HEAD_DIM).transpose(0, 2, 3, 1, 4)
    k = k.reshape(b, s, A_KV_HEADS, HEAD_DIM).transpose(0, 2, 1, 3)
    v = v.reshape(b, s, A_KV_HEADS, HEAD_DIM).transpose(0, 2, 1, 3)
    bias = band_bias(table[:, :A_Q_HEADS], A_HALF_WINDOW, A_BLOCK, 1).reshape(
        A_KV_HEADS, A_GROUP, A_BLOCK, A_BLOCK + 2 * A_HALF_WINDOW)
    out, _ = banded_attention(q, k, v, bias, A_HALF_WINDOW, A_BLOCK,
                              sink.reshape(A_KV_HEADS, A_GROUP))
    return out.transpose(0, 3, 1, 2, 4).reshape(b, s, A_Q_W)


def to_residue(t, dil):
    b, s, h, hd = t.shape
    return t.reshape(b, s // dil, dil, h, hd).transpose(0, 2, 3, 1, 4).reshape(b * dil, h, s // dil, hd)


def dilated_attention(q, k, v, table):
    b, s, _ = q.shape
    hg = B_HEADS_PER_GROUP
    qg = q.reshape(b, s, B_N_GROUPS, hg, HEAD_DIM)
    kg = k.reshape(b, s, B_N_GROUPS, hg, HEAD_DIM)
    vg = v.reshape(b, s, B_N_GROUPS, hg, HEAD_DIM)
    outs = []
    lses = []
    for gi, (window, dil) in enumerate(B_PATTERNS):
        half = window // (2 * dil)
        L = s // dil
        h0 = A_Q_HEADS + gi * hg
        bias = band_bias(table[:, h0:h0 + hg], half, B_BLOCK, dil)[:, None]
        o, lse = banded_attention(to_residue(qg[:, :, gi], dil)[:, :, None],
                                  to_residue(kg[:, :, gi], dil),
                                  to_residue(vg[:, :, gi], dil), bias, half, B_BLOCK)
        outs.append(o[:, :, 0].reshape(b, dil, hg, L, HEAD_DIM).transpose(0, 3, 1, 2, 4).reshape(b, s, hg, HEAD_DIM))
        lses.append(lse[:, :, 0].reshape(b, dil, hg, L).transpose(0, 3, 1, 2).reshape(b, s, hg))
    alpha = jax.nn.softmax(jnp.stack(lses, axis=0), axis=0)
    y = jnp.sum(alpha[..., None] * jnp.stack(outs, axis=0).astype(jnp.float32), axis=0)
    return y.astype(q.dtype).reshape(b, s, B_OUT_W)


def dwconv_centred(t, w, bias):
    pad = CONV_WIDTH // 2
    s = t.shape[1]
    tp = jnp.pad(t, ((0, 0), (pad, pad), (0, 0)))
    acc = tp[:, 0:s] * w[0]
    for j in range(1, CONV_WIDTH):
        acc = acc + tp[:, j:j + s] * w[j]
    return acc + bias


def setup_inputs(seed: int = 0) -> dict:
    key = jax.random.key(seed)
    ks = jax.random.split(key, 20)
    f32 = jnp.float32

    def nrm(k, shape, scale):
        return jax.random.normal(k, shape, f32) * scale

    def gain(k, shape):
        return 1.0 + 0.05 * jax.random.normal(k, shape, f32)

    return {
        "x": nrm(ks[0], (BATCH, SEQ, D_MODEL), 1.0),
        "p": nrm(ks[1], (DEPTH, BATCH, SEQ, PLE_DIM), 1.0),
        "rel_bias_table": nrm(ks[2], (N_BUCKETS, N_BIAS_HEADS), 0.5),
        "attn_norm": gain(ks[3], (DEPTH, D_MODEL)),
        "w_in": nrm(ks[4], (DEPTH, D_MODEL, IN_PROJ_W), D_MODEL ** -0.5),
        "sink_a": nrm(ks[5], (DEPTH, A_Q_HEADS), 0.5),
        "w_branch_a": nrm(ks[6], (DEPTH, A_Q_W, D_MODEL), A_Q_W ** -0.5),
        "w_branch_b": nrm(ks[7], (DEPTH, B_OUT_W, D_MODEL), B_OUT_W ** -0.5),
        "w_out": nrm(ks[8], (DEPTH, D_MODEL, D_MODEL), D_MODEL ** -0.5),
        "ffn_norm": gain(ks[9], (DEPTH, D_MODEL)),
        "w_ffn_gate": nrm(ks[10], (DEPTH, D_MODEL, D_FF), D_MODEL ** -0.5),
        "w_ffn_up": nrm(ks[11], (DEPTH, D_MODEL, D_FF), D_MODEL ** -0.5),
        "conv_w": nrm(ks[12], (DEPTH, CONV_WIDTH, D_FF), CONV_WIDTH ** -0.5),
        "conv_b": nrm(ks[13], (DEPTH, D_FF), 0.02),
        "w_ffn_down": nrm(ks[14], (DEPTH, D_FF, D_MODEL), D_FF ** -0.5),
        "ple_norm": gain(ks[15], (DEPTH, D_MODEL)),
        "w_ple_gate": nrm(ks[16], (DEPTH, D_MODEL, D_MODEL), D_MODEL ** -0.5),
        "w_ple_proj": nrm(ks[17], (DEPTH, PLE_DIM, D_MODEL), PLE_DIM ** -0.5),
        "final_norm": gain(ks[18], (D_MODEL,)),
    }


def reference(x, p, rel_bias_table, attn_norm, w_in, sink_a, w_branch_a, w_branch_b, w_out,
              ffn_norm, w_ffn_gate, w_ffn_up, conv_w, conv_b, w_ffn_down,
              ple_norm, w_ple_gate, w_ple_proj, final_norm):
    split_points = np.cumsum([A_Q_W, A_KV_W, A_KV_W, B_W, B_W, B_W, D_MODEL]).tolist()
    for i in range(DEPTH):
        h = rms_norm(x, attn_norm[i])
        proj = h @ w_in[i]
        qa, ka, va, qb, kb, vb, ga, gb = jnp.split(proj, split_points, axis=-1)
        ya = windowed_gqa(qa, ka, va, rel_bias_table, sink_a[i])
        yb = dilated_attention(qb, kb, vb, rel_bias_table)
        merged = jax.nn.sigmoid(ga) * (ya @ w_branch_a[i]) + jax.nn.sigmoid(gb) * (yb @ w_branch_b[i])
        x = x + merged @ w_out[i]
        hf = rms_norm(x, ffn_norm[i])
        g = dwconv_centred(hf @ w_ffn_gate[i], conv_w[i], conv_b[i])
        x = x + (jax.nn.gelu(g) * (hf @ w_ffn_up[i])) @ w_ffn_down[i]
        gate_p = jax.nn.sigmoid(rms_norm(x, ple_norm[i]) @ w_ple_gate[i])
        x = x + gate_p * (p[i] @ w_ple_proj[i])
    return rms_norm(x, final_norm)
```

```python
import math
from contextlib import ExitStack

import numpy as np
import ml_dtypes
import concourse.bass as bass
import concourse.mybir as mybir
from concourse.bass_utils import run_bass_kernel_spmd

F32 = mybir.dt.float32
BF16 = mybir.dt.bfloat16
AF = mybir.ActivationFunctionType
ALU = mybir.AluOpType

S = 2048
D = 2048
NT = 16
KC = 16
DFF = 5632
NCC = 44
HD = 128
SCALE = HD ** -0.5
EPS = 1e-6
NEG = -30000.0
N_BUCKETS = 32
MAX_DISTANCE = 1024

QA0, KA0, VA0 = 0, 1024, 1280
QB0, KB0, VB0 = 1536, 3072, 4608
GA0, GB0 = 6144, 8192

B_PAT = ((128, 1), (512, 4), (2048, 16))
B_OFFS = ((-1, 1), (-2, 2))
B_EBASE = (0, 3, 8)
NB_TILES = 12

ENGS = ["pe", "act", "dve", "pool", "sp"]


class Tracker:
    def __init__(self):
        self.ops = {e: [] for e in ENGS}
        self.res = {}
        self.dma_cnt = {}

    def add(self, eng, fn, reads=(), writes=(), dma_key=None):
        idx = len(self.ops[eng])
        ref = (eng, idx)
        deps = set()
        for r in reads:
            st = self.res.get(r)
            if st is not None and st[0] is not None:
                deps.add(st[0])
            if st is not None and isinstance(r, tuple) and r[0] == "ps":
                deps.update(v for k_, v in st[1].items() if k_ != eng)
        for w in writes:
            st = self.res.get(w)
            if st is not None:
                if st[0] is not None:
                    deps.add(st[0])
                deps.update(st[1].values())
                deps.update(st[2])
        isdma = dma_key is not None
        if eng == "pe":
            deps = {d for d in deps if not (d[0] == "pe")}
        deps.discard(ref)
        for r in reads:
            st = self.res.setdefault(r, [None, {}, []])
            if isdma:
                st[2].append(ref)
            else:
                st[1][eng] = ref
        for w in writes:
            self.res[w] = [ref, {}, []]
        op = {"fn": fn, "deps": deps, "key": dma_key, "inc": False}
        if isdma:
            c = self.dma_cnt.get(dma_key, 0) + 1
            self.dma_cnt[dma_key] = c
            op["dval"] = 16 * c
        self.ops[eng].append(op)
        return ref

    def barrier(self):
        last = {}
        for e in ENGS:
            for i in range(len(self.ops[e]) - 1, -1, -1):
                if self.ops[e][i]["fn"] is not None and self.ops[e][i]["key"] is None:
                    last[e] = (e, i)
                    break
        dmas = {}
        for e in ENGS:
            for i, op in enumerate(self.ops[e]):
                if op["key"] is not None:
                    dmas[op["key"]] = (e, i)
        for e in ENGS:
            deps = set(v for k, v in last.items() if k != e) | set(dmas.values())
            self.ops[e].append({"fn": None, "deps": deps, "key": None, "inc": False})

    def finalize(self):
        for e in ENGS:
            for op in self.ops[e]:
                for d in op["deps"]:
                    dop = self.ops[d[0]][d[1]]
                    if dop["key"] is None:
                        dop["inc"] = True
        for e in ENGS:
            c = 0
            for op in self.ops[e]:
                if op["key"] is None and op["inc"]:
                    c += 1
                op["val"] = c

    def emit(self, eng, engobj, csem, dsem):
        waited = {}
        n_wait = 0
        for op in self.ops[eng]:
            need = {}
            for d in op["deps"]:
                dop = self.ops[d[0]][d[1]]
                if dop["key"] is not None:
                    k = ("d", dop["key"])
                    v = dop["dval"]
                else:
                    k = ("c", d[0])
                    v = dop["val"]
                if need.get(k, 0) < v:
                    need[k] = v
            for k, v in need.items():
                if waited.get(k, 0) < v:
                    sem = dsem[k[1]] if k[0] == "d" else csem[k[1]]
                    engobj.wait_ge(sem, v)
                    waited[k] = v
                    n_wait += 1
            if op["fn"] is None:
                continue
            ins = op["fn"](engobj)
            if op["key"] is not None:
                ins.then_inc(dsem[op["key"]], 16)
            elif op["inc"]:
                ins.then_inc(csem[eng], 1)
        return n_wait


def I_act(out, in_, func, **kw):
    return lambda e: e.activation(out=out, in_=in_, func=func, **kw)


def I_mm(out, lhsT, rhs, start=True, stop=True):
    return lambda e: e.matmul(out, lhsT=lhsT, rhs=rhs, start=start, stop=stop, skip_group_check=True)


def I_tr(out, in_, ident):
    return lambda e: e.transpose(out=out, in_=in_, identity=ident)


def I_tt(out, in0, in1, op):
    return lambda e: e.tensor_tensor(out=out, in0=in0, in1=in1, op=op)


def I_ts(out, in0, s1, s2, op0, op1):
    return lambda e: e.tensor_scalar(out=out, in0=in0, scalar1=s1, scalar2=s2, op0=op0, op1=op1)


def I_stt(out, in0, scalar, in1, op0, op1):
    return lambda e: e.scalar_tensor_tensor(out=out, in0=in0, scalar=scalar, in1=in1, op0=op0, op1=op1)


def I_copy(out, in_):
    return lambda e: e.tensor_copy(out=out, in_=in_)


def I_acopy(out, in_):
    return lambda e: e.copy(out=out, in_=in_)


def I_amul(out, in_, mul):
    return lambda e: e.mul(out=out, in_=in_, mul=mul)


def I_recip(out, in_):
    return lambda e: e.reciprocal(out=out, in_=in_)


def I_dma(out, in_):
    return lambda e: e.dma_start(out=out, in_=in_)


def I_memset(ap, c):
    return lambda e: e.memset(ap, c)


class Region:
    def __init__(self, t, nbytes):
        self.t = t
        self.nbytes = nbytes
        self.off = 0

    def reset(self):
        self.off = 0

    def alloc(self, shape, dtype):
        n = 1
        for s_ in shape:
            n *= s_
        esz = 4 if dtype == F32 else 2
        nb = n * esz
        self.off = (self.off + 63) // 64 * 64
        assert self.off + nb <= self.nbytes, ("region overflow", self.off, nb, self.nbytes)
        ap = self.t[:, self.off // 2:(self.off + nb) // 2]
        self.off += nb
        if dtype == F32:
            ap = ap.bitcast(F32)
        if len(shape) == 2:
            ap = ap.rearrange("p (a b) -> p a b", a=shape[0])
        elif len(shape) == 3:
            ap = ap.rearrange("p (a b c) -> p a b c", a=shape[0], b=shape[1])
        return ap


def build_nc(stop_after=None):
    nc = bass.Bass("TRN2", target_bir_lowering=False)

    def din(name, shape, dt=F32):
        return nc.dram_tensor(name, list(shape), dt, kind="ExternalInput").ap()

    x_d = din("x", [S, D])
    p_d = din("p", [S, 256])
    w_in = din("w_in", [D, 10240])
    w_ba = din("w_ba", [1024, D])
    w_bb = din("w_bb", [512, D])
    w_out = din("w_out", [D, D])
    w_g = din("w_g", [D, DFF])
    w_u = din("w_u", [D, DFF])
    w_d = din("w_d", [DFF, D])
    w_pg = din("w_pg", [D, D])
    w_pp = din("w_pp", [256, D])
    gfm_d = din("gfm", [128, 3 * 16])
    gfin_d = din("gfin", [1, D])
    convp_d = din("convp", [128, 4 * NCC])
    sink_d = din("sinkb", [128, 8])
    biasA_d = din("biasA", [2, 128, 12 * 128])
    biasB_d = din("biasB", [4, 128, NB_TILES * 128])
    ident_d = din("ident", [128, 128], BF16)
    out_d = nc.dram_tensor("out", [S, D], F32, kind="ExternalOutput").ap()
    x1s = nc.dram_tensor("x1s", [S, D], F32, kind=("ExternalOutput" if stop_after == "B" else "Internal")).ap()
    dbg = {}

    T = Tracker()

    with ExitStack() as es:
        PB = 112 * 1024
        WB = 16 * 1024
        SB = 74 * 1024
        CB = 3 * 1024
        Pt = es.enter_context(nc.sbuf_tensor("Pt", [128, PB // 2], BF16))
        Wt = es.enter_context(nc.sbuf_tensor("Wt", [128, WB // 2], BF16))
        St = es.enter_context(nc.sbuf_tensor("St", [128, SB // 2], BF16))
        Ct = es.enter_context(nc.sbuf_tensor("Ct", [128, CB // 2], BF16))
        banks = [es.enter_context(nc.psum_tensor("ps%d" % i, [128, 512], F32)) for i in range(8)]
        RP = Region(Pt, PB)
        RS = Region(St, SB)
        RC = Region(Ct, CB)

        ident = RC.alloc((128,), BF16)
        ones = RC.alloc((128,), BF16)
        sel0 = RC.alloc((128,), BF16)
        epsb = RC.alloc((1,), F32)
        gfm = RC.alloc((3, 16), F32)
        convp = RC.alloc((4, NCC), F32)
        esink = RC.alloc((8,), F32)
        stat = RC.alloc((3, 24), F32)

        T.add("sp", I_dma(ident, ident_d[:, :]), writes=["ident"], dma_key="c0")
        T.add("sp", I_dma(gfm, gfm_d[:, :].rearrange("p (a b) -> p a b", a=3)), writes=["gfm"], dma_key="c1")
        T.add("sp", I_dma(convp, convp_d[:, :].rearrange("p (a b) -> p a b", a=4)), writes=["convp"], dma_key="c2")
        T.add("sp", I_dma(esink, sink_d[:, :]), writes=["esink"], dma_key="c3")
        T.add("dve", I_memset(ones, 1.0), writes=["ones"])
        T.add("dve", I_memset(sel0, 0.0), writes=["sel0"])
        T.add("dve", I_memset(sel0[0:1, :], 1.0), reads=["sel0"], writes=["sel0"])
        T.add("dve", I_memset(epsb, EPS), writes=["epsb"])
        T.add("act", I_act(esink, esink, AF.Exp), reads=["esink"], writes=["esink"])

        wslots = [Wt[:, i * 2048:(i + 1) * 2048] for i in range(4)]
        wctr = [0]

        def wslot():
            i = wctr[0] % 4
            wctr[0] += 1
            return i, wslots[i]

        preloaded = {}

        def prefetch_w(src2d, c0, ncols=128, nk=KC):
            preloaded[(id(src2d), c0)] = load_w_cols(src2d, c0, ncols, nk)

        def load_w_cols(src2d, c0, ncols=128, nk=KC):
            if (id(src2d), c0) in preloaded:
                return preloaded.pop((id(src2d), c0))
            i, sl = wslot()
            v = sl[:, 0:nk * ncols].rearrange("p (k c) -> p k c", k=nk)
            T.add("pool", I_dma(v, src2d[:, c0:c0 + ncols].rearrange("(k p) c -> p k c", p=128)),
                  writes=[("w", i)], dma_key=("w", i))
            return ("w", i), v

        bankctr = [0]

        def next_bank(lst):
            b = lst[bankctr[0] % len(lst)]
            bankctr[0] += 1
            return b

        evctr = [0]

        def evac_eng():
            evctr[0] += 1
            return "act" if evctr[0] % 2 else "dve"

        def evac_copy(eng, out, in_, reads, writes):
            if eng == "act":
                T.add("act", I_acopy(out, in_), reads=reads, writes=writes)
            else:
                T.add("dve", I_copy(out, in_), reads=reads, writes=writes)

        def rms_stats(xt, xres, slot, junk):
            jb = junk[slot % 2]
            T.add("act", I_act(jb, xt, AF.Square, accum_out=stat[:, 0, slot:slot + 1]), reads=[xres],
                  writes=[("ss", slot), ("junk", id(junk), slot % 2)])

        def rms_rstd(s0, s1):
            T.add("act", I_act(stat[:, 1, s0:s1], stat[:, 0, s0:s1], AF.Sqrt, scale=1.0 / D, bias=epsb[:, 0:1]),
                  reads=[("ss", k_) for k_ in range(s0, s1)] + ["epsb"], writes=[("sq", k_) for k_ in range(s0, s1)])
            T.add("dve", I_recip(stat[:, 2, s0:s1], stat[:, 1, s0:s1]),
                  reads=[("sq", k_) for k_ in range(s0, s1)], writes=[("rs", k_) for k_ in range(s0, s1)])

        def rms_apply(xt, xres, gi, xn, xnres, slot, dst_fn, dstres, eng="act", tbanks=(6, 7)):
            rs = stat[:, 2, slot:slot + 1]
            if eng == "act":
                T.add("act", I_amul(xn, xt, rs), reads=[xres, ("rs", slot)], writes=[xnres])
            else:
                T.add("pool", I_tt(xn, xt, rs.to_broadcast([128, D]), ALU.mult),
                      reads=[xres, ("rs", slot)], writes=[xnres])
            for half in range(2):
                b = tbanks[half]
                bv = banks[b][:].bitcast(BF16)
                for j in range(8):
                    kc = half * 8 + j
                    T.add("pe", I_tr(bv[:, j * 128:(j + 1) * 128], xn[:, kc * 128:(kc + 1) * 128], ident),
                          reads=[xnres, "ident"], writes=[("ps", b)])
                g_b = gfm[:, gi, half * 8:(half + 1) * 8].unsqueeze(2).to_broadcast([128, 8, 128])
                T.add("dve", I_tt(dst_fn(half), bv[:, 0:1024].rearrange("p (a b) -> p a b", a=8), g_b, ALU.mult),
                      reads=[("ps", b), "gfm"], writes=[dstres + (half,)])

        hT = RP.alloc((KC, S), BF16)
        yaT = RP.alloc((8, S), BF16)
        ybT = RP.alloc((4, S), BF16)

        RA = Region(Pt, PB)
        RA.off = 64 * 1024
        NXT = 4
        xts = [RA.alloc((D,), F32) for _ in range(NXT)]
        xns = [RA.alloc((D,), BF16) for _ in range(2)]
        junkA = [RA.alloc((D,), BF16) for _ in range(2)]
        assert RA.off <= 112 * 1024
        a2_units = [("A", 0), ("A", 1), ("B", 0), ("B", 1), ("B", 2), ("B", 3)]

        def unit_cols(kind, u):
            if kind == "A":
                return [KA0 + u * 128, VA0 + u * 128] + [QA0 + (4 * u + h) * 128 for h in range(4)]
            cols = []
            for gi in range(3):
                hc = (gi * 4 + u) * 128
                cols += [KB0 + hc, VB0 + hc, QB0 + hc]
            return cols

        if stop_after != "A1":
            for c0 in unit_cols(*a2_units[0])[:4]:
                prefetch_w(w_in, c0)
        def a1_apply(tt):
            bsel = tt % NXT
            rms_rstd(tt, tt + 1)
            rms_apply(xts[bsel], ("xt", bsel), 0, xns[tt % 2], ("xn", tt % 2), tt,
                      lambda half, tt=tt: hT[:, half * 8:(half + 1) * 8, tt * 128:(tt + 1) * 128], ("hT", tt), eng="act")

        for tt in range(NT):
            bsel = tt % NXT
            T.add("sp", I_dma(xts[bsel], x_d[tt * 128:(tt + 1) * 128, :]), writes=[("xt", bsel)], dma_key=("xt", bsel))
            rms_stats(xts[bsel], ("xt", bsel), tt, junkA)
            if tt >= 1:
                a1_apply(tt - 1)
        a1_apply(NT - 1)
        if stop_after == "A1":
            T.barrier()
            dbg["hT"] = nc.dram_tensor("dbg_hT", [128, KC * S], BF16, kind="ExternalOutput").ap()
            T.add("sp", I_dma(dbg["hT"][:, :].rearrange("p (a b) -> p a b", a=KC), hT), dma_key="dbg")

        if stop_after not in ("A1",):
            RS.reset()
            QTf = RS.alloc((4 * S,), BF16)
            QT = QTf.rearrange("p (h t) -> p h t", h=4)
            QTa = QTf.rearrange("p (q h c) -> p q h c", q=NT, h=4)
            U2sb = RS.alloc((S,), BF16)
            D2sb = RS.alloc((S,), BF16)
            KT = RS.alloc((3, S), BF16)
            VTs = [RS.alloc((S,), BF16)]
            Vb = RS.alloc((3, NT, 128), BF16)
            EB = RS.alloc((12, 128), F32)
            Es = [RS.alloc((512,), F32) for _ in range(3)]
            Ps = [RS.alloc((512,), BF16) for _ in range(3)]
            recs = [RS.alloc((128,), F32) for _ in range(2)]
            PROJ_BANKS = [0, 1, 2, 5]
            S_BANKS = [0, 1, 2]
            UD_BANKS = [3, 4]
            vtc = [0]

            def project(c0, dst, dstres, dst_fn=None, deint=False):
                wres, wv = load_w_cols(w_in, c0)
                for tb in range(4):
                    b = next_bank(PROJ_BANKS)
                    for kc in range(KC):
                        T.add("pe", I_mm(banks[b][:, :], wv[:, kc, :], hT[:, kc, tb * 512:(tb + 1) * 512],
                                         start=(kc == 0), stop=(kc == KC - 1)),
                              reads=[wres] + [("hT", 4 * tb + q_, hf_) for q_ in range(4) for hf_ in range(2)],
                              writes=[("ps", b)])
                    if deint:
                        evac_copy(evac_eng(), dst.rearrange("p (r i) -> p r i", r=16)[:, :, 32 * tb:32 * tb + 32],
                                  banks[b][:, :].rearrange("p (i r) -> p r i", r=16), [("ps", b)], [dstres + (tb,)])
                    elif dst_fn is None:
                        evac_copy(evac_eng(), dst[:, tb * 512:(tb + 1) * 512], banks[b][:, :], [("ps", b)], [dstres + (tb,)])
                    else:
                        evac_copy(evac_eng(), dst_fn(tb), banks[b][:, :].rearrange("p (a b) -> p a b", a=4),
                                  [("ps", b)], [dstres + (tb,)])

            def project_v(c0, hv, deint=False):
                i = 0
                project(c0, VTs[i], ("VT", i), deint=deint)
                for half in range(2):
                    b = 6 + half
                    bv = banks[b][:].bitcast(BF16)
                    for j in range(8):
                        kb = half * 8 + j
                        T.add("pe", I_tr(bv[:, j * 128:(j + 1) * 128], VTs[i][:, kb * 128:(kb + 1) * 128], ident),
                              reads=([("VT", i, t_) for t_ in range(4)] if deint else [("VT", i, kb // 4)]) + ["ident"],
                              writes=[("ps", b)])
                    evac_copy(evac_eng(), Vb[:, hv, half * 8:(half + 1) * 8, :],
                              bv[:, 0:1024].rearrange("p (a b) -> p a b", a=8), [("ps", b)], [("V", hv)])

            udc = [0]
            stc = [0]
            mulc = [0]

            def mul_eng():
                mulc[0] += 1
                return "pool" if mulc[0] % 3 == 0 else "dve"

            def attention(groups, inject=False):
                steps = []
                for gidx, (sources, sink_col, dstT, dchunk) in enumerate(groups):
                    for qb in range(NT):
                        first = True
                        lst = []
                        for (qsel, ksel, vsel, ebase, omin, omax) in sources:
                            kbs = [kb for kb in range(qb + omin, qb + omax + 1) if 0 <= kb < NT]
                            for c in range(0, len(kbs), 4):
                                ch = kbs[c:c + 4]
                                lst.append([gidx, qb, qsel, ksel, vsel, ebase + (ch[0] - qb - omin), ch, False, False])
                        lst[0][7] = True
                        lst[-1][8] = True
                        steps.extend(lst)
                LAG = 2
                n = len(steps)
                info = {}
                for s_ in range(n + LAG):
                    if s_ < n:
                        gidx, qb, qsel, ksel, vsel, t0, ch, isfirst, islast = steps[s_]
                        k = stc[0] % 3
                        stc[0] += 1
                        sb = S_BANKS[k]
                        w = len(ch) * 128
                        for i, kb in enumerate(ch):
                            T.add("pe", I_mm(banks[sb][:, i * 128:(i + 1) * 128], KT[:, ksel, kb * 128:(kb + 1) * 128],
                                             QT[:, qsel, qb * 128:(qb + 1) * 128]),
                                  reads=[("KT", ksel, kb // 4), ("QT", qsel, qb // 4)], writes=[("ps", sb)])
                        T.add("act", I_act(Es[k][:, 0:w], banks[sb][:, 0:w], AF.Exp, scale=SCALE),
                              reads=[("ps", sb)], writes=[("E", k)])
                        T.add(mul_eng(), I_tt(Ps[k][:, 0:w], Es[k][:, 0:w],
                                           EB[:, t0:t0 + len(ch), :].rearrange("p a b -> p (a b)"), ALU.mult),
                              reads=[("E", k), "EB"], writes=[("P", k)])
                        info[s_] = k
                    s2 = s_ - LAG
                    if s2 >= 0:
                        gidx, qb, qsel, ksel, vsel, t0, ch, isfirst, islast = steps[s2]
                        k = info[s2]
                        if isfirst:
                            udc[0] += 1
                        ub = UD_BANKS[udc[0] % 2]
                        if isfirst and inject:
                            T.add("pe", I_mm(banks[ub][:, 0:128], ident, U2sb[:, qb * 128:(qb + 1) * 128], True, False),
                                  reads=["ident", "U2sb"], writes=[("ps", ub)])
                            T.add("pe", I_mm(banks[ub][:, 128:256], sel0, D2sb[:, qb * 128:(qb + 1) * 128], False, False),
                                  reads=["sel0", "D2sb"], writes=[("ps", ub)])
                        for i, kb in enumerate(ch):
                            T.add("pe", I_mm(banks[ub][:, 0:128], Vb[:, vsel, kb, :], Ps[k][:, i * 128:(i + 1) * 128],
                                             start=(isfirst and i == 0 and not inject), stop=False),
                                  reads=[("V", vsel), ("P", k)], writes=[("ps", ub)])
                            T.add("pe", I_mm(banks[ub][:, 128:256], ones, Ps[k][:, i * 128:(i + 1) * 128],
                                             start=False, stop=(islast and i == len(ch) - 1)),
                                  reads=["ones", ("P", k)], writes=[("ps", ub)])
                        if islast:
                            sources, sink_col, dstT, dchunk = groups[gidx]
                            r = udc[0] % 2
                            if sink_col is not None:
                                T.add("dve", (lambda e, o=recs[r], i_=banks[ub][:, 128:256], s1=esink[:, sink_col:sink_col + 1]:
                                              e.tensor_scalar_add(out=o, in0=i_, scalar1=s1)),
                                      reads=[("ps", ub), "esink"], writes=[("rec", r)])
                                T.add("dve", I_recip(recs[r], recs[r]), reads=[("rec", r)], writes=[("rec", r)])
                            else:
                                T.add("dve", I_recip(recs[r], banks[ub][:, 128:256]), reads=[("ps", ub)], writes=[("rec", r)])
                            T.add("dve", I_tt(dstT[:, dchunk, qb * 128:(qb + 1) * 128], banks[ub][:, 0:128], recs[r], ALU.mult),
                                  reads=[("ps", ub), ("rec", r)], writes=[("yT", id(dstT), dchunk, qb)])

            def attention_g2():
                KTd = KT[:, 2, :].rearrange("p (r i) -> p r i", r=16)
                QTd = QT[:, 2, :].rearrange("p (r i) -> p r i", r=16)
                U2v = U2sb.rearrange("p (i r) -> p r i", r=16)
                D2v = D2sb.rearrange("p (i r) -> p r i", r=16)
                kq_reads = [("KT", 2, t_) for t_ in range(4)] + [("QT", 2, t_) for t_ in range(4)]
                LAG = 2
                info = {}
                for s_ in range(4 + LAG):
                    if s_ < 4:
                        k = stc[0] % 3
                        stc[0] += 1
                        sb = S_BANKS[k]
                        for c in range(4):
                            r = s_ * 4 + c
                            T.add("pe", I_mm(banks[sb][:, c * 128:(c + 1) * 128], KTd[:, r, :], QTd[:, r, :]),
                                  reads=kq_reads, writes=[("ps", sb)])
                        T.add("act", I_act(Es[k][:, :], banks[sb][:, :], AF.Exp, scale=SCALE),
                              reads=[("ps", sb)], writes=[("E", k)])
                        T.add(mul_eng(), I_tt(Ps[k][:, :].rearrange("p (a b) -> p a b", a=4),
                                           Es[k][:, :].rearrange("p (a b) -> p a b", a=4),
                                           EB[:, B_EBASE[2]:B_EBASE[2] + 4, :], ALU.mult),
                              reads=[("E", k), "EB"], writes=[("P", k)])
                        info[s_] = k
                    s2 = s_ - LAG
                    if s2 >= 0:
                        k = info[s2]
                        for half in range(2):
                            udc[0] += 1
                            ub = UD_BANKS[udc[0] % 2]
                            for c2 in range(2):
                                c = half * 2 + c2
                                r = s2 * 4 + c
                                T.add("pe", I_mm(banks[ub][:, c2 * 256:c2 * 256 + 128], Vb[:, 2, r, :],
                                                 Ps[k][:, c * 128:(c + 1) * 128], c2 == 0, False),
                                      reads=[("V", 2), ("P", k)], writes=[("ps", ub)])
                                T.add("pe", I_mm(banks[ub][:, c2 * 256 + 128:c2 * 256 + 256], ones,
                                                 Ps[k][:, c * 128:(c + 1) * 128], False, c2 == 1),
                                      reads=["ones", ("P", k)], writes=[("ps", ub)])
                            r0 = s2 * 4 + half * 2
                            bview = banks[ub][:, :].rearrange("p (a b) -> p a b", a=2)
                            ee = "act" if half == 0 else "dve"
                            evac_copy(ee, U2v[:, r0:r0 + 2, :], bview[:, :, 0:128], [("ps", ub)], [("U2sb", r0)])
                            evac_copy(ee, D2v[:, r0:r0 + 2, :], bview[:, :, 128:256], [("ps", ub)], [("D2sb", r0)])
                T.add("dve", I_memset(recs[0][:, 0:1], 0.0),
                      reads=[("U2sb", r0_) for r0_ in range(0, 16, 2)] + [("D2sb", r0_) for r0_ in range(0, 16, 2)],
                      writes=["U2sb", "D2sb"])

            def attention_gqa(u):
                steps = []
                for qb in range(NT):
                    kbs = [kb for kb in (qb - 1, qb, qb + 1) if 0 <= kb < NT]
                    for kb in kbs:
                        steps.append((qb, kb, kb == kbs[0], kb == kbs[-1]))
                LAG = 2
                n = len(steps)
                info = {}
                for s_ in range(n + LAG):
                    if s_ < n:
                        qb, kb, isfirst, islast = steps[s_]
                        k = stc[0] % 3
                        stc[0] += 1
                        sb = S_BANKS[k]
                        oi = kb - qb + 1
                        T.add("pe", I_mm(banks[sb][:, :], KT[:, 0, kb * 128:(kb + 1) * 128],
                                         QTa[:, qb, :, :].rearrange("p h c -> p (h c)")),
                              reads=[("KT", 0, kb // 4)] + [("QT", h, qb // 4) for h in range(4)], writes=[("ps", sb)])
                        T.add("act", I_act(Es[k][:, :], banks[sb][:, :], AF.Exp, scale=SCALE),
                              reads=[("ps", sb)], writes=[("E", k)])
                        T.add(mul_eng(), I_tt(Ps[k][:, :], Es[k][:, :],
                                           EB[:, oi * 4:(oi + 1) * 4, :].rearrange("p a b -> p (a b)"), ALU.mult),
                              reads=[("E", k), "EB"], writes=[("P", k)])
                        info[s_] = k
                    s2 = s_ - LAG
                    if s2 >= 0:
                        qb, kb, isfirst, islast = steps[s2]
                        k = info[s2]
                        if isfirst:
                            udc[0] += 1
                        ub = (3, 4)[udc[0] % 2]
                        db = (5, 6)[udc[0] % 2]
                        T.add("pe", I_mm(banks[ub][:, :], Vb[:, 0, kb, :], Ps[k][:, :], isfirst, islast),
                              reads=[("V", 0), ("P", k)], writes=[("ps", ub)])
                        T.add("pe", I_mm(banks[db][:, :], ones, Ps[k][:, :], isfirst, islast),
                              reads=["ones", ("P", k)], writes=[("ps", db)])
                        if islast:
                            r = udc[0] % 2
                            import os
                            if os.environ.get("K_OLDFIN"):
                                for h in range(4):
                                    T.add("dve", (lambda e, o=rec4[r][:, h * 128:(h + 1) * 128], i_=banks[db][:, h * 128:(h + 1) * 128],
                                                  s1=esink[:, 4 * u + h:4 * u + h + 1]: e.tensor_scalar_add(out=o, in0=i_, scalar1=s1)),
                                          reads=[("ps", db), "esink"], writes=[("rec4", r, h)])
                                T.add("dve", I_recip(rec4[r], rec4[r]),
                                      reads=[("rec4", r, h) for h in range(4)], writes=[("rec4", r, h) for h in range(4)])
                            else:
                              for h in range(4):
                                T.add("act", I_act(rec4[r][:, h * 128:(h + 1) * 128], banks[db][:, h * 128:(h + 1) * 128],
                                                   AF.Ln, bias=esink[:, 4 * u + h:4 * u + h + 1]),
                                      reads=[("ps", db), "esink"], writes=[("rec4", r, h)])
                              T.add("act", I_act(rec4[r], rec4[r], AF.Exp, scale=-1.0),
                                  reads=[("rec4", r, h) for h in range(4)], writes=[("rec4", r, h) for h in range(4)])
                            T.add("dve", I_tt(yaT[:, 4 * u:4 * u + 4, qb * 128:(qb + 1) * 128],
                                              banks[ub][:, :].rearrange("p (a b) -> p a b", a=4),
                                              rec4[r].rearrange("p (a b) -> p a b", a=4), ALU.mult),
                                  reads=[("ps", ub)] + [("rec4", r, h) for h in range(4)], writes=[("yaT", u, qb)])

            rec4 = [RS.alloc((512,), F32) for _ in range(2)]
            for ui, (kind, u) in enumerate(a2_units):
                if kind == "A":
                    ntile = 12
                    T.add("sp", I_dma(EB[:, 0:ntile, :], biasA_d[u].rearrange("p (a b) -> p a b", a=ntile)),
                          writes=["EB"], dma_key="EB")
                else:
                    ntile = NB_TILES
                    T.add("sp", I_dma(EB[:, 0:ntile, :], biasB_d[u].rearrange("p (a b) -> p a b", a=ntile)),
                          writes=["EB"], dma_key="EB")
                T.add("act", I_act(EB[:, 0:ntile, :], EB[:, 0:ntile, :], AF.Exp), reads=["EB"], writes=["EB"])
                if kind == "A":
                    project(KA0 + u * 128, KT[:, 0, :], ("KT", 0))
                    project_v(VA0 + u * 128, 0)
                    for h in range(4):
                        project(QA0 + (4 * u + h) * 128, None, ("QT", h),
                                dst_fn=lambda tb, h=h: QTa[:, tb * 4:(tb + 1) * 4, h, :])
                else:
                    for gi in range(3):
                        hc = (gi * 4 + u) * 128
                        project(KB0 + hc, KT[:, gi, :], ("KT", gi), deint=(gi == 2))
                        project_v(VB0 + hc, gi, deint=(gi == 2))
                        project(QB0 + hc, QT[:, gi, :], ("QT", gi), deint=(gi == 2))
                if ui + 1 < len(a2_units):
                    for c0 in unit_cols(*a2_units[ui + 1])[:4]:
                        prefetch_w(w_in, c0)
                if kind == "A":
                    attention_gqa(u)
                else:
                    attention_g2()
                    srcs = [(gi, gi, gi, B_EBASE[gi], B_OFFS[gi][0], B_OFFS[gi][1]) for gi in range(2)]
                    attention([(srcs, None, ybT, u)], inject=True)
            T.barrier()
            if stop_after == "A2":
                dbg["yaT"] = nc.dram_tensor("dbg_yaT", [128, 8 * S], BF16, kind="ExternalOutput").ap()
                dbg["ybT"] = nc.dram_tensor("dbg_ybT", [128, 4 * S], BF16, kind="ExternalOutput").ap()
                T.add("sp", I_dma(dbg["yaT"][:, :].rearrange("p (a b) -> p a b", a=8), yaT), dma_key="dbg")
                T.add("sp", I_dma(dbg["ybT"][:, :].rearrange("p (a b) -> p a b", a=4), ybT), dma_key="dbg2")

        if stop_after not in ("A1", "A2"):
            RS.reset()
            mergedT = RS.alloc((KC, 1024), BF16)
            WoA = RS.alloc((KC, 512), BF16)
            WoB = Wt[:, 0:KC * 512].rearrange("p (k c) -> p k c", k=KC)
            WOS = [(WoA, [("WoA",)]), (WoB, [("w", i) for i in range(4)])]
            sg = [RS.alloc((512,), F32) for _ in range(4)]
            xs = [RS.alloc((512,), F32) for _ in range(3)]
            os_ = [RS.alloc((512,), F32) for _ in range(3)]
            MB = [0, 1, 2, 3, 4, 5, 6, 7]
            cB = [0]
            for th in range(2):
                for fc in range(KC):
                    rga, wga = load_w_cols(w_in, GA0 + fc * 128)
                    rgb, wgb = load_w_cols(w_in, GB0 + fc * 128)
                    rba, wba = load_w_cols(w_ba, fc * 128, nk=8)
                    rbb, wbb = load_w_cols(w_bb, fc * 128, nk=4)
                    bk = [[next_bank(MB) for _ in range(2)] for _ in range(4)]
                    for wi_, (wres_, wv_, src_, nk_) in enumerate(((rga, wga, hT, KC), (rgb, wgb, hT, KC),
                                                                   (rba, wba, yaT, 8), (rbb, wbb, ybT, 4))):
                        for tb2 in range(2):
                            t0 = th * 1024 + tb2 * 512
                            for kc in range(nk_):
                                T.add("pe", I_mm(banks[bk[wi_][tb2]][:, :], wv_[:, kc, :], src_[:, kc, t0:t0 + 512],
                                                 kc == 0, kc == nk_ - 1),
                                      reads=[wres_], writes=[("ps", bk[wi_][tb2])])
                    for tb2 in range(2):
                        bga, bgb, bza, bzb = (bk[w_][tb2] for w_ in range(4))
                        i0 = (cB[0] % 2) * 2
                        cB[0] += 1
                        T.add("act", I_act(sg[i0], banks[bga][:, :], AF.Sigmoid), reads=[("ps", bga)], writes=[("sg", i0)])
                        T.add("act", I_act(sg[i0 + 1], banks[bgb][:, :], AF.Sigmoid), reads=[("ps", bgb)], writes=[("sg", i0 + 1)])
                        T.add("dve", I_tt(sg[i0], sg[i0], banks[bza][:, :], ALU.mult),
                              reads=[("sg", i0), ("ps", bza)], writes=[("sg", i0)])
                        T.add("dve", I_tt(sg[i0 + 1], sg[i0 + 1], banks[bzb][:, :], ALU.mult),
                              reads=[("sg", i0 + 1), ("ps", bzb)], writes=[("sg", i0 + 1)])
                        T.add("dve", I_tt(mergedT[:, fc, tb2 * 512:(tb2 + 1) * 512], sg[i0], sg[i0 + 1], ALU.add),
                              reads=[("sg", i0), ("sg", i0 + 1)], writes=[("mg", fc, tb2)])
                its = [(fb, tt) for fb in range(4) for tt in range(8)]

                def xload(it):
                    fb, tt = its[it]
                    trow = th * 1024 + tt * 128
                    xi = it % 3
                    T.add("sp", I_dma(xs[xi], x_d[trow:trow + 128, fb * 512:(fb + 1) * 512]),
                          writes=[("xs", xi)], dma_key=("xs", xi))

                xload(0)
                xload(1)
                for it, (fb, tt) in enumerate(its):
                    Wo_, wres_ = WOS[fb % 2]
                    if tt == 0:
                        T.add("pool", I_dma(Wo_, w_out[:, fb * 512:(fb + 1) * 512].rearrange("(k p) c -> p k c", p=128)),
                              writes=wres_, dma_key=("Wo", fb % 2))
                    trow = th * 1024 + tt * 128
                    b = next_bank(MB)
                    xi = it % 3
                    if it + 2 < len(its):
                        xload(it + 2)
                    for kc in range(KC):
                        T.add("pe", I_mm(banks[b][:, :], mergedT[:, kc, tt * 128:(tt + 1) * 128], Wo_[:, kc, :],
                                         kc == 0, kc == KC - 1),
                              reads=wres_ + [("mg", kc, tt // 4)], writes=[("ps", b)])
                    T.add("dve", I_tt(os_[xi], banks[b][:, :], xs[xi], ALU.add),
                          reads=[("ps", b), ("xs", xi)], writes=[("os", xi)])
                    T.add("sp", I_dma(x1s[trow:trow + 128, fb * 512:(fb + 1) * 512], os_[xi]),
                          reads=[("os", xi)], writes=[("x1s", trow, fb)], dma_key=("os", xi))
            T.barrier()

        if stop_after not in ("A1", "A2", "B"):
            RP.reset()
            X1 = RP.alloc((8, D), F32)
            hfT = RP.alloc((KC, 1032), BF16)
            pT = RP.alloc((2, 1024), BF16)
            pf = [RP.alloc((256,), F32) for _ in range(2)]
            pb = [RP.alloc((256,), BF16) for _ in range(2)]

            def p_tiles(th):
                for tt in range(8):
                    trow = th * 1024 + tt * 128
                    pi = tt % 2
                    T.add("sp", I_dma(pf[pi], p_d[trow:trow + 128, :]), writes=[("pf", pi)], dma_key=("pf", pi))
                    T.add("act", I_acopy(pb[pi], pf[pi]), reads=[("pf", pi)], writes=[("pb", pi)])
                    b = 5
                    bv = banks[b][:].bitcast(BF16)
                    for j in range(2):
                        T.add("pe", I_tr(bv[:, j * 128:(j + 1) * 128], pb[pi][:, j * 128:(j + 1) * 128], ident),
                              reads=[("pb", pi), "ident"], writes=[("ps", b)])
                    T.add("act", I_acopy(pT[:, :, tt * 128:(tt + 1) * 128], bv[:, 0:256].rearrange("p (a b) -> p a b", a=2)),
                          reads=[("ps", b)], writes=[("pT", tt)])
            xnC = [RP.alloc((D,), BF16) for _ in range(2)]
            RS.reset()
            junkC = [RS.alloc((D,), BF16) for _ in range(2)]
            xh = RS.alloc((D,), F32)
            halT = RS.alloc((KC, 128), BF16)
            Wpg = [RS.alloc((KC, 256), BF16) for _ in range(2)]
            Wpp = [RS.alloc((2, 256), BF16) for _ in range(2)]
            sg3 = [RS.alloc((256,), F32) for _ in range(2)]
            tm3 = [RS.alloc((256,), F32) for _ in range(2)]
            gB = RS.alloc((D,), F32)
            ost = [RS.alloc((D,), F32) for _ in range(2)]
            xn1s = xnC
            xn3s = xnC
            junk1 = junkC
            junk3 = junkC

            def c1_load_halo(th_):
                hrow_ = 1024 if th_ == 0 else 896
                T.add("sp", I_dma(xh, x1s[hrow_:hrow_ + 128, :]), writes=["xh"], dma_key="xh")

            def c1_load_tile(th_, tt):
                trow_ = th_ * 1024 + tt * 128
                T.add("sp", I_dma(X1[:, tt, :], x1s[trow_:trow_ + 128, :]), writes=[("X1", tt)], dma_key=("X1", tt))

            for th in range(2):
                hcol = 0 if th == 0 else 127
                if th == 0:
                    c1_load_halo(0)
                    for tt in range(8):
                        c1_load_tile(0, tt)
                rms_stats(xh, "xh", 0, junk1)
                for tt in range(4):
                    rms_stats(X1[:, tt, :], ("X1", tt), tt + 1, junk1)
                rms_rstd(0, 5)
                rms_apply(xh, "xh", 1, xn1s[0], ("xnC", 0), 0,
                          lambda half: halT[:, half * 8:(half + 1) * 8, :], ("halT",), eng="act")
                T.add("dve", I_copy(hfT[:, :, 1024:1025], halT[:, :, hcol:hcol + 1]), reads=[("halT", 0), ("halT", 1)],
                      writes=[("hfT", -1)])
                for tt in range(4, 8):
                    rms_stats(X1[:, tt, :], ("X1", tt), tt + 1, junk1)
                rms_rstd(5, 9)
                for tt in range(8):
                    rms_apply(X1[:, tt, :], ("X1", tt), 1, xn1s[(tt + 1) % 2], ("xnC", (tt + 1) % 2), tt + 1,
                              lambda half, tt=tt: hfT[:, half * 8:(half + 1) * 8, tt * 128:(tt + 1) * 128],
                              ("hfT", tt), eng="act")
                T.barrier()
                RS.reset()
                aT = RS.alloc((11, 1024), BF16)
                Wd = [RS.alloc((11, 512), BF16) for _ in range(2)]
                G = [RS.alloc((1026,), F32) for _ in range(2)]
                ACC = [RS.alloc((1024,), F32) for _ in range(2)]
                GEL = [RS.alloc((1024,), F32) for _ in range(2)]
                FB = [0, 1, 2, 3, 4, 5, 6]
                cC = [0]
                for qd in range(4):
                    for cl in range(11):
                        c = qd * 11 + cl
                        rg_, wg_ = load_w_cols(w_g, c * 128)
                        ru_, wu_ = load_w_cols(w_u, c * 128)
                        gi = cC[0] % 2
                        cC[0] += 1
                        bg = [next_bank(FB), next_bank(FB)]
                        for tb2 in range(2):
                            for kc in range(KC):
                                T.add("pe", I_mm(banks[bg[tb2]][:, :], wg_[:, kc, :],
                                                 hfT[:, kc, tb2 * 512:(tb2 + 1) * 512], kc == 0, kc == KC - 1),
                                      reads=[rg_], writes=[("ps", bg[tb2])])
                        for kc in range(KC):
                            T.add("pe", I_mm(banks[7][:, 0:1], wg_[:, kc, :], hfT[:, kc, 1024:1025], kc == 0, kc == KC - 1),
                                  reads=[rg_], writes=[("ps", 7)])
                        bu = [next_bank(FB), next_bank(FB)]
                        for tb2 in range(2):
                            for kc in range(KC):
                                T.add("pe", I_mm(banks[bu[tb2]][:, :], wu_[:, kc, :],
                                                 hfT[:, kc, tb2 * 512:(tb2 + 1) * 512], kc == 0, kc == KC - 1),
                                      reads=[ru_], writes=[("ps", bu[tb2])])
                        Gt = G[gi]
                        T.add("act", I_acopy(Gt[:, 1:513], banks[bg[0]][:, :]), reads=[("ps", bg[0])], writes=[("G", gi, 0)])
                        T.add("act", I_acopy(Gt[:, 513:1025], banks[bg[1]][:, :]), reads=[("ps", bg[1])], writes=[("G", gi, 1)])
                        if th == 0:
                            T.add("act", I_acopy(Gt[:, 1025:1026], banks[7][:, 0:1]), reads=[("ps", 7)], writes=[("G", gi, 2)])
                            T.add("dve", I_memset(Gt[:, 0:1], 0.0), writes=[("G", gi, 3)])
                        else:
                            T.add("act", I_acopy(Gt[:, 0:1], banks[7][:, 0:1]), reads=[("ps", 7)], writes=[("G", gi, 2)])
                            T.add("dve", I_memset(Gt[:, 1025:1026], 0.0), writes=[("G", gi, 3)])
                        gres = [("G", gi, k_) for k_ in range(4)]
                        A_ = ACC[gi]
                        T.add("dve", I_ts(A_, Gt[:, 1:1025], convp[:, 1, c:c + 1], convp[:, 3, c:c + 1], ALU.mult, ALU.add),
                              reads=gres + ["convp"], writes=[("ACC", gi)])
                        T.add("dve", I_stt(A_, Gt[:, 0:1024], convp[:, 0, c:c + 1], A_, ALU.mult, ALU.add),
                              reads=gres + [("ACC", gi)], writes=[("ACC", gi)])
                        T.add("dve", I_stt(A_, Gt[:, 2:1026], convp[:, 2, c:c + 1], A_, ALU.mult, ALU.add),
                              reads=gres + [("ACC", gi)], writes=[("ACC", gi)])
                        T.add("act", I_act(GEL[gi], A_, AF.Gelu_apprx_tanh), reads=[("ACC", gi)], writes=[("GEL", gi)])
                        for tb2 in range(2):
                            T.add("dve", I_tt(aT[:, cl, tb2 * 512:(tb2 + 1) * 512], GEL[gi][:, tb2 * 512:(tb2 + 1) * 512],
                                              banks[bu[tb2]][:, :], ALU.mult),
                                  reads=[("GEL", gi), ("ps", bu[tb2])], writes=[("aT", cl, tb2)])
                        if c == 1:
                            p_tiles(th)
                    for fb in range(4):
                        wi = cC[0] % 2
                        cC[0] += 1
                        T.add("pool", I_dma(Wd[wi], w_d[qd * 11 * 128:(qd + 1) * 11 * 128, fb * 512:(fb + 1) * 512]
                                            .rearrange("(c p) f -> p c f", p=128)),
                              writes=[("Wd", wi)], dma_key=("Wd", wi))
                        for t2 in range(2):
                            bd = [next_bank(FB) for _ in range(4)]
                            for cl in range(11):
                                for t4 in range(4):
                                    tt = t2 * 4 + t4
                                    T.add("pe", I_mm(banks[bd[t4]][:, :], aT[:, cl, tt * 128:(tt + 1) * 128], Wd[wi][:, cl, :],
                                                     cl == 0, cl == 10),
                                          reads=[("Wd", wi), ("aT", cl, tt // 4)], writes=[("ps", bd[t4])])
                            for t4 in range(4):
                                tt = t2 * 4 + t4
                                xsl = X1[:, tt, fb * 512:(fb + 1) * 512]
                                T.add("dve", I_tt(xsl, banks[bd[t4]][:, :], xsl, ALU.add),
                                      reads=[("ps", bd[t4]), ("X1", tt)], writes=[("X1", tt)])
                T.barrier()
                T.add("sp", I_dma(gB, gfin_d.partition_broadcast(128)), writes=["gB"], dma_key="gB")
                for tt in range(8):
                    rms_stats(X1[:, tt, :], ("X1", tt), tt, junk3)
                rms_rstd(0, 8)
                for tt in range(8):
                    rms_apply(X1[:, tt, :], ("X1", tt), 2, xn3s[tt % 2], ("xnC", tt % 2), tt,
                              lambda half, tt=tt: hfT[:, half * 8:(half + 1) * 8, tt * 128:(tt + 1) * 128],
                              ("hfT", tt), eng="act")
                PB_ = [0, 1, 2, 3, 4]
                c3 = [0]

                def c3_final(tt):
                    rms_stats(X1[:, tt, :], ("X1", tt), 16 + tt, junk3)
                    if tt % 4 != 3:
                        return
                    rms_rstd(16 + tt - 3, 16 + tt + 1)
                    if th == 0 and tt == 3:
                        c1_load_halo(1)
                    for t_ in range(tt - 3, tt + 1):
                        trow = th * 1024 + t_ * 128
                        oi = t_ % 2
                        T.add("dve", I_stt(ost[oi], X1[:, t_, :], stat[:, 2, 16 + t_:17 + t_], gB, ALU.mult, ALU.mult),
                              reads=[("X1", t_), ("rs", 16 + t_), "gB"], writes=[("ost", oi)])
                        T.add("sp", I_dma(out_d[trow:trow + 128, :], ost[oi]), reads=[("ost", oi)], writes=[("out", trow)],
                              dma_key=("ost", oi))
                        if th == 0:
                            c1_load_tile(1, t_)

                for fb in range(8):
                    wi = fb % 2
                    T.add("pool", I_dma(Wpg[wi], w_pg[:, fb * 256:(fb + 1) * 256].rearrange("(k p) c -> p k c", p=128)),
                          writes=[("Wpg", wi)], dma_key=("Wpg", wi))
                    T.add("pool", I_dma(Wpp[wi], w_pp[:, fb * 256:(fb + 1) * 256].rearrange("(k p) c -> p k c", p=128)),
                          writes=[("Wpp", wi)], dma_key=("Wpp", wi))
                    for tt in range(8):
                        ba = next_bank(PB_)
                        for kc in range(KC):
                            T.add("pe", I_mm(banks[ba][:, 0:256], hfT[:, kc, tt * 128:(tt + 1) * 128], Wpg[wi][:, kc, :],
                                             kc == 0, kc == KC - 1),
                                  reads=[("Wpg", wi), ("hfT", tt, 0), ("hfT", tt, 1)], writes=[("ps", ba)])
                        for kc in range(2):
                            T.add("pe", I_mm(banks[ba][:, 256:512], pT[:, kc, tt * 128:(tt + 1) * 128], Wpp[wi][:, kc, :],
                                             False, kc == 1),
                                  reads=[("Wpp", wi), ("pT", tt)], writes=[("ps", ba)])
                        i3 = c3[0] % 2
                        c3[0] += 1
                        T.add("act", I_act(sg3[i3], banks[ba][:, 0:256], AF.Sigmoid), reads=[("ps", ba)], writes=[("sg3", i3)])
                        T.add("dve", I_tt(tm3[i3], sg3[i3], banks[ba][:, 256:512], ALU.mult),
                              reads=[("sg3", i3), ("ps", ba)], writes=[("tm3", i3)])
                        xsl = X1[:, tt, fb * 256:(fb + 1) * 256]
                        T.add("dve", I_tt(xsl, xsl, tm3[i3], ALU.add), reads=[("tm3", i3), ("X1", tt)], writes=[("X1", tt)])
                        if fb == 7 and tt >= 1:
                            c3_final(tt - 1)
                c3_final(7)
                if th == 1:
                    T.barrier()

        T.barrier()
        T.finalize()

        csem = {e: es.enter_context(nc.semaphore("c_" + e)) for e in ["pe", "act", "dve", "pool"]}
        dsem = {}
        for i, k in enumerate(sorted(T.dma_cnt.keys(), key=str)):
            dsem[k] = es.enter_context(nc.semaphore("d%d" % i))
        with nc.Block() as block:
            @block.tensor
            def _(e):
                T.emit("pe", e, csem, dsem)

            @block.scalar
            def _(e):
                T.emit("act", e, csem, dsem)

            @block.vector
            def _(e):
                T.emit("dve", e, csem, dsem)

            @block.gpsimd
            def _(e):
                T.emit("pool", e, csem, dsem)

            @block.sync
            def _(e):
                T.emit("sp", e, csem, dsem)
    return nc, dbg


def _t5_bucket(rel):
    half = N_BUCKETS // 2
    max_exact = half // 2
    n = np.abs(rel)
    side = np.where(rel > 0, half, 0)
    nf = np.maximum(n, 1).astype(np.float32)
    large = max_exact + (np.log(nf / np.float32(max_exact)) / np.float32(math.log(MAX_DISTANCE / max_exact))
                         * np.float32(half - max_exact)).astype(np.int32)
    large = np.minimum(large, half - 1)
    return side + np.where(n < max_exact, n, large)


def _bias_index_tiles():
    k = np.arange(128)[:, None]
    q = np.arange(128)[None, :]
    a_idx = np.zeros((3, 128, 128), np.int64)
    for oi, o in enumerate((-1, 0, 1)):
        rel = o * 128 + k - q
        valid = np.abs(rel) <= 128
        a_idx[oi] = np.where(valid, _t5_bucket(rel), N_BUCKETS)
    b_idx = np.zeros((NB_TILES, 128, 128), np.int64)
    for gi in range(2):
        window, dil = B_PAT[gi]
        omin, omax = B_OFFS[gi]
        for o in range(omin, omax + 1):
            rel = o * 128 + k - q
            valid = (rel % dil == 0) & (np.abs(rel) <= (window // 2))
            b_idx[B_EBASE[gi] + o - omin] = np.where(valid, _t5_bucket(rel), N_BUCKETS)
    window, dil = B_PAT[2]
    rel_sub = k - q
    valid = np.abs(rel_sub) <= (window // (2 * dil))
    b_idx[B_EBASE[2]] = np.where(valid, _t5_bucket(rel_sub * dil), N_BUCKETS)
    return a_idx, b_idx


_NC_CACHE = {}


def _host_inputs(inputs):
    f = lambda a: np.ascontiguousarray(np.asarray(a, dtype=np.float32))
    table = f(inputs["rel_bias_table"])
    table_ext = np.concatenate([table, np.full((1, table.shape[1]), NEG, np.float32)], axis=0)
    a_idx, b_idx = _bias_index_tiles()
    biasA = np.zeros((2, 128, 12, 128), np.float32)
    for g in range(2):
        for h in range(4):
            for oi in range(3):
                biasA[g, :, oi * 4 + h, :] = table_ext[a_idx[oi], 4 * g + h]
    biasB = np.zeros((4, 128, NB_TILES, 128), np.float32)
    for j in range(4):
        for gi in range(3):
            if gi < 2:
                nt_ = B_OFFS[gi][1] - B_OFFS[gi][0] + 1
                for t in range(B_EBASE[gi], B_EBASE[gi] + nt_):
                    biasB[j, :, t, :] = table_ext[b_idx[t], 8 + gi * 4 + j]
            else:
                for t in range(B_EBASE[2], B_EBASE[2] + 4):
                    biasB[j, :, t, :] = table_ext[b_idx[B_EBASE[2]], 8 + gi * 4 + j]
    gains = np.stack([f(inputs["attn_norm"])[0], f(inputs["ffn_norm"])[0], f(inputs["ple_norm"])[0]], 0)
    gfm = np.ascontiguousarray(gains.reshape(3, 16, 128).transpose(2, 0, 1)).reshape(128, 48)
    cw = np.concatenate([f(inputs["conv_w"])[0], f(inputs["conv_b"])], 0)
    convp = np.ascontiguousarray(cw.reshape(4, NCC, 128).transpose(2, 0, 1)).reshape(128, 4 * NCC)
    sinkb = np.ascontiguousarray(np.broadcast_to(f(inputs["sink_a"])[0][None, :], (128, 8)))
    shared = {
        "w_in": f(inputs["w_in"])[0], "w_ba": f(inputs["w_branch_a"])[0], "w_bb": f(inputs["w_branch_b"])[0],
        "w_out": f(inputs["w_out"])[0], "w_g": f(inputs["w_ffn_gate"])[0], "w_u": f(inputs["w_ffn_up"])[0],
        "w_d": f(inputs["w_ffn_down"])[0], "w_pg": f(inputs["w_ple_gate"])[0], "w_pp": f(inputs["w_ple_proj"])[0],
        "gfm": gfm, "gfin": f(inputs["final_norm"]).reshape(1, D), "convp": convp, "sinkb": sinkb,
        "biasA": biasA.reshape(2, 128, 12 * 128), "biasB": biasB.reshape(4, 128, NB_TILES * 128),
        "ident": np.eye(128, dtype=np.float32).astype(ml_dtypes.bfloat16),
    }
    return shared


def kernel(**inputs):
    x = np.asarray(inputs["x"], dtype=np.float32)
    p = np.asarray(inputs["p"], dtype=np.float32)
    shared = _host_inputs(inputs)
    if "nc" not in _NC_CACHE:
        _NC_CACHE["nc"] = build_nc()[0]
    nc = _NC_CACHE["nc"]
    n = x.shape[0]
    in_maps = []
    for b in range(n):
        m = dict(shared)
        m["x"] = np.ascontiguousarray(x[b])
        m["p"] = np.ascontiguousarray(p[0, b])
        in_maps.append(m)
    res = run_bass_kernel_spmd(nc, in_maps, core_ids=list(range(n)))
    return np.stack([np.asarray(r["out"], dtype=np.float32) for r in res.results], 0)
```

```python
import math
from contextlib import ExitStack

import numpy as np
import ml_dtypes
import concourse.bass as bass
import concourse.mybir as mybir
from concourse.bass_utils import run_bass_kernel_spmd

F32 = mybir.dt.float32
BF16 = mybir.dt.bfloat16
AF = mybir.ActivationFunctionType
ALU = mybir.AluOpType

S = 2048
D = 2048
NT = 16
KC = 16
DFF = 5632
NCC = 44
HD = 128
SCALE = HD ** -0.5
EPS = 1e-6
NEG = -30000.0
N_BUCKETS = 32
MAX_DISTANCE = 1024

QA0, KA0, VA0 = 0, 1024, 1280
QB0, KB0, VB0 = 1536, 3072, 4608
GA0, GB0 = 6144, 8192

B_PAT = ((128, 1), (512, 4), (2048, 16))
B_OFFS = ((-1, 1), (-2, 2))
B_EBASE = (0, 3, 8)
NB_TILES = 12

ENGS = ["pe", "act", "dve", "pool", "sp"]


class Tracker:
    def __init__(self):
        self.ops = {e: [] for e in ENGS}
        self.res = {}
        self.dma_cnt = {}

    def add(self, eng, fn, reads=(), writes=(), dma_key=None):
        idx = len(self.ops[eng])
        ref = (eng, idx)
        deps = set()
        for r in reads:
            st = self.res.get(r)
            if st is not None and st[0] is not None:
                deps.add(st[0])
            if st is not None and isinstance(r, tuple) and r[0] == "ps":
                deps.update(v for k_, v in st[1].items() if k_ != eng)
        for w in writes:
            st = self.res.get(w)
            if st is not None:
                if st[0] is not None:
                    deps.add(st[0])
                deps.update(st[1].values())
                deps.update(st[2])
        isdma = dma_key is not None
        if eng == "pe":
            deps = {d for d in deps if not (d[0] == "pe")}
        deps.discard(ref)
        for r in reads:
            st = self.res.setdefault(r, [None, {}, []])
            if isdma:
                st[2].append(ref)
            else:
                st[1][eng] = ref
        for w in writes:
            self.res[w] = [ref, {}, []]
        op = {"fn": fn, "deps": deps, "key": dma_key, "inc": False}
        if isdma:
            c = self.dma_cnt.get(dma_key, 0) + 1
            self.dma_cnt[dma_key] = c
            op["dval"] = 16 * c
        self.ops[eng].append(op)
        return ref

    def barrier(self):
        last = {}
        for e in ENGS:
            for i in range(len(self.ops[e]) - 1, -1, -1):
                if self.ops[e][i]["fn"] is not None and self.ops[e][i]["key"] is None:
                    last[e] = (e, i)
                    break
        dmas = {}
        for e in ENGS:
            for i, op in enumerate(self.ops[e]):
                if op["key"] is not None:
                    dmas[op["key"]] = (e, i)
        for e in ENGS:
            deps = set(v for k, v in last.items() if k != e) | set(dmas.values())
            self.ops[e].append({"fn": None, "deps": deps, "key": None, "inc": False})

    def finalize(self):
        for e in ENGS:
            for op in self.ops[e]:
                for d in op["deps"]:
                    dop = self.ops[d[0]][d[1]]
                    if dop["key"] is None:
                        dop["inc"] = True
        for e in ENGS:
            c = 0
            for op in self.ops[e]:
                if op["key"] is None and op["inc"]:
                    c += 1
                op["val"] = c

    def emit(self, eng, engobj, csem, dsem):
        waited = {}
        n_wait = 0
        for op in self.ops[eng]:
            need = {}
            for d in op["deps"]:
                dop = self.ops[d[0]][d[1]]
                if dop["key"] is not None:
                    k = ("d", dop["key"])
                    v = dop["dval"]
                else:
                    k = ("c", d[0])
                    v = dop["val"]
                if need.get(k, 0) < v:
                    need[k] = v
            for k, v in need.items():
                if waited.get(k, 0) < v:
                    sem = dsem[k[1]] if k[0] == "d" else csem[k[1]]
                    engobj.wait_ge(sem, v)
                    waited[k] = v
                    n_wait += 1
            if op["fn"] is None:
                continue
            ins = op["fn"](engobj)
            if op["key"] is not None:
                ins.then_inc(dsem[op["key"]], 16)
            elif op["inc"]:
                ins.then_inc(csem[eng], 1)
        return n_wait


def I_act(out, in_, func, **kw):
    return lambda e: e.activation(out=out, in_=in_, func=func, **kw)


def I_mm(out, lhsT, rhs, start=True, stop=True):
    return lambda e: e.matmul(out, lhsT=lhsT, rhs=rhs, start=start, stop=stop, skip_group_check=True)


def I_tr(out, in_, ident):
    return lambda e: e.transpose(out=out, in_=in_, identity=ident)


def I_tt(out, in0, in1, op):
    return lambda e: e.tensor_tensor(out=out, in0=in0, in1=in1, op=op)


def I_ts(out, in0, s1, s2, op0, op1):
    return lambda e: e.tensor_scalar(out=out, in0=in0, scalar1=s1, scalar2=s2, op0=op0, op1=op1)


def I_stt(out, in0, scalar, in1, op0, op1):
    return lambda e: e.scalar_tensor_tensor(out=out, in0=in0, scalar=scalar, in1=in1, op0=op0, op1=op1)


def I_copy(out, in_):
    return lambda e: e.tensor_copy(out=out, in_=in_)


def I_acopy(out, in_):
    return lambda e: e.copy(out=out, in_=in_)


def I_amul(out, in_, mul):
    return lambda e: e.mul(out=out, in_=in_, mul=mul)


def I_recip(out, in_):
    return lambda e: e.reciprocal(out=out, in_=in_)


def I_dma(out, in_):
    return lambda e: e.dma_start(out=out, in_=in_)


def I_memset(ap, c):
    return lambda e: e.memset(ap, c)


class Region:
    def __init__(self, t, nbytes):
        self.t = t
        self.nbytes = nbytes
        self.off = 0

    def reset(self):
        self.off = 0

    def alloc(self, shape, dtype):
        n = 1
        for s_ in shape:
            n *= s_
        esz = 4 if dtype == F32 else 2
        nb = n * esz
        self.off = (self.off + 63) // 64 * 64
        assert self.off + nb <= self.nbytes, ("region overflow", self.off, nb, self.nbytes)
        ap = self.t[:, self.off // 2:(self.off + nb) // 2]
        self.off += nb
        if dtype == F32:
            ap = ap.bitcast(F32)
        if len(shape) == 2:
            ap = ap.rearrange("p (a b) -> p a b", a=shape[0])
        elif len(shape) == 3:
            ap = ap.rearrange("p (a b c) -> p a b c", a=shape[0], b=shape[1])
        return ap


def build_nc(stop_after=None):
    nc = bass.Bass("TRN2", target_bir_lowering=False)

    def din(name, shape, dt=F32):
        return nc.dram_tensor(name, list(shape), dt, kind="ExternalInput").ap()

    x_d = din("x", [S, D])
    p_d = din("p", [S, 256])
    w_in = din("w_in", [D, 10240])
    w_ba = din("w_ba", [1024, D])
    w_bb = din("w_bb", [512, D])
    w_out = din("w_out", [D, D])
    w_g = din("w_g", [D, DFF])
    w_u = din("w_u", [D, DFF])
    w_d = din("w_d", [DFF, D])
    w_pg = din("w_pg", [D, D])
    w_pp = din("w_pp", [256, D])
    gfm_d = din("gfm", [128, 3 * 16])
    gfin_d = din("gfin", [1, D])
    convp_d = din("convp", [128, 4 * NCC])
    sink_d = din("sinkb", [128, 8])
    biasA_d = din("biasA", [2, 128, 12 * 128])
    biasB_d = din("biasB", [4, 128, NB_TILES * 128])
    ident_d = din("ident", [128, 128], BF16)
    out_d = nc.dram_tensor("out", [S, D], F32, kind="ExternalOutput").ap()
    x1s = nc.dram_tensor("x1s", [S, D], F32, kind=("ExternalOutput" if stop_after == "B" else "Internal")).ap()
    dbg = {}

    T = Tracker()

    with ExitStack() as es:
        PB = 112 * 1024
        WB = 16 * 1024
        SB = 74 * 1024
        CB = 3 * 1024
        Pt = es.enter_context(nc.sbuf_tensor("Pt", [128, PB // 2], BF16))
        Wt = es.enter_context(nc.sbuf_tensor("Wt", [128, WB // 2], BF16))
        St = es.enter_context(nc.sbuf_tensor("St", [128, SB // 2], BF16))
        Ct = es.enter_context(nc.sbuf_tensor("Ct", [128, CB // 2], BF16))
        banks = [es.enter_context(nc.psum_tensor("ps%d" % i, [128, 512], F32)) for i in range(8)]
        RP = Region(Pt, PB)
        RS = Region(St, SB)
        RC = Region(Ct, CB)

        ident = RC.alloc((128,), BF16)
        ones = RC.alloc((128,), BF16)
        sel0 = RC.alloc((128,), BF16)
        epsb = RC.alloc((1,), F32)
        gfm = RC.alloc((3, 16), F32)
        convp = RC.alloc((4, NCC), F32)
        esink = RC.alloc((8,), F32)
        stat = RC.alloc((3, 24), F32)

        T.add("sp", I_dma(ident, ident_d[:, :]), writes=["ident"], dma_key="c0")
        T.add("sp", I_dma(gfm, gfm_d[:, :].rearrange("p (a b) -> p a b", a=3)), writes=["gfm"], dma_key="c1")
        T.add("sp", I_dma(convp, convp_d[:, :].rearrange("p (a b) -> p a b", a=4)), writes=["convp"], dma_key="c2")
        T.add("sp", I_dma(esink, sink_d[:, :]), writes=["esink"], dma_key="c3")
        T.add("dve", I_memset(ones, 1.0), writes=["ones"])
        T.add("dve", I_memset(sel0, 0.0), writes=["sel0"])
        T.add("dve", I_memset(sel0[0:1, :], 1.0), reads=["sel0"], writes=["sel0"])
        T.add("dve", I_memset(epsb, EPS), writes=["epsb"])
        T.add("act", I_act(esink, esink, AF.Exp), reads=["esink"], writes=["esink"])

        wslots = [Wt[:, i * 2048:(i + 1) * 2048] for i in range(4)]
        wctr = [0]

        def wslot():
            i = wctr[0] % 4
            wctr[0] += 1
            return i, wslots[i]

        preloaded = {}

        def prefetch_w(src2d, c0, ncols=128, nk=KC):
            preloaded[(id(src2d), c0)] = load_w_cols(src2d, c0, ncols, nk)

        def load_w_cols(src2d, c0, ncols=128, nk=KC):
            if (id(src2d), c0) in preloaded:
                return preloaded.pop((id(src2d), c0))
            i, sl = wslot()
            v = sl[:, 0:nk * ncols].rearrange("p (k c) -> p k c", k=nk)
            T.add("pool", I_dma(v, src2d[:, c0:c0 + ncols].rearrange("(k p) c -> p k c", p=128)),
                  writes=[("w", i)], dma_key=("w", i))
            return ("w", i), v

        bankctr = [0]

        def next_bank(lst):
            b = lst[bankctr[0] % len(lst)]
            bankctr[0] += 1
            return b

        evctr = [0]

        def evac_eng():
            evctr[0] += 1
            return "act" if evctr[0] % 2 else "dve"

        def evac_copy(eng, out, in_, reads, writes):
            if eng == "act":
                T.add("act", I_acopy(out, in_), reads=reads, writes=writes)
            else:
                T.add("dve", I_copy(out, in_), reads=reads, writes=writes)

        def rms_stats(xt, xres, slot, junk):
            jb = junk[slot % 2]
            T.add("act", I_act(jb, xt, AF.Square, accum_out=stat[:, 0, slot:slot + 1]), reads=[xres],
                  writes=[("ss", slot), ("junk", id(junk), slot % 2)])

        def rms_rstd(s0, s1):
            T.add("act", I_act(stat[:, 1, s0:s1], stat[:, 0, s0:s1], AF.Sqrt, scale=1.0 / D, bias=epsb[:, 0:1]),
                  reads=[("ss", k_) for k_ in range(s0, s1)] + ["epsb"], writes=[("sq", k_) for k_ in range(s0, s1)])
            T.add("dve", I_recip(stat[:, 2, s0:s1], stat[:, 1, s0:s1]),
                  reads=[("sq", k_) for k_ in range(s0, s1)], writes=[("rs", k_) for k_ in range(s0, s1)])

        def rms_apply(xt, xres, gi, xn, xnres, slot, dst_fn, dstres, eng="act", tbanks=(6, 7)):
            rs = stat[:, 2, slot:slot + 1]
            if eng == "act":
                T.add("act", I_amul(xn, xt, rs), reads=[xres, ("rs", slot)], writes=[xnres])
            else:
                T.add("pool", I_tt(xn, xt, rs.to_broadcast([128, D]), ALU.mult),
                      reads=[xres, ("rs", slot)], writes=[xnres])
            for half in range(2):
                b = tbanks[half]
                bv = banks[b][:].bitcast(BF16)
                for j in range(8):
                    kc = half * 8 + j
                    T.add("pe", I_tr(bv[:, j * 128:(j + 1) * 128], xn[:, kc * 128:(kc + 1) * 128], ident),
                          reads=[xnres, "ident"], writes=[("ps", b)])
                g_b = gfm[:, gi, half * 8:(half + 1) * 8].unsqueeze(2).to_broadcast([128, 8, 128])
                T.add("dve", I_tt(dst_fn(half), bv[:, 0:1024].rearrange("p (a b) -> p a b", a=8), g_b, ALU.mult),
                      reads=[("ps", b), "gfm"], writes=[dstres + (half,)])

        hT = RP.alloc((KC, S), BF16)
        yaT = RP.alloc((8, S), BF16)
        ybT = RP.alloc((4, S), BF16)

        RA = Region(Pt, PB)
        RA.off = 64 * 1024
        NXT = 4
        xts = [RA.alloc((D,), F32) for _ in range(NXT)]
        xns = [RA.alloc((D,), BF16) for _ in range(2)]
        junkA = [RA.alloc((D,), BF16) for _ in range(2)]
        assert RA.off <= 112 * 1024
        a2_units = [("A", 0), ("A", 1), ("B", 0), ("B", 1), ("B", 2), ("B", 3)]

        def unit_cols(kind, u):
            if kind == "A":
                return [KA0 + u * 128, VA0 + u * 128] + [QA0 + (4 * u + h) * 128 for h in range(4)]
            cols = []
            for gi in range(3):
                hc = (gi * 4 + u) * 128
                cols += [KB0 + hc, VB0 + hc, QB0 + hc]
            return cols

        if stop_after != "A1":
            for c0 in unit_cols(*a2_units[0])[:4]:
                prefetch_w(w_in, c0)
        def a1_apply(tt):
            bsel = tt % NXT
            rms_rstd(tt, tt + 1)
            rms_apply(xts[bsel], ("xt", bsel), 0, xns[tt % 2], ("xn", tt % 2), tt,
                      lambda half, tt=tt: hT[:, half * 8:(half + 1) * 8, tt * 128:(tt + 1) * 128], ("hT", tt), eng="act")

        for tt in range(NT):
            bsel = tt % NXT
            T.add("sp", I_dma(xts[bsel], x_d[tt * 128:(tt + 1) * 128, :]), writes=[("xt", bsel)], dma_key=("xt", bsel))
            rms_stats(xts[bsel], ("xt", bsel), tt, junkA)
            if tt >= 1:
                a1_apply(tt - 1)
        a1_apply(NT - 1)
        if stop_after == "A1":
            T.barrier()
            dbg["hT"] = nc.dram_tensor("dbg_hT", [128, KC * S], BF16, kind="ExternalOutput").ap()
            T.add("sp", I_dma(dbg["hT"][:, :].rearrange("p (a b) -> p a b", a=KC), hT), dma_key="dbg")

        if stop_after not in ("A1",):
            RS.reset()
            QTf = RS.alloc((4 * S,), BF16)
            QT = QTf.rearrange("p (h t) -> p h t", h=4)
            QTa = QTf.rearrange("p (q h c) -> p q h c", q=NT, h=4)
            U2sb = RS.alloc((S,), BF16)
            D2sb = RS.alloc((S,), BF16)
            KT = RS.alloc((3, S), BF16)
            VTs = [RS.alloc((S,), BF16)]
            Vb = RS.alloc((3, NT, 128), BF16)
            EB = RS.alloc((12, 128), F32)
            Es = [RS.alloc((512,), F32) for _ in range(3)]
            Ps = [RS.alloc((512,), BF16) for _ in range(3)]
            recs = [RS.alloc((128,), F32) for _ in range(2)]
            PROJ_BANKS = [0, 1, 2, 5]
            S_BANKS = [0, 1, 2]
            UD_BANKS = [3, 4]
            vtc = [0]

            def project(c0, dst, dstres, dst_fn=None, deint=False):
                wres, wv = load_w_cols(w_in, c0)
                for tb in range(4):
                    b = next_bank(PROJ_BANKS)
                    for kc in range(KC):
                        T.add("pe", I_mm(banks[b][:, :], wv[:, kc, :], hT[:, kc, tb * 512:(tb + 1) * 512],
                                         start=(kc == 0), stop=(kc == KC - 1)),
                              reads=[wres] + [("hT", 4 * tb + q_, hf_) for q_ in range(4) for hf_ in range(2)],
                              writes=[("ps", b)])
                    if deint:
                        evac_copy(evac_eng(), dst.rearrange("p (r i) -> p r i", r=16)[:, :, 32 * tb:32 * tb + 32],
                                  banks[b][:, :].rearrange("p (i r) -> p r i", r=16), [("ps", b)], [dstres + (tb,)])
                    elif dst_fn is None:
                        evac_copy(evac_eng(), dst[:, tb * 512:(tb + 1) * 512], banks[b][:, :], [("ps", b)], [dstres + (tb,)])
                    else:
                        evac_copy(evac_eng(), dst_fn(tb), banks[b][:, :].rearrange("p (a b) -> p a b", a=4),
                                  [("ps", b)], [dstres + (tb,)])

            def project_v(c0, hv, deint=False):
                i = 0
                project(c0, VTs[i], ("VT", i), deint=deint)
                for half in range(2):
                    b = 6 + half
                    bv = banks[b][:].bitcast(BF16)
                    for j in range(8):
                        kb = half * 8 + j
                        T.add("pe", I_tr(bv[:, j * 128:(j + 1) * 128], VTs[i][:, kb * 128:(kb + 1) * 128], ident),
                              reads=([("VT", i, t_) for t_ in range(4)] if deint else [("VT", i, kb // 4)]) + ["ident"],
                              writes=[("ps", b)])
                    evac_copy(evac_eng(), Vb[:, hv, half * 8:(half + 1) * 8, :],
                              bv[:, 0:1024].rearrange("p (a b) -> p a b", a=8), [("ps", b)], [("V", hv)])

            udc = [0]
            stc = [0]
            mulc = [0]

            def mul_eng():
                mulc[0] += 1
                return "pool" if mulc[0] % 3 == 0 else "dve"

            def attention(groups, inject=False):
                steps = []
                for gidx, (sources, sink_col, dstT, dchunk) in enumerate(groups):
                    for qb in range(NT):
                        first = True
                        lst = []
                        for (qsel, ksel, vsel, ebase, omin, omax) in sources:
                            kbs = [kb for kb in range(qb + omin, qb + omax + 1) if 0 <= kb < NT]
                            for c in range(0, len(kbs), 4):
                                ch = kbs[c:c + 4]
                                lst.append([gidx, qb, qsel, ksel, vsel, ebase + (ch[0] - qb - omin), ch, False, False])
                        lst[0][7] = True
                        lst[-1][8] = True
                        steps.extend(lst)
                LAG = 2
                n = len(steps)
                info = {}
                for s_ in range(n + LAG):
                    if s_ < n:
                        gidx, qb, qsel, ksel, vsel, t0, ch, isfirst, islast = steps[s_]
                        k = stc[0] % 3
                        stc[0] += 1
                        sb = S_BANKS[k]
                        w = len(ch) * 128
                        for i, kb in enumerate(ch):
                            T.add("pe", I_mm(banks[sb][:, i * 128:(i + 1) * 128], KT[:, ksel, kb * 128:(kb + 1) * 128],
                                             QT[:, qsel, qb * 128:(qb + 1) * 128]),
                                  reads=[("KT", ksel, kb // 4), ("QT", qsel, qb // 4)], writes=[("ps", sb)])
                        T.add("act", I_act(Es[k][:, 0:w], banks[sb][:, 0:w], AF.Exp, scale=SCALE),
                              reads=[("ps", sb)], writes=[("E", k)])
                        T.add(mul_eng(), I_tt(Ps[k][:, 0:w], Es[k][:, 0:w],
                                           EB[:, t0:t0 + len(ch), :].rearrange("p a b -> p (a b)"), ALU.mult),
                              reads=[("E", k), "EB"], writes=[("P", k)])
                        info[s_] = k
                    s2 = s_ - LAG
                    if s2 >= 0:
                        gidx, qb, qsel, ksel, vsel, t0, ch, isfirst, islast = steps[s2]
                        k = info[s2]
                        if isfirst:
                            udc[0] += 1
                        ub = UD_BANKS[udc[0] % 2]
                        if isfirst and inject:
                            T.add("pe", I_mm(banks[ub][:, 0:128], ident, U2sb[:, qb * 128:(qb + 1) * 128], True, False),
                                  reads=["ident", "U2sb"], writes=[("ps", ub)])
                            T.add("pe", I_mm(banks[ub][:, 128:256], sel0, D2sb[:, qb * 128:(qb + 1) * 128], False, False),
                                  reads=["sel0", "D2sb"], writes=[("ps", ub)])
                        for i, kb in enumerate(ch):
                            T.add("pe", I_mm(banks[ub][:, 0:128], Vb[:, vsel, kb, :], Ps[k][:, i * 128:(i + 1) * 128],
                                             start=(isfirst and i == 0 and not inject), stop=False),
                                  reads=[("V", vsel), ("P", k)], writes=[("ps", ub)])
                            T.add("pe", I_mm(banks[ub][:, 128:256], ones, Ps[k][:, i * 128:(i + 1) * 128],
                                             start=False, stop=(islast and i == len(ch) - 1)),
                                  reads=["ones", ("P", k)], writes=[("ps", ub)])
                        if islast:
                            sources, sink_col, dstT, dchunk = groups[gidx]
                            r = udc[0] % 2
                            if sink_col is not None:
                                T.add("dve", (lambda e, o=recs[r], i_=banks[ub][:, 128:256], s1=esink[:, sink_col:sink_col + 1]:
                                              e.tensor_scalar_add(out=o, in0=i_, scalar1=s1)),
                                      reads=[("ps", ub), "esink"], writes=[("rec", r)])
                                T.add("dve", I_recip(recs[r], recs[r]), reads=[("rec", r)], writes=[("rec", r)])
                            else:
                                T.add("dve", I_recip(recs[r], banks[ub][:, 128:256]), reads=[("ps", ub)], writes=[("rec", r)])
                            T.add("dve", I_tt(dstT[:, dchunk, qb * 128:(qb + 1) * 128], banks[ub][:, 0:128], recs[r], ALU.mult),
                                  reads=[("ps", ub), ("rec", r)], writes=[("yT", id(dstT), dchunk, qb)])

            def attention_g2():
                KTd = KT[:, 2, :].rearrange("p (r i) -> p r i", r=16)
                QTd = QT[:, 2, :].rearrange("p (r i) -> p r i", r=16)
                U2v = U2sb.rearrange("p (i r) -> p r i", r=16)
                D2v = D2sb.rearrange("p (i r) -> p r i", r=16)
                kq_reads = [("KT", 2, t_) for t_ in range(4)] + [("QT", 2, t_) for t_ in range(4)]
                LAG = 2
                info = {}
                for s_ in range(4 + LAG):
                    if s_ < 4:
                        k = stc[0] % 3
                        stc[0] += 1
                        sb = S_BANKS[k]
                        for c in range(4):
                            r = s_ * 4 + c
                            T.add("pe", I_mm(banks[sb][:, c * 128:(c + 1) * 128], KTd[:, r, :], QTd[:, r, :]),
                                  reads=kq_reads, writes=[("ps", sb)])
                        T.add("act", I_act(Es[k][:, :], banks[sb][:, :], AF.Exp, scale=SCALE),
                              reads=[("ps", sb)], writes=[("E", k)])
                        T.add(mul_eng(), I_tt(Ps[k][:, :].rearrange("p (a b) -> p a b", a=4),
                                           Es[k][:, :].rearrange("p (a b) -> p a b", a=4),
                                           EB[:, B_EBASE[2]:B_EBASE[2] + 4, :], ALU.mult),
                              reads=[("E", k), "EB"], writes=[("P", k)])
                        info[s_] = k
                    s2 = s_ - LAG
                    if s2 >= 0:
                        k = info[s2]
                        for half in range(2):
                            udc[0] += 1
                            ub = UD_BANKS[udc[0] % 2]
                            for c2 in range(2):
                                c = half * 2 + c2
                                r = s2 * 4 + c
                                T.add("pe", I_mm(banks[ub][:, c2 * 256:c2 * 256 + 128], Vb[:, 2, r, :],
                                                 Ps[k][:, c * 128:(c + 1) * 128], c2 == 0, False),
                                      reads=[("V", 2), ("P", k)], writes=[("ps", ub)])
                                T.add("pe", I_mm(banks[ub][:, c2 * 256 + 128:c2 * 256 + 256], ones,
                                                 Ps[k][:, c * 128:(c + 1) * 128], False, c2 == 1),
                                      reads=["ones", ("P", k)], writes=[("ps", ub)])
                            r0 = s2 * 4 + half * 2
                            bview = banks[ub][:, :].rearrange("p (a b) -> p a b", a=2)
                            ee = "act" if half == 0 else "dve"
                            evac_copy(ee, U2v[:, r0:r0 + 2, :], bview[:, :, 0:128], [("ps", ub)], [("U2sb", r0)])
                            evac_copy(ee, D2v[:, r0:r0 + 2, :], bview[:, :, 128:256], [("ps", ub)], [("D2sb", r0)])
                T.add("dve", I_memset(recs[0][:, 0:1], 0.0),
                      reads=[("U2sb", r0_) for r0_ in range(0, 16, 2)] + [("D2sb", r0_) for r0_ in range(0, 16, 2)],
                      writes=["U2sb", "D2sb"])

            def attention_gqa(u):
                steps = []
                for qb in range(NT):
                    kbs = [kb for kb in (qb - 1, qb, qb + 1) if 0 <= kb < NT]
                    for kb in kbs:
                        steps.append((qb, kb, kb == kbs[0], kb == kbs[-1]))
                LAG = 2
                n = len(steps)
                info = {}
                for s_ in range(n + LAG):
                    if s_ < n:
                        qb, kb, isfirst, islast = steps[s_]
                        k = stc[0] % 3
                        stc[0] += 1
                        sb = S_BANKS[k]
                        oi = kb - qb + 1
                        T.add("pe", I_mm(banks[sb][:, :], KT[:, 0, kb * 128:(kb + 1) * 128],
                                         QTa[:, qb, :, :].rearrange("p h c -> p (h c)")),
                              reads=[("KT", 0, kb // 4)] + [("QT", h, qb // 4) for h in range(4)], writes=[("ps", sb)])
                        T.add("act", I_act(Es[k][:, :], banks[sb][:, :], AF.Exp, scale=SCALE),
                              reads=[("ps", sb)], writes=[("E", k)])
                        T.add(mul_eng(), I_tt(Ps[k][:, :], Es[k][:, :],
                                           EB[:, oi * 4:(oi + 1) * 4, :].rearrange("p a b -> p (a b)"), ALU.mult),
                              reads=[("E", k), "EB"], writes=[("P", k)])
                        info[s_] = k
                    s2 = s_ - LAG
                    if s2 >= 0:
                        qb, kb, isfirst, islast = steps[s2]
                        k = info[s2]
                        if isfirst:
                            udc[0] += 1
                        ub = (3, 4)[udc[0] % 2]
                        db = (5, 6)[udc[0] % 2]
                        T.add("pe", I_mm(banks[ub][:, :], Vb[:, 0, kb, :], Ps[k][:, :], isfirst, islast),
                              reads=[("V", 0), ("P", k)], writes=[("ps", ub)])
                        T.add("pe", I_mm(banks[db][:, :], ones, Ps[k][:, :], isfirst, islast),
                              reads=["ones", ("P", k)], writes=[("ps", db)])
                        if islast:
                            r = udc[0] % 2
                            import os
                            if os.environ.get("K_OLDFIN"):
                                for h in range(4):
                                    T.add("dve", (lambda e, o=rec4[r][:, h * 128:(h + 1) * 128], i_=banks[db][:, h * 128:(h + 1) * 128],
                                                  s1=esink[:, 4 * u + h:4 * u + h + 1]: e.tensor_scalar_add(out=o, in0=i_, scalar1=s1)),
                                          reads=[("ps", db), "esink"], writes=[("rec4", r, h)])
                                T.add("dve", I_recip(rec4[r], rec4[r]),
                                      reads=[("rec4", r, h) for h in range(4)], writes=[("rec4", r, h) for h in range(4)])
                            else:
                              for h in range(4):
                                T.add("act", I_act(rec4[r][:, h * 128:(h + 1) * 128], banks[db][:, h * 128:(h + 1) * 128],
                                                   AF.Ln, bias=esink[:, 4 * u + h:4 * u + h + 1]),
                                      reads=[("ps", db), "esink"], writes=[("rec4", r, h)])
                              T.add("act", I_act(rec4[r], rec4[r], AF.Exp, scale=-1.0),
                                  reads=[("rec4", r, h) for h in range(4)], writes=[("rec4", r, h) for h in range(4)])
                            T.add("dve", I_tt(yaT[:, 4 * u:4 * u + 4, qb * 128:(qb + 1) * 128],
                                              banks[ub][:, :].rearrange("p (a b) -> p a b", a=4),
                                              rec4[r].rearrange("p (a b) -> p a b", a=4), ALU.mult),
                                  reads=[("ps", ub)] + [("rec4", r, h) for h in range(4)], writes=[("yaT", u, qb)])

            rec4 = [RS.alloc((512,), F32) for _ in range(2)]
            for ui, (kind, u) in enumerate(a2_units):
                if kind == "A":
                    ntile = 12
                    T.add("sp", I_dma(EB[:, 0:ntile, :], biasA_d[u].rearrange("p (a b) -> p a b", a=ntile)),
                          writes=["EB"], dma_key="EB")
                else:
                    ntile = NB_TILES
                    T.add("sp", I_dma(EB[:, 0:ntile, :], biasB_d[u].rearrange("p (a b) -> p a b", a=ntile)),
                          writes=["EB"], dma_key="EB")
                T.add("act", I_act(EB[:, 0:ntile, :], EB[:, 0:ntile, :], AF.Exp), reads=["EB"], writes=["EB"])
                if kind == "A":
                    project(KA0 + u * 128, KT[:, 0, :], ("KT", 0))
                    project_v(VA0 + u * 128, 0)
                    for h in range(4):
                        project(QA0 + (4 * u + h) * 128, None, ("QT", h),
                                dst_fn=lambda tb, h=h: QTa[:, tb * 4:(tb + 1) * 4, h, :])
                else:
                    for gi in range(3):
                        hc = (gi * 4 + u) * 128
                        project(KB0 + hc, KT[:, gi, :], ("KT", gi), deint=(gi == 2))
                        project_v(VB0 + hc, gi, deint=(gi == 2))
                        project(QB0 + hc, QT[:, gi, :], ("QT", gi), deint=(gi == 2))
                if ui + 1 < len(a2_units):
                    for c0 in unit_cols(*a2_units[ui + 1])[:4]:
                        prefetch_w(w_in, c0)
                if kind == "A":
                    attention_gqa(u)
                else:
                    attention_g2()
                    srcs = [(gi, gi, gi, B_EBASE[gi], B_OFFS[gi][0], B_OFFS[gi][1]) for gi in range(2)]
                    attention([(srcs, None, ybT, u)], inject=True)
            T.barrier()
            if stop_after == "A2":
                dbg["yaT"] = nc.dram_tensor("dbg_yaT", [128, 8 * S], BF16, kind="ExternalOutput").ap()
                dbg["ybT"] = nc.dram_tensor("dbg_ybT", [128, 4 * S], BF16, kind="ExternalOutput").ap()
                T.add("sp", I_dma(dbg["yaT"][:, :].rearrange("p (a b) -> p a b", a=8), yaT), dma_key="dbg")
                T.add("sp", I_dma(dbg["ybT"][:, :].rearrange("p (a b) -> p a b", a=4), ybT), dma_key="dbg2")

        if stop_after not in ("A1", "A2"):
            RS.reset()
            mergedT = RS.alloc((KC, 1024), BF16)
            WoA = RS.alloc((KC, 512), BF16)
            WoB = Wt[:, 0:KC * 512].rearrange("p (k c) -> p k c", k=KC)
            WOS = [(WoA, [("WoA",)]), (WoB, [("w", i) for i in range(4)])]
            sg = [RS.alloc((512,), F32) for _ in range(4)]
            xs = [RS.alloc((512,), F32) for _ in range(3)]
            os_ = [RS.alloc((512,), F32) for _ in range(3)]
            MB = [0, 1, 2, 3, 4, 5, 6, 7]
            cB = [0]
            for th in range(2):
                for fc in range(KC):
                    rga, wga = load_w_cols(w_in, GA0 + fc * 128)
                    rgb, wgb = load_w_cols(w_in, GB0 + fc * 128)
                    rba, wba = load_w_cols(w_ba, fc * 128, nk=8)
                    rbb, wbb = load_w_cols(w_bb, fc * 128, nk=4)
                    bk = [[next_bank(MB) for _ in range(2)] for _ in range(4)]
                    for wi_, (wres_, wv_, src_, nk_) in enumerate(((rga, wga, hT, KC), (rgb, wgb, hT, KC),
                                                                   (rba, wba, yaT, 8), (rbb, wbb, ybT, 4))):
                        for tb2 in range(2):
                            t0 = th * 1024 + tb2 * 512
                            for kc in range(nk_):
                                T.add("pe", I_mm(banks[bk[wi_][tb2]][:, :], wv_[:, kc, :], src_[:, kc, t0:t0 + 512],
                                                 kc == 0, kc == nk_ - 1),
                                      reads=[wres_], writes=[("ps", bk[wi_][tb2])])
                    for tb2 in range(2):
                        bga, bgb, bza, bzb = (bk[w_][tb2] for w_ in range(4))
                        i0 = (cB[0] % 2) * 2
                        cB[0] += 1
                        T.add("act", I_act(sg[i0], banks[bga][:, :], AF.Sigmoid), reads=[("ps", bga)], writes=[("sg", i0)])
                        T.add("act", I_act(sg[i0 + 1], banks[bgb][:, :], AF.Sigmoid), reads=[("ps", bgb)], writes=[("sg", i0 + 1)])
                        T.add("dve", I_tt(sg[i0], sg[i0], banks[bza][:, :], ALU.mult),
                              reads=[("sg", i0), ("ps", bza)], writes=[("sg", i0)])
                        T.add("dve", I_tt(sg[i0 + 1], sg[i0 + 1], banks[bzb][:, :], ALU.mult),
                              reads=[("sg", i0 + 1), ("ps", bzb)], writes=[("sg", i0 + 1)])
                        T.add("dve", I_tt(mergedT[:, fc, tb2 * 512:(tb2 + 1) * 512], sg[i0], sg[i0 + 1], ALU.add),
                              reads=[("sg", i0), ("sg", i0 + 1)], writes=[("mg", fc, tb2)])
                its = [(fb, tt) for fb in range(4) for tt in range(8)]

                def xload(it):
                    fb, tt = its[it]
                    trow = th * 1024 + tt * 128
                    xi = it % 3
                    T.add("sp", I_dma(xs[xi], x_d[trow:trow + 128, fb * 512:(fb + 1) * 512]),
                          writes=[("xs", xi)], dma_key=("xs", xi))

                xload(0)
                xload(1)
                for it, (fb, tt) in enumerate(its):
                    Wo_, wres_ = WOS[fb % 2]
                    if tt == 0:
                        T.add("pool", I_dma(Wo_, w_out[:, fb * 512:(fb + 1) * 512].rearrange("(k p) c -> p k c", p=128)),
                              writes=wres_, dma_key=("Wo", fb % 2))
                    trow = th * 1024 + tt * 128
                    b = next_bank(MB)
                    xi = it % 3
                    if it + 2 < len(its):
                        xload(it + 2)
                    for kc in range(KC):
                        T.add("pe", I_mm(banks[b][:, :], mergedT[:, kc, tt * 128:(tt + 1) * 128], Wo_[:, kc, :],
                                         kc == 0, kc == KC - 1),
                              reads=wres_ + [("mg", kc, tt // 4)], writes=[("ps", b)])
                    T.add("dve", I_tt(os_[xi], banks[b][:, :], xs[xi], ALU.add),
                          reads=[("ps", b), ("xs", xi)], writes=[("os", xi)])
                    T.add("sp", I_dma(x1s[trow:trow + 128, fb * 512:(fb + 1) * 512], os_[xi]),
                          reads=[("os", xi)], writes=[("x1s", trow, fb)], dma_key=("os", xi))
            T.barrier()

        if stop_after not in ("A1", "A2", "B"):
            RP.reset()
            X1 = RP.alloc((8, D), F32)
            hfT = RP.alloc((KC, 1032), BF16)
            pT = RP.alloc((2, 1024), BF16)
            pf = [RP.alloc((256,), F32) for _ in range(2)]
            pb = [RP.alloc((256,), BF16) for _ in range(2)]

            def p_load(th, tt):
                trow = th * 1024 + tt * 128
                pi = tt % 2
                T.add("sp", I_dma(pf[pi], p_d[trow:trow + 128, :]), writes=[("pf", pi)], dma_key=("pf", pi))
                T.add("act", I_acopy(pb[pi], pf[pi]), reads=[("pf", pi)], writes=[("pb", pi)])

            def p_transpose(th, tt):
                if True:
                    pi = tt % 2
                    b = 5
                    bv = banks[b][:].bitcast(BF16)
                    for j in range(2):
                        T.add("pe", I_tr(bv[:, j * 128:(j + 1) * 128], pb[pi][:, j * 128:(j + 1) * 128], ident),
                              reads=[("pb", pi), "ident"], writes=[("ps", b)])
                    T.add("act", I_acopy(pT[:, :, tt * 128:(tt + 1) * 128], bv[:, 0:256].rearrange("p (a b) -> p a b", a=2)),
                          reads=[("ps", b)], writes=[("pT", tt)])
            xnC = [RP.alloc((D,), BF16) for _ in range(2)]
            RS.reset()
            junkC = [RS.alloc((D,), BF16) for _ in range(2)]
            xh = RS.alloc((D,), F32)
            halT = RS.alloc((KC, 128), BF16)
            Wpg = [RS.alloc((KC, 256), BF16) for _ in range(2)]
            Wpp = [RS.alloc((2, 256), BF16) for _ in range(2)]
            sg3 = [RS.alloc((256,), F32) for _ in range(2)]
            tm3 = [RS.alloc((256,), F32) for _ in range(2)]
            gB = RS.alloc((D,), F32)
            ost = [RS.alloc((D,), F32) for _ in range(2)]
            xn1s = xnC
            xn3s = xnC
            junk1 = junkC
            junk3 = junkC

            def c1_load_halo(th_):
                hrow_ = 1024 if th_ == 0 else 896
                T.add("sp", I_dma(xh, x1s[hrow_:hrow_ + 128, :]), writes=["xh"], dma_key="xh")

            def c1_load_tile(th_, tt):
                trow_ = th_ * 1024 + tt * 128
                T.add("sp", I_dma(X1[:, tt, :], x1s[trow_:trow_ + 128, :]), writes=[("X1", tt)], dma_key=("X1", tt))

            for th in range(2):
                hcol = 0 if th == 0 else 127
                if th == 0:
                    c1_load_halo(0)
                    for tt in range(8):
                        c1_load_tile(0, tt)
                rms_stats(xh, "xh", 0, junk1)
                for tt in range(4):
                    rms_stats(X1[:, tt, :], ("X1", tt), tt + 1, junk1)
                rms_rstd(0, 5)
                rms_apply(xh, "xh", 1, xn1s[0], ("xnC", 0), 0,
                          lambda half: halT[:, half * 8:(half + 1) * 8, :], ("halT",), eng="act")
                T.add("dve", I_copy(hfT[:, :, 1024:1025], halT[:, :, hcol:hcol + 1]), reads=[("halT", 0), ("halT", 1)],
                      writes=[("hfT", -1)])
                for tt in range(4, 8):
                    rms_stats(X1[:, tt, :], ("X1", tt), tt + 1, junk1)
                rms_rstd(5, 9)
                for tt in range(8):
                    rms_apply(X1[:, tt, :], ("X1", tt), 1, xn1s[(tt + 1) % 2], ("xnC", (tt + 1) % 2), tt + 1,
                              lambda half, tt=tt: hfT[:, half * 8:(half + 1) * 8, tt * 128:(tt + 1) * 128],
                              ("hfT", tt), eng="act")
                T.barrier()
                RS.reset()
                aT = RS.alloc((11, 1024), BF16)
                Wd = [RS.alloc((11, 512), BF16) for _ in range(2)]
                G = [RS.alloc((1026,), F32) for _ in range(2)]
                ACC = [RS.alloc((1024,), F32) for _ in range(2)]
                GEL = [RS.alloc((1024,), F32) for _ in range(2)]
                FB = [0, 1, 2, 3, 4, 5, 6]
                cC = [0]
                for qd in range(4):
                    for cl in range(11):
                        c = qd * 11 + cl
                        rg_, wg_ = load_w_cols(w_g, c * 128)
                        ru_, wu_ = load_w_cols(w_u, c * 128)
                        gi = cC[0] % 2
                        cC[0] += 1
                        bg = [next_bank(FB), next_bank(FB)]
                        for tb2 in range(2):
                            for kc in range(KC):
                                T.add("pe", I_mm(banks[bg[tb2]][:, :], wg_[:, kc, :],
                                                 hfT[:, kc, tb2 * 512:(tb2 + 1) * 512], kc == 0, kc == KC - 1),
                                      reads=[rg_], writes=[("ps", bg[tb2])])
                        for kc in range(KC):
                            T.add("pe", I_mm(banks[7][:, 0:1], wg_[:, kc, :], hfT[:, kc, 1024:1025], kc == 0, kc == KC - 1),
                                  reads=[rg_], writes=[("ps", 7)])
                        bu = [next_bank(FB), next_bank(FB)]
                        for tb2 in range(2):
                            for kc in range(KC):
                                T.add("pe", I_mm(banks[bu[tb2]][:, :], wu_[:, kc, :],
                                                 hfT[:, kc, tb2 * 512:(tb2 + 1) * 512], kc == 0, kc == KC - 1),
                                      reads=[ru_], writes=[("ps", bu[tb2])])
                        Gt = G[gi]
                        T.add("act", I_acopy(Gt[:, 1:513], banks[bg[0]][:, :]), reads=[("ps", bg[0])], writes=[("G", gi, 0)])
                        T.add("act", I_acopy(Gt[:, 513:1025], banks[bg[1]][:, :]), reads=[("ps", bg[1])], writes=[("G", gi, 1)])
                        if th == 0:
                            T.add("act", I_acopy(Gt[:, 1025:1026], banks[7][:, 0:1]), reads=[("ps", 7)], writes=[("G", gi, 2)])
                            T.add("dve", I_memset(Gt[:, 0:1], 0.0), writes=[("G", gi, 3)])
                        else:
                            T.add("act", I_acopy(Gt[:, 0:1], banks[7][:, 0:1]), reads=[("ps", 7)], writes=[("G", gi, 2)])
                            T.add("dve", I_memset(Gt[:, 1025:1026], 0.0), writes=[("G", gi, 3)])
                        gres = [("G", gi, k_) for k_ in range(4)]
                        A_ = ACC[gi]
                        T.add("dve", I_ts(A_, Gt[:, 1:1025], convp[:, 1, c:c + 1], convp[:, 3, c:c + 1], ALU.mult, ALU.add),
                              reads=gres + ["convp"], writes=[("ACC", gi)])
                        T.add("dve", I_stt(A_, Gt[:, 0:1024], convp[:, 0, c:c + 1], A_, ALU.mult, ALU.add),
                              reads=gres + [("ACC", gi)], writes=[("ACC", gi)])
                        T.add("dve", I_stt(A_, Gt[:, 2:1026], convp[:, 2, c:c + 1], A_, ALU.mult, ALU.add),
                              reads=gres + [("ACC", gi)], writes=[("ACC", gi)])
                        T.add("act", I_act(GEL[gi], A_, AF.Gelu_apprx_tanh), reads=[("ACC", gi)], writes=[("GEL", gi)])
                        for tb2 in range(2):
                            T.add("dve", I_tt(aT[:, cl, tb2 * 512:(tb2 + 1) * 512], GEL[gi][:, tb2 * 512:(tb2 + 1) * 512],
                                              banks[bu[tb2]][:, :], ALU.mult),
                                  reads=[("GEL", gi), ("ps", bu[tb2])], writes=[("aT", cl, tb2)])
                        if c < 8:
                            p_load(th, c)
                        if 1 <= c < 9:
                            p_transpose(th, c - 1)
                    for fb in range(4):
                        wi = cC[0] % 2
                        cC[0] += 1
                        T.add("pool", I_dma(Wd[wi], w_d[qd * 11 * 128:(qd + 1) * 11 * 128, fb * 512:(fb + 1) * 512]
                                            .rearrange("(c p) f -> p c f", p=128)),
                              writes=[("Wd", wi)], dma_key=("Wd", wi))
                        for t2 in range(2):
                            bd = [next_bank(FB) for _ in range(4)]
                            for cl in range(11):
                                for t4 in range(4):
                                    tt = t2 * 4 + t4
                                    T.add("pe", I_mm(banks[bd[t4]][:, :], aT[:, cl, tt * 128:(tt + 1) * 128], Wd[wi][:, cl, :],
                                                     cl == 0, cl == 10),
                                          reads=[("Wd", wi), ("aT", cl, tt // 4)], writes=[("ps", bd[t4])])
                            for t4 in range(4):
                                tt = t2 * 4 + t4
                                xsl = X1[:, tt, fb * 512:(fb + 1) * 512]
                                T.add("dve", I_tt(xsl, banks[bd[t4]][:, :], xsl, ALU.add),
                                      reads=[("ps", bd[t4]), ("X1", tt)], writes=[("X1", tt)])
                T.barrier()
                T.add("sp", I_dma(gB, gfin_d.partition_broadcast(128)), writes=["gB"], dma_key="gB")
                for tt in range(8):
                    rms_stats(X1[:, tt, :], ("X1", tt), tt, junk3)
                rms_rstd(0, 8)
                for tt in range(8):
                    rms_apply(X1[:, tt, :], ("X1", tt), 2, xn3s[tt % 2], ("xnC", tt % 2), tt,
                              lambda half, tt=tt: hfT[:, half * 8:(half + 1) * 8, tt * 128:(tt + 1) * 128],
                              ("hfT", tt), eng="act")
                PB_ = [0, 1, 2, 3, 4]
                c3 = [0]

                def c3_final(tt):
                    rms_stats(X1[:, tt, :], ("X1", tt), 16 + tt, junk3)
                    if tt % 4 != 3:
                        return
                    rms_rstd(16 + tt - 3, 16 + tt + 1)
                    if th == 0 and tt == 3:
                        c1_load_halo(1)
                    for t_ in range(tt - 3, tt + 1):
                        trow = th * 1024 + t_ * 128
                        oi = t_ % 2
                        T.add("dve", I_stt(ost[oi], X1[:, t_, :], stat[:, 2, 16 + t_:17 + t_], gB, ALU.mult, ALU.mult),
                              reads=[("X1", t_), ("rs", 16 + t_), "gB"], writes=[("ost", oi)])
                        T.add("sp", I_dma(out_d[trow:trow + 128, :], ost[oi]), reads=[("ost", oi)], writes=[("out", trow)],
                              dma_key=("ost", oi))
                        if th == 0:
                            c1_load_tile(1, t_)

                def c3_wload(fb):
                    wi = fb % 2
                    T.add("pool", I_dma(Wpg[wi], w_pg[:, fb * 256:(fb + 1) * 256].rearrange("(k p) c -> p k c", p=128)),
                          writes=[("Wpg", wi)], dma_key=("Wpg", wi))
                    T.add("pool", I_dma(Wpp[wi], w_pp[:, fb * 256:(fb + 1) * 256].rearrange("(k p) c -> p k c", p=128)),
                          writes=[("Wpp", wi)], dma_key=("Wpp", wi))

                order = [(fb, tt) for fb in range(6) for tt in range(8)]
                order += [(fb, tt) for fb in (6, 7) for tt in range(4)] + [(fb, tt) for fb in (6, 7) for tt in range(4, 8)]
                loaded = set()
                for (fb, tt) in order:
                    wi = fb % 2
                    if fb not in loaded:
                        c3_wload(fb)
                        loaded.add(fb)
                        if fb == 6:
                            c3_wload(7)
                            loaded.add(7)
                    if True:
                        ba = next_bank(PB_)
                        for kc in range(KC):
                            T.add("pe", I_mm(banks[ba][:, 0:256], hfT[:, kc, tt * 128:(tt + 1) * 128], Wpg[wi][:, kc, :],
                                             kc == 0, kc == KC - 1),
                                  reads=[("Wpg", wi), ("hfT", tt, 0), ("hfT", tt, 1)], writes=[("ps", ba)])
                        for kc in range(2):
                            T.add("pe", I_mm(banks[ba][:, 256:512], pT[:, kc, tt * 128:(tt + 1) * 128], Wpp[wi][:, kc, :],
                                             False, kc == 1),
                                  reads=[("Wpp", wi), ("pT", tt)], writes=[("ps", ba)])
                        i3 = c3[0] % 2
                        c3[0] += 1
                        T.add("act", I_act(sg3[i3], banks[ba][:, 0:256], AF.Sigmoid), reads=[("ps", ba)], writes=[("sg3", i3)])
                        T.add("dve", I_tt(tm3[i3], sg3[i3], banks[ba][:, 256:512], ALU.mult),
                              reads=[("sg3", i3), ("ps", ba)], writes=[("tm3", i3)])
                        xsl = X1[:, tt, fb * 256:(fb + 1) * 256]
                        T.add("dve", I_tt(xsl, xsl, tm3[i3], ALU.add), reads=[("tm3", i3), ("X1", tt)], writes=[("X1", tt)])
                        if fb == 7:
                            c3_final(tt)
                if th == 1:
                    T.barrier()

        T.barrier()
        T.finalize()

        csem = {e: es.enter_context(nc.semaphore("c_" + e)) for e in ["pe", "act", "dve", "pool"]}
        dsem = {}
        for i, k in enumerate(sorted(T.dma_cnt.keys(), key=str)):
            dsem[k] = es.enter_context(nc.semaphore("d%d" % i))
        with nc.Block() as block:
            @block.tensor
            def _(e):
                T.emit("pe", e, csem, dsem)

            @block.scalar
            def _(e):
                T.emit("act", e, csem, dsem)

            @block.vector
            def _(e):
                T.emit("dve", e, csem, dsem)

            @block.gpsimd
            def _(e):
                T.emit("pool", e, csem, dsem)

            @block.sync
            def _(e):
                T.emit("sp", e, csem, dsem)
    return nc, dbg


def _t5_bucket(rel):
    half = N_BUCKETS // 2
    max_exact = half // 2
    n = np.abs(rel)
    side = np.where(rel > 0, half, 0)
    nf = np.maximum(n, 1).astype(np.float32)
    large = max_exact + (np.log(nf / np.float32(max_exact)) / np.float32(math.log(MAX_DISTANCE / max_exact))
                         * np.float32(half - max_exact)).astype(np.int32)
    large = np.minimum(large, half - 1)
    return side + np.where(n < max_exact, n, large)


def _bias_index_tiles():
    k = np.arange(128)[:, None]
    q = np.arange(128)[None, :]
    a_idx = np.zeros((3, 128, 128), np.int64)
    for oi, o in enumerate((-1, 0, 1)):
        rel = o * 128 + k - q
        valid = np.abs(rel) <= 128
        a_idx[oi] = np.where(valid, _t5_bucket(rel), N_BUCKETS)
    b_idx = np.zeros((NB_TILES, 128, 128), np.int64)
    for gi in range(2):
        window, dil = B_PAT[gi]
        omin, omax = B_OFFS[gi]
        for o in range(omin, omax + 1):
            rel = o * 128 + k - q
            valid = (rel % dil == 0) & (np.abs(rel) <= (window // 2))
            b_idx[B_EBASE[gi] + o - omin] = np.where(valid, _t5_bucket(rel), N_BUCKETS)
    window, dil = B_PAT[2]
    rel_sub = k - q
    valid = np.abs(rel_sub) <= (window // (2 * dil))
    b_idx[B_EBASE[2]] = np.where(valid, _t5_bucket(rel_sub * dil), N_BUCKETS)
    return a_idx, b_idx


_NC_CACHE = {}


def _host_inputs(inputs):
    f = lambda a: np.ascontiguousarray(np.asarray(a, dtype=np.float32))
    table = f(inputs["rel_bias_table"])
    table_ext = np.concatenate([table, np.full((1, table.shape[1]), NEG, np.float32)], axis=0)
    a_idx, b_idx = _bias_index_tiles()
    biasA = np.zeros((2, 128, 12, 128), np.float32)
    for g in range(2):
        for h in range(4):
            for oi in range(3):
                biasA[g, :, oi * 4 + h, :] = table_ext[a_idx[oi], 4 * g + h]
    biasB = np.zeros((4, 128, NB_TILES, 128), np.float32)
    for j in range(4):
        for gi in range(3):
            if gi < 2:
                nt_ = B_OFFS[gi][1] - B_OFFS[gi][0] + 1
                for t in range(B_EBASE[gi], B_EBASE[gi] + nt_):
                    biasB[j, :, t, :] = table_ext[b_idx[t], 8 + gi * 4 + j]
            else:
                for t in range(B_EBASE[2], B_EBASE[2] + 4):
                    biasB[j, :, t, :] = table_ext[b_idx[B_EBASE[2]], 8 + gi * 4 + j]
    gains = np.stack([f(inputs["attn_norm"])[0], f(inputs["ffn_norm"])[0], f(inputs["ple_norm"])[0]], 0)
    gfm = np.ascontiguousarray(gains.reshape(3, 16, 128).transpose(2, 0, 1)).reshape(128, 48)
    cw = np.concatenate([f(inputs["conv_w"])[0], f(inputs["conv_b"])], 0)
    convp = np.ascontiguousarray(cw.reshape(4, NCC, 128).transpose(2, 0, 1)).reshape(128, 4 * NCC)
    sinkb = np.ascontiguousarray(np.broadcast_to(f(inputs["sink_a"])[0][None, :], (128, 8)))
    shared = {
        "w_in": f(inputs["w_in"])[0], "w_ba": f(inputs["w_branch_a"])[0], "w_bb": f(inputs["w_branch_b"])[0],
        "w_out": f(inputs["w_out"])[0], "w_g": f(inputs["w_ffn_gate"])[0], "w_u": f(inputs["w_ffn_up"])[0],
        "w_d": f(inputs["w_ffn_down"])[0], "w_pg": f(inputs["w_ple_gate"])[0], "w_pp": f(inputs["w_ple_proj"])[0],
        "gfm": gfm, "gfin": f(inputs["final_norm"]).reshape(1, D), "convp": convp, "sinkb": sinkb,
        "biasA": biasA.reshape(2, 128, 12 * 128), "biasB": biasB.reshape(4, 128, NB_TILES * 128),
        "ident": np.eye(128, dtype=np.float32).astype(ml_dtypes.bfloat16),
    }
    return shared


def kernel(**inputs):
    x = np.asarray(inputs["x"], dtype=np.float32)
    p = np.asarray(inputs["p"], dtype=np.float32)
    shared = _host_inputs(inputs)
    if "nc" not in _NC_CACHE:
        _NC_CACHE["nc"] = build_nc()[0]
    nc = _NC_CACHE["nc"]
    n = x.shape[0]
    in_maps = []
    for b in range(n):
        m = dict(shared)
        m["x"] = np.ascontiguousarray(x[b])
        m["p"] = np.ascontiguousarray(p[0, b])
        in_maps.append(m)
    res = run_bass_kernel_spmd(nc, in_maps, core_ids=list(range(n)))
    return np.stack([np.asarray(r["out"], dtype=np.float32) for r in res.results], 0)
```

```python
import math
from contextlib import ExitStack

import numpy as np
import ml_dtypes
import concourse.bass as bass
import concourse.mybir as mybir
from concourse.bass_utils import run_bass_kernel_spmd

F32 = mybir.dt.float32
BF16 = mybir.dt.bfloat16
AF = mybir.ActivationFunctionType
ALU = mybir.AluOpType

S = 2048
D = 2048
NT = 16
KC = 16
DFF = 5632
NCC = 44
HD = 128
SCALE = HD ** -0.5
EPS = 1e-6
NEG = -30000.0
N_BUCKETS = 32
MAX_DISTANCE = 1024

QA0, KA0, VA0 = 0, 1024, 1280
QB0, KB0, VB0 = 1536, 3072, 4608
GA0, GB0 = 6144, 8192

B_PAT = ((128, 1), (512, 4), (2048, 16))
B_OFFS = ((-1, 1), (-2, 2))
B_EBASE = (0, 3, 8)
NB_TILES = 12

ENGS = ["pe", "act", "dve", "pool", "sp"]


class Tracker:
    def __init__(self):
        self.ops = {e: [] for e in ENGS}
        self.res = {}
        self.dma_cnt = {}

    def add(self, eng, fn, reads=(), writes=(), dma_key=None):
        idx = len(self.ops[eng])
        ref = (eng, idx)
        deps = set()
        for r in reads:
            st = self.res.get(r)
            if st is not None and st[0] is not None:
                deps.add(st[0])
            if st is not None and isinstance(r, tuple) and r[0] == "ps":
                deps.update(v for k_, v in st[1].items() if k_ != eng)
        for w in writes:
            st = self.res.get(w)
            if st is not None:
                if st[0] is not None:
                    deps.add(st[0])
                deps.update(st[1].values())
                deps.update(st[2])
        isdma = dma_key is not None
        if eng == "pe":
            deps = {d for d in deps if not (d[0] == "pe")}
        deps.discard(ref)
        for r in reads:
            st = self.res.setdefault(r, [None, {}, []])
            if isdma:
                st[2].append(ref)
            else:
                st[1][eng] = ref
        for w in writes:
            self.res[w] = [ref, {}, []]
        op = {"fn": fn, "deps": deps, "key": dma_key, "inc": False}
        if isdma:
            c = self.dma_cnt.get(dma_key, 0) + 1
            self.dma_cnt[dma_key] = c
            op["dval"] = 16 * c
        self.ops[eng].append(op)
        return ref

    def barrier(self):
        last = {}
        for e in ENGS:
            for i in range(len(self.ops[e]) - 1, -1, -1):
                if self.ops[e][i]["fn"] is not None and self.ops[e][i]["key"] is None:
                    last[e] = (e, i)
                    break
        dmas = {}
        for e in ENGS:
            for i, op in enumerate(self.ops[e]):
                if op["key"] is not None:
                    dmas[op["key"]] = (e, i)
        for e in ENGS:
            deps = set(v for k, v in last.items() if k != e) | set(dmas.values())
            self.ops[e].append({"fn": None, "deps": deps, "key": None, "inc": False})

    def finalize(self):
        for e in ENGS:
            for op in self.ops[e]:
                for d in op["deps"]:
                    dop = self.ops[d[0]][d[1]]
                    if dop["key"] is None:
                        dop["inc"] = True
        for e in ENGS:
            c = 0
            for op in self.ops[e]:
                if op["key"] is None and op["inc"]:
                    c += 1
                op["val"] = c

    def emit(self, eng, engobj, csem, dsem):
        waited = {}
        n_wait = 0
        for op in self.ops[eng]:
            need = {}
            for d in op["deps"]:
                dop = self.ops[d[0]][d[1]]
                if dop["key"] is not None:
                    k = ("d", dop["key"])
                    v = dop["dval"]
                else:
                    k = ("c", d[0])
                    v = dop["val"]
                if need.get(k, 0) < v:
                    need[k] = v
            for k, v in need.items():
                if waited.get(k, 0) < v:
                    sem = dsem[k[1]] if k[0] == "d" else csem[k[1]]
                    engobj.wait_ge(sem, v)
                    waited[k] = v
                    n_wait += 1
            if op["fn"] is None:
                continue
            ins = op["fn"](engobj)
            if op["key"] is not None:
                ins.then_inc(dsem[op["key"]], 16)
            elif op["inc"]:
                ins.then_inc(csem[eng], 1)
        return n_wait


def I_act(out, in_, func, **kw):
    return lambda e: e.activation(out=out, in_=in_, func=func, **kw)


def I_mm(out, lhsT, rhs, start=True, stop=True):
    return lambda e: e.matmul(out, lhsT=lhsT, rhs=rhs, start=start, stop=stop, skip_group_check=True)


def I_tr(out, in_, ident):
    return lambda e: e.transpose(out=out, in_=in_, identity=ident)


def I_tt(out, in0, in1, op):
    return lambda e: e.tensor_tensor(out=out, in0=in0, in1=in1, op=op)


def I_ts(out, in0, s1, s2, op0, op1):
    return lambda e: e.tensor_scalar(out=out, in0=in0, scalar1=s1, scalar2=s2, op0=op0, op1=op1)


def I_stt(out, in0, scalar, in1, op0, op1):
    return lambda e: e.scalar_tensor_tensor(out=out, in0=in0, scalar=scalar, in1=in1, op0=op0, op1=op1)


def I_copy(out, in_):
    return lambda e: e.tensor_copy(out=out, in_=in_)


def I_acopy(out, in_):
    return lambda e: e.copy(out=out, in_=in_)


def I_amul(out, in_, mul):
    return lambda e: e.mul(out=out, in_=in_, mul=mul)


def I_recip(out, in_):
    return lambda e: e.reciprocal(out=out, in_=in_)


def I_dma(out, in_):
    return lambda e: e.dma_start(out=out, in_=in_)


def I_memset(ap, c):
    return lambda e: e.memset(ap, c)


class Region:
    def __init__(self, t, nbytes):
        self.t = t
        self.nbytes = nbytes
        self.off = 0

    def reset(self):
        self.off = 0

    def alloc(self, shape, dtype):
        n = 1
        for s_ in shape:
            n *= s_
        esz = 4 if dtype == F32 else 2
        nb = n * esz
        self.off = (self.off + 63) // 64 * 64
        assert self.off + nb <= self.nbytes, ("region overflow", self.off, nb, self.nbytes)
        ap = self.t[:, self.off // 2:(self.off + nb) // 2]
        self.off += nb
        if dtype == F32:
            ap = ap.bitcast(F32)
        if len(shape) == 2:
            ap = ap.rearrange("p (a b) -> p a b", a=shape[0])
        elif len(shape) == 3:
            ap = ap.rearrange("p (a b c) -> p a b c", a=shape[0], b=shape[1])
        return ap


def build_nc(stop_after=None):
    nc = bass.Bass("TRN2", target_bir_lowering=False)

    def din(name, shape, dt=F32):
        return nc.dram_tensor(name, list(shape), dt, kind="ExternalInput").ap()

    x_d = din("x", [S, D])
    p_d = din("p", [S, 256])
    w_in = din("w_in", [D, 10240])
    w_ba = din("w_ba", [1024, D])
    w_bb = din("w_bb", [512, D])
    w_out = din("w_out", [D, D])
    w_g = din("w_g", [D, DFF])
    w_u = din("w_u", [D, DFF])
    w_d = din("w_d", [DFF, D])
    w_pg = din("w_pg", [D, D])
    w_pp = din("w_pp", [256, D])
    gfm_d = din("gfm", [128, 3 * 16])
    gfin_d = din("gfin", [1, D])
    convp_d = din("convp", [128, 4 * NCC])
    sink_d = din("sinkb", [128, 8])
    biasA_d = din("biasA", [2, 128, 12 * 128])
    biasB_d = din("biasB", [4, 128, NB_TILES * 128])
    ident_d = din("ident", [128, 128], BF16)
    out_d = nc.dram_tensor("out", [S, D], F32, kind="ExternalOutput").ap()
    x1s = nc.dram_tensor("x1s", [S, D], F32, kind=("ExternalOutput" if stop_after == "B" else "Internal")).ap()
    dbg = {}

    T = Tracker()

    with ExitStack() as es:
        PB = 112 * 1024
        WB = 16 * 1024
        SB = 75 * 1024
        CB = 3 * 1024
        Pt = es.enter_context(nc.sbuf_tensor("Pt", [128, PB // 2], BF16))
        Wt = es.enter_context(nc.sbuf_tensor("Wt", [128, WB // 2], BF16))
        St = es.enter_context(nc.sbuf_tensor("St", [128, SB // 2], BF16))
        Ct = es.enter_context(nc.sbuf_tensor("Ct", [128, CB // 2], BF16))
        banks = [es.enter_context(nc.psum_tensor("ps%d" % i, [128, 512], F32)) for i in range(8)]
        RP = Region(Pt, PB)
        RS = Region(St, SB)
        RC = Region(Ct, CB)

        ident = RC.alloc((128,), BF16)
        ones = RC.alloc((128,), BF16)
        sel0 = RC.alloc((128,), BF16)
        epsb = RC.alloc((1,), F32)
        gfm = RC.alloc((3, 16), F32)
        convp = RC.alloc((4, NCC), F32)
        esink = RC.alloc((8,), F32)
        stat = RC.alloc((3, 24), F32)

        T.add("sp", I_dma(ident, ident_d[:, :]), writes=["ident"], dma_key="c0")
        T.add("sp", I_dma(gfm, gfm_d[:, :].rearrange("p (a b) -> p a b", a=3)), writes=["gfm"], dma_key="c1")
        T.add("sp", I_dma(convp, convp_d[:, :].rearrange("p (a b) -> p a b", a=4)), writes=["convp"], dma_key="c2")
        T.add("sp", I_dma(esink, sink_d[:, :]), writes=["esink"], dma_key="c3")
        T.add("dve", I_memset(ones, 1.0), writes=["ones"])
        T.add("dve", I_memset(sel0, 0.0), writes=["sel0"])
        T.add("dve", I_memset(sel0[0:1, :], 1.0), reads=["sel0"], writes=["sel0"])
        T.add("dve", I_memset(epsb, EPS), writes=["epsb"])
        T.add("act", I_act(esink, esink, AF.Exp), reads=["esink"], writes=["esink"])

        wslots = [Wt[:, i * 2048:(i + 1) * 2048] for i in range(4)]
        wctr = [0]

        def wslot():
            i = wctr[0] % 4
            wctr[0] += 1
            return i, wslots[i]

        preloaded = {}

        def prefetch_w(src2d, c0, ncols=128, nk=KC):
            preloaded[(id(src2d), c0)] = load_w_cols(src2d, c0, ncols, nk)

        def load_w_cols(src2d, c0, ncols=128, nk=KC):
            if (id(src2d), c0) in preloaded:
                return preloaded.pop((id(src2d), c0))
            i, sl = wslot()
            v = sl[:, 0:nk * ncols].rearrange("p (k c) -> p k c", k=nk)
            T.add("pool", I_dma(v, src2d[:, c0:c0 + ncols].rearrange("(k p) c -> p k c", p=128)),
                  writes=[("w", i)], dma_key=("w", i))
            return ("w", i), v

        bankctr = [0]

        def next_bank(lst):
            b = lst[bankctr[0] % len(lst)]
            bankctr[0] += 1
            return b

        evctr = [0]

        def evac_eng():
            evctr[0] += 1
            return "act" if evctr[0] % 2 else "dve"

        def evac_copy(eng, out, in_, reads, writes):
            if eng == "act":
                T.add("act", I_acopy(out, in_), reads=reads, writes=writes)
            else:
                T.add("dve", I_copy(out, in_), reads=reads, writes=writes)

        def rms_stats(xt, xres, slot, junk):
            jb = junk[slot % 2]
            T.add("act", I_act(jb, xt, AF.Square, accum_out=stat[:, 0, slot:slot + 1]), reads=[xres],
                  writes=[("ss", slot), ("junk", id(junk), slot % 2)])

        def rms_rstd(s0, s1):
            T.add("act", I_act(stat[:, 1, s0:s1], stat[:, 0, s0:s1], AF.Sqrt, scale=1.0 / D, bias=epsb[:, 0:1]),
                  reads=[("ss", k_) for k_ in range(s0, s1)] + ["epsb"], writes=[("sq", k_) for k_ in range(s0, s1)])
            T.add("dve", I_recip(stat[:, 2, s0:s1], stat[:, 1, s0:s1]),
                  reads=[("sq", k_) for k_ in range(s0, s1)], writes=[("rs", k_) for k_ in range(s0, s1)])

        def rms_apply(xt, xres, gi, xn, xnres, slot, dst_fn, dstres, eng="act", tbanks=(6, 7)):
            rs = stat[:, 2, slot:slot + 1]
            if eng == "act":
                T.add("act", I_amul(xn, xt, rs), reads=[xres, ("rs", slot)], writes=[xnres])
            else:
                T.add("pool", I_tt(xn, xt, rs.to_broadcast([128, D]), ALU.mult),
                      reads=[xres, ("rs", slot)], writes=[xnres])
            for half in range(2):
                b = tbanks[half]
                bv = banks[b][:].bitcast(BF16)
                for j in range(8):
                    kc = half * 8 + j
                    T.add("pe", I_tr(bv[:, j * 128:(j + 1) * 128], xn[:, kc * 128:(kc + 1) * 128], ident),
                          reads=[xnres, "ident"], writes=[("ps", b)])
                g_b = gfm[:, gi, half * 8:(half + 1) * 8].unsqueeze(2).to_broadcast([128, 8, 128])
                T.add("dve", I_tt(dst_fn(half), bv[:, 0:1024].rearrange("p (a b) -> p a b", a=8), g_b, ALU.mult),
                      reads=[("ps", b), "gfm"], writes=[dstres + (half,)])

        hT = RP.alloc((KC, S), BF16)
        yaT = RP.alloc((8, S), BF16)
        ybT = RP.alloc((4, S), BF16)

        RA = Region(Pt, PB)
        RA.off = 64 * 1024
        NXT = 4
        xts = [RA.alloc((D,), F32) for _ in range(NXT)]
        xns = [RA.alloc((D,), BF16) for _ in range(2)]
        junkA = [RA.alloc((D,), BF16) for _ in range(2)]
        assert RA.off <= 112 * 1024
        a2_units = [("A", 0), ("A", 1), ("B", 0), ("B", 1), ("B", 2), ("B", 3)]

        def unit_cols(kind, u):
            if kind == "A":
                return [KA0 + u * 128, VA0 + u * 128] + [QA0 + (4 * u + h) * 128 for h in range(4)]
            cols = []
            for gi in range(3):
                hc = (gi * 4 + u) * 128
                cols += [KB0 + hc, VB0 + hc, QB0 + hc]
            return cols

        if stop_after != "A1":
            for c0 in unit_cols(*a2_units[0])[:4]:
                prefetch_w(w_in, c0)
        def a1_apply(tt):
            bsel = tt % NXT
            rms_rstd(tt, tt + 1)
            rms_apply(xts[bsel], ("xt", bsel), 0, xns[tt % 2], ("xn", tt % 2), tt,
                      lambda half, tt=tt: hT[:, half * 8:(half + 1) * 8, tt * 128:(tt + 1) * 128], ("hT", tt), eng="act")

        for tt in range(NT):
            bsel = tt % NXT
            T.add("sp", I_dma(xts[bsel], x_d[tt * 128:(tt + 1) * 128, :]), writes=[("xt", bsel)], dma_key=("xt", bsel))
            rms_stats(xts[bsel], ("xt", bsel), tt, junkA)
            if tt >= 1:
                a1_apply(tt - 1)
        a1_apply(NT - 1)
        if stop_after == "A1":
            T.barrier()
            dbg["hT"] = nc.dram_tensor("dbg_hT", [128, KC * S], BF16, kind="ExternalOutput").ap()
            T.add("sp", I_dma(dbg["hT"][:, :].rearrange("p (a b) -> p a b", a=KC), hT), dma_key="dbg")

        if stop_after not in ("A1",):
            RS.reset()
            QTf = RS.alloc((4 * S,), BF16)
            QT = QTf.rearrange("p (h t) -> p h t", h=4)
            QTa = QTf.rearrange("p (q h c) -> p q h c", q=NT, h=4)
            U2sb = RS.alloc((S,), BF16)
            D2sb = RS.alloc((S,), BF16)
            KT = RS.alloc((3, S), BF16)
            VTs = [RS.alloc((S,), BF16)]
            Vb = RS.alloc((3, NT, 128), BF16)
            EB = RS.alloc((12, 128), F32)
            NRING = 4
            Es = [RS.alloc((512,), F32) for _ in range(NRING)]
            Ps = [RS.alloc((512,), BF16) for _ in range(NRING)]
            recs = [RS.alloc((128,), F32) for _ in range(2)]
            PROJ_BANKS = [0, 1, 2, 5]
            S_BANKS = [0, 1, 2, 7]
            UD_BANKS = [3, 4]
            vtc = [0]

            def project(c0, dst, dstres, dst_fn=None, deint=False):
                wres, wv = load_w_cols(w_in, c0)
                for tb in range(4):
                    b = next_bank(PROJ_BANKS)
                    for kc in range(KC):
                        T.add("pe", I_mm(banks[b][:, :], wv[:, kc, :], hT[:, kc, tb * 512:(tb + 1) * 512],
                                         start=(kc == 0), stop=(kc == KC - 1)),
                              reads=[wres] + [("hT", 4 * tb + q_, hf_) for q_ in range(4) for hf_ in range(2)],
                              writes=[("ps", b)])
                    if deint:
                        evac_copy(evac_eng(), dst.rearrange("p (r i) -> p r i", r=16)[:, :, 32 * tb:32 * tb + 32],
                                  banks[b][:, :].rearrange("p (i r) -> p r i", r=16), [("ps", b)], [dstres + (tb,)])
                    elif dst_fn is None:
                        evac_copy(evac_eng(), dst[:, tb * 512:(tb + 1) * 512], banks[b][:, :], [("ps", b)], [dstres + (tb,)])
                    else:
                        evac_copy(evac_eng(), dst_fn(tb), banks[b][:, :].rearrange("p (a b) -> p a b", a=4),
                                  [("ps", b)], [dstres + (tb,)])

            def project_v(c0, hv, deint=False):
                i = 0
                project(c0, VTs[i], ("VT", i), deint=deint)
                for half in range(2):
                    b = 6 + half
                    bv = banks[b][:].bitcast(BF16)
                    for j in range(8):
                        kb = half * 8 + j
                        T.add("pe", I_tr(bv[:, j * 128:(j + 1) * 128], VTs[i][:, kb * 128:(kb + 1) * 128], ident),
                              reads=([("VT", i, t_) for t_ in range(4)] if deint else [("VT", i, kb // 4)]) + ["ident"],
                              writes=[("ps", b)])
                    evac_copy(evac_eng(), Vb[:, hv, half * 8:(half + 1) * 8, :],
                              bv[:, 0:1024].rearrange("p (a b) -> p a b", a=8), [("ps", b)], [("V", hv)])

            udc = [0]
            stc = [0]
            mulc = [0]

            def mul_eng():
                mulc[0] += 1
                return "pool" if mulc[0] % 3 == 0 else "dve"

            def attention(groups, inject=False):
                steps = []
                for gidx, (sources, sink_col, dstT, dchunk) in enumerate(groups):
                    for qb in range(NT):
                        first = True
                        lst = []
                        for (qsel, ksel, vsel, ebase, omin, omax) in sources:
                            kbs = [kb for kb in range(qb + omin, qb + omax + 1) if 0 <= kb < NT]
                            for c in range(0, len(kbs), 4):
                                ch = kbs[c:c + 4]
                                lst.append([gidx, qb, qsel, ksel, vsel, ebase + (ch[0] - qb - omin), ch, False, False])
                        lst[0][7] = True
                        lst[-1][8] = True
                        steps.extend(lst)
                LAG = 3
                n = len(steps)
                info = {}
                for s_ in range(n + LAG):
                    if s_ < n:
                        gidx, qb, qsel, ksel, vsel, t0, ch, isfirst, islast = steps[s_]
                        k = stc[0] % NRING
                        stc[0] += 1
                        sb = S_BANKS[k]
                        w = len(ch) * 128
                        for i, kb in enumerate(ch):
                            T.add("pe", I_mm(banks[sb][:, i * 128:(i + 1) * 128], KT[:, ksel, kb * 128:(kb + 1) * 128],
                                             QT[:, qsel, qb * 128:(qb + 1) * 128]),
                                  reads=[("KT", ksel, kb // 4), ("QT", qsel, qb // 4)], writes=[("ps", sb)])
                        T.add("act", I_act(Es[k][:, 0:w], banks[sb][:, 0:w], AF.Exp, scale=SCALE),
                              reads=[("ps", sb)], writes=[("E", k)])
                        T.add(mul_eng(), I_tt(Ps[k][:, 0:w], Es[k][:, 0:w],
                                           EB[:, t0:t0 + len(ch), :].rearrange("p a b -> p (a b)"), ALU.mult),
                              reads=[("E", k), "EB"], writes=[("P", k)])
                        info[s_] = k
                    s2 = s_ - LAG
                    if s2 >= 0:
                        gidx, qb, qsel, ksel, vsel, t0, ch, isfirst, islast = steps[s2]
                        k = info[s2]
                        if isfirst:
                            udc[0] += 1
                        ub = UD_BANKS[udc[0] % 2]
                        if isfirst and inject:
                            T.add("pe", I_mm(banks[ub][:, 0:128], ident, U2sb[:, qb * 128:(qb + 1) * 128], True, False),
                                  reads=["ident", "U2sb"], writes=[("ps", ub)])
                            T.add("pe", I_mm(banks[ub][:, 128:256], sel0, D2sb[:, qb * 128:(qb + 1) * 128], False, False),
                                  reads=["sel0", "D2sb"], writes=[("ps", ub)])
                        for i, kb in enumerate(ch):
                            T.add("pe", I_mm(banks[ub][:, 0:128], Vb[:, vsel, kb, :], Ps[k][:, i * 128:(i + 1) * 128],
                                             start=(isfirst and i == 0 and not inject), stop=False),
                                  reads=[("V", vsel), ("P", k)], writes=[("ps", ub)])
                            T.add("pe", I_mm(banks[ub][:, 128:256], ones, Ps[k][:, i * 128:(i + 1) * 128],
                                             start=False, stop=(islast and i == len(ch) - 1)),
                                  reads=["ones", ("P", k)], writes=[("ps", ub)])
                        if islast:
                            sources, sink_col, dstT, dchunk = groups[gidx]
                            r = udc[0] % 2
                            if sink_col is not None:
                                T.add("dve", (lambda e, o=recs[r], i_=banks[ub][:, 128:256], s1=esink[:, sink_col:sink_col + 1]:
                                              e.tensor_scalar_add(out=o, in0=i_, scalar1=s1)),
                                      reads=[("ps", ub), "esink"], writes=[("rec", r)])
                                T.add("dve", I_recip(recs[r], recs[r]), reads=[("rec", r)], writes=[("rec", r)])
                            else:
                                T.add("act", I_act(recs[r], banks[ub][:, 128:256], AF.Ln), reads=[("ps", ub)], writes=[("rec", r)])
                                T.add("act", I_act(recs[r], recs[r], AF.Exp, scale=-1.0), reads=[("rec", r)], writes=[("rec", r)])
                            T.add("dve", I_tt(dstT[:, dchunk, qb * 128:(qb + 1) * 128], banks[ub][:, 0:128], recs[r], ALU.mult),
                                  reads=[("ps", ub), ("rec", r)], writes=[("yT", id(dstT), dchunk, qb)])

            def attention_g2():
                KTd = KT[:, 2, :].rearrange("p (r i) -> p r i", r=16)
                QTd = QT[:, 2, :].rearrange("p (r i) -> p r i", r=16)
                U2v = U2sb.rearrange("p (i r) -> p r i", r=16)
                D2v = D2sb.rearrange("p (i r) -> p r i", r=16)
                kq_reads = [("KT", 2, t_) for t_ in range(4)] + [("QT", 2, t_) for t_ in range(4)]
                LAG = 3
                info = {}
                for s_ in range(4 + LAG):
                    if s_ < 4:
                        k = stc[0] % NRING
                        stc[0] += 1
                        sb = S_BANKS[k]
                        for c in range(4):
                            r = s_ * 4 + c
                            T.add("pe", I_mm(banks[sb][:, c * 128:(c + 1) * 128], KTd[:, r, :], QTd[:, r, :]),
                                  reads=kq_reads, writes=[("ps", sb)])
                        T.add("act", I_act(Es[k][:, :], banks[sb][:, :], AF.Exp, scale=SCALE),
                              reads=[("ps", sb)], writes=[("E", k)])
                        T.add(mul_eng(), I_tt(Ps[k][:, :].rearrange("p (a b) -> p a b", a=4),
                                           Es[k][:, :].rearrange("p (a b) -> p a b", a=4),
                                           EB[:, B_EBASE[2]:B_EBASE[2] + 4, :], ALU.mult),
                              reads=[("E", k), "EB"], writes=[("P", k)])
                        info[s_] = k
                    s2 = s_ - LAG
                    if s2 >= 0:
                        k = info[s2]
                        for half in range(2):
                            udc[0] += 1
                            ub = UD_BANKS[udc[0] % 2]
                            for c2 in range(2):
                                c = half * 2 + c2
                                r = s2 * 4 + c
                                T.add("pe", I_mm(banks[ub][:, c2 * 256:c2 * 256 + 128], Vb[:, 2, r, :],
                                                 Ps[k][:, c * 128:(c + 1) * 128], c2 == 0, False),
                                      reads=[("V", 2), ("P", k)], writes=[("ps", ub)])
                                T.add("pe", I_mm(banks[ub][:, c2 * 256 + 128:c2 * 256 + 256], ones,
                                                 Ps[k][:, c * 128:(c + 1) * 128], False, c2 == 1),
                                      reads=["ones", ("P", k)], writes=[("ps", ub)])
                            r0 = s2 * 4 + half * 2
                            bview = banks[ub][:, :].rearrange("p (a b) -> p a b", a=2)
                            ee = "act" if half == 0 else "dve"
                            evac_copy(ee, U2v[:, r0:r0 + 2, :], bview[:, :, 0:128], [("ps", ub)], [("U2sb", r0)])
                            evac_copy(ee, D2v[:, r0:r0 + 2, :], bview[:, :, 128:256], [("ps", ub)], [("D2sb", r0)])
                T.add("dve", I_memset(recs[0][:, 0:1], 0.0),
                      reads=[("U2sb", r0_) for r0_ in range(0, 16, 2)] + [("D2sb", r0_) for r0_ in range(0, 16, 2)],
                      writes=["U2sb", "D2sb"])

            def attention_gqa(u):
                steps = []
                for qb in range(NT):
                    kbs = [kb for kb in (qb - 1, qb, qb + 1) if 0 <= kb < NT]
                    for kb in kbs:
                        steps.append((qb, kb, kb == kbs[0], kb == kbs[-1]))
                LAG = 3
                n = len(steps)
                info = {}
                for s_ in range(n + LAG):
                    if s_ < n:
                        qb, kb, isfirst, islast = steps[s_]
                        k = stc[0] % NRING
                        stc[0] += 1
                        sb = S_BANKS[k]
                        oi = kb - qb + 1
                        T.add("pe", I_mm(banks[sb][:, :], KT[:, 0, kb * 128:(kb + 1) * 128],
                                         QTa[:, qb, :, :].rearrange("p h c -> p (h c)")),
                              reads=[("KT", 0, kb // 4)] + [("QT", h, qb // 4) for h in range(4)], writes=[("ps", sb)])
                        T.add("act", I_act(Es[k][:, :], banks[sb][:, :], AF.Exp, scale=SCALE),
                              reads=[("ps", sb)], writes=[("E", k)])
                        T.add(mul_eng(), I_tt(Ps[k][:, :], Es[k][:, :],
                                           EB[:, oi * 4:(oi + 1) * 4, :].rearrange("p a b -> p (a b)"), ALU.mult),
                              reads=[("E", k), "EB"], writes=[("P", k)])
                        info[s_] = k
                    s2 = s_ - LAG
                    if s2 >= 0:
                        qb, kb, isfirst, islast = steps[s2]
                        k = info[s2]
                        if isfirst:
                            udc[0] += 1
                        ub = (3, 4)[udc[0] % 2]
                        db = (5, 6)[udc[0] % 2]
                        T.add("pe", I_mm(banks[ub][:, :], Vb[:, 0, kb, :], Ps[k][:, :], isfirst, islast),
                              reads=[("V", 0), ("P", k)], writes=[("ps", ub)])
                        T.add("pe", I_mm(banks[db][:, :], ones, Ps[k][:, :], isfirst, islast),
                              reads=["ones", ("P", k)], writes=[("ps", db)])
                        if islast:
                            r = udc[0] % 2
                            import os
                            if os.environ.get("K_OLDFIN"):
                                for h in range(4):
                                    T.add("dve", (lambda e, o=rec4[r][:, h * 128:(h + 1) * 128], i_=banks[db][:, h * 128:(h + 1) * 128],
                                                  s1=esink[:, 4 * u + h:4 * u + h + 1]: e.tensor_scalar_add(out=o, in0=i_, scalar1=s1)),
                                          reads=[("ps", db), "esink"], writes=[("rec4", r, h)])
                                T.add("dve", I_recip(rec4[r], rec4[r]),
                                      reads=[("rec4", r, h) for h in range(4)], writes=[("rec4", r, h) for h in range(4)])
                            else:
                              for h in range(4):
                                T.add("act", I_act(rec4[r][:, h * 128:(h + 1) * 128], banks[db][:, h * 128:(h + 1) * 128],
                                                   AF.Ln, bias=esink[:, 4 * u + h:4 * u + h + 1]),
                                      reads=[("ps", db), "esink"], writes=[("rec4", r, h)])
                              T.add("act", I_act(rec4[r], rec4[r], AF.Exp, scale=-1.0),
                                  reads=[("rec4", r, h) for h in range(4)], writes=[("rec4", r, h) for h in range(4)])
                            T.add("dve", I_tt(yaT[:, 4 * u:4 * u + 4, qb * 128:(qb + 1) * 128],
                                              banks[ub][:, :].rearrange("p (a b) -> p a b", a=4),
                                              rec4[r].rearrange("p (a b) -> p a b", a=4), ALU.mult),
                                  reads=[("ps", ub)] + [("rec4", r, h) for h in range(4)], writes=[("yaT", u, qb)])

            rec4 = [RS.alloc((512,), F32) for _ in range(2)]
            for ui, (kind, u) in enumerate(a2_units):
                if kind == "A":
                    ntile = 12
                    T.add("sp", I_dma(EB[:, 0:ntile, :], biasA_d[u].rearrange("p (a b) -> p a b", a=ntile)),
                          writes=["EB"], dma_key="EB")
                else:
                    ntile = NB_TILES
                    T.add("sp", I_dma(EB[:, 0:ntile, :], biasB_d[u].rearrange("p (a b) -> p a b", a=ntile)),
                          writes=["EB"], dma_key="EB")
                T.add("act", I_act(EB[:, 0:ntile, :], EB[:, 0:ntile, :], AF.Exp), reads=["EB"], writes=["EB"])
                if kind == "A":
                    project(KA0 + u * 128, KT[:, 0, :], ("KT", 0))
                    project_v(VA0 + u * 128, 0)
                    for h in range(4):
                        project(QA0 + (4 * u + h) * 128, None, ("QT", h),
                                dst_fn=lambda tb, h=h: QTa[:, tb * 4:(tb + 1) * 4, h, :])
                else:
                    for gi in range(3):
                        hc = (gi * 4 + u) * 128
                        project(KB0 + hc, KT[:, gi, :], ("KT", gi), deint=(gi == 2))
                        project_v(VB0 + hc, gi, deint=(gi == 2))
                        project(QB0 + hc, QT[:, gi, :], ("QT", gi), deint=(gi == 2))
                if ui + 1 < len(a2_units):
                    for c0 in unit_cols(*a2_units[ui + 1])[:4]:
                        prefetch_w(w_in, c0)
                if kind == "A":
                    attention_gqa(u)
                else:
                    attention_g2()
                    srcs = [(gi, gi, gi, B_EBASE[gi], B_OFFS[gi][0], B_OFFS[gi][1]) for gi in range(2)]
                    attention([(srcs, None, ybT, u)], inject=True)
            T.barrier()
            if stop_after == "A2":
                dbg["yaT"] = nc.dram_tensor("dbg_yaT", [128, 8 * S], BF16, kind="ExternalOutput").ap()
                dbg["ybT"] = nc.dram_tensor("dbg_ybT", [128, 4 * S], BF16, kind="ExternalOutput").ap()
                T.add("sp", I_dma(dbg["yaT"][:, :].rearrange("p (a b) -> p a b", a=8), yaT), dma_key="dbg")
                T.add("sp", I_dma(dbg["ybT"][:, :].rearrange("p (a b) -> p a b", a=4), ybT), dma_key="dbg2")

        if stop_after not in ("A1", "A2"):
            RS.reset()
            mergedT = RS.alloc((KC, 1024), BF16)
            WoA = RS.alloc((KC, 512), BF16)
            WoB = Wt[:, 0:KC * 512].rearrange("p (k c) -> p k c", k=KC)
            WOS = [(WoA, [("WoA",)]), (WoB, [("w", i) for i in range(4)])]
            sg = [RS.alloc((512,), F32) for _ in range(4)]
            xs = [RS.alloc((512,), F32) for _ in range(3)]
            os_ = [RS.alloc((512,), F32) for _ in range(3)]
            MB = [0, 1, 2, 3, 4, 5, 6, 7]
            cB = [0]
            for th in range(2):
                for fc in range(KC):
                    rga, wga = load_w_cols(w_in, GA0 + fc * 128)
                    rgb, wgb = load_w_cols(w_in, GB0 + fc * 128)
                    rba, wba = load_w_cols(w_ba, fc * 128, nk=8)
                    rbb, wbb = load_w_cols(w_bb, fc * 128, nk=4)
                    bk = [[next_bank(MB) for _ in range(2)] for _ in range(4)]
                    for wi_, (wres_, wv_, src_, nk_) in enumerate(((rga, wga, hT, KC), (rgb, wgb, hT, KC),
                                                                   (rba, wba, yaT, 8), (rbb, wbb, ybT, 4))):
                        for tb2 in range(2):
                            t0 = th * 1024 + tb2 * 512
                            for kc in range(nk_):
                                T.add("pe", I_mm(banks[bk[wi_][tb2]][:, :], wv_[:, kc, :], src_[:, kc, t0:t0 + 512],
                                                 kc == 0, kc == nk_ - 1),
                                      reads=[wres_], writes=[("ps", bk[wi_][tb2])])
                    for tb2 in range(2):
                        bga, bgb, bza, bzb = (bk[w_][tb2] for w_ in range(4))
                        i0 = (cB[0] % 2) * 2
                        cB[0] += 1
                        T.add("act", I_act(sg[i0], banks[bga][:, :], AF.Sigmoid), reads=[("ps", bga)], writes=[("sg", i0)])
                        T.add("act", I_act(sg[i0 + 1], banks[bgb][:, :], AF.Sigmoid), reads=[("ps", bgb)], writes=[("sg", i0 + 1)])
                        T.add("dve", I_tt(sg[i0], sg[i0], banks[bza][:, :], ALU.mult),
                              reads=[("sg", i0), ("ps", bza)], writes=[("sg", i0)])
                        T.add("dve", I_tt(sg[i0 + 1], sg[i0 + 1], banks[bzb][:, :], ALU.mult),
                              reads=[("sg", i0 + 1), ("ps", bzb)], writes=[("sg", i0 + 1)])
                        T.add("dve", I_tt(mergedT[:, fc, tb2 * 512:(tb2 + 1) * 512], sg[i0], sg[i0 + 1], ALU.add),
                              reads=[("sg", i0), ("sg", i0 + 1)], writes=[("mg", fc, tb2)])
                its = [(fb, tt) for fb in range(4) for tt in range(8)]

                def xload(it):
                    fb, tt = its[it]
                    trow = th * 1024 + tt * 128
                    xi = it % 3
                    T.add("sp", I_dma(xs[xi], x_d[trow:trow + 128, fb * 512:(fb + 1) * 512]),
                          writes=[("xs", xi)], dma_key=("xs", xi))

                xload(0)
                xload(1)
                for it, (fb, tt) in enumerate(its):
                    Wo_, wres_ = WOS[fb % 2]
                    if tt == 0:
                        T.add("pool", I_dma(Wo_, w_out[:, fb * 512:(fb + 1) * 512].rearrange("(k p) c -> p k c", p=128)),
                              writes=wres_, dma_key=("Wo", fb % 2))
                    trow = th * 1024 + tt * 128
                    b = next_bank(MB)
                    xi = it % 3
                    if it + 2 < len(its):
                        xload(it + 2)
                    for kc in range(KC):
                        T.add("pe", I_mm(banks[b][:, :], mergedT[:, kc, tt * 128:(tt + 1) * 128], Wo_[:, kc, :],
                                         kc == 0, kc == KC - 1),
                              reads=wres_ + [("mg", kc, tt // 4)], writes=[("ps", b)])
                    T.add("dve", I_tt(os_[xi], banks[b][:, :], xs[xi], ALU.add),
                          reads=[("ps", b), ("xs", xi)], writes=[("os", xi)])
                    T.add("sp", I_dma(x1s[trow:trow + 128, fb * 512:(fb + 1) * 512], os_[xi]),
                          reads=[("os", xi)], writes=[("x1s", trow, fb)], dma_key=("os", xi))
            T.barrier()

        if stop_after not in ("A1", "A2", "B"):
            RP.reset()
            X1 = RP.alloc((8, D), F32)
            hfT = RP.alloc((KC, 1032), BF16)
            pT = RP.alloc((2, 1024), BF16)
            pf = [RP.alloc((256,), F32) for _ in range(2)]
            pb = [RP.alloc((256,), BF16) for _ in range(2)]

            def p_load(th, tt):
                trow = th * 1024 + tt * 128
                pi = tt % 2
                T.add("sp", I_dma(pf[pi], p_d[trow:trow + 128, :]), writes=[("pf", pi)], dma_key=("pf", pi))
                T.add("act", I_acopy(pb[pi], pf[pi]), reads=[("pf", pi)], writes=[("pb", pi)])

            def p_transpose(th, tt):
                if True:
                    pi = tt % 2
                    b = 5
                    bv = banks[b][:].bitcast(BF16)
                    for j in range(2):
                        T.add("pe", I_tr(bv[:, j * 128:(j + 1) * 128], pb[pi][:, j * 128:(j + 1) * 128], ident),
                              reads=[("pb", pi), "ident"], writes=[("ps", b)])
                    T.add("act", I_acopy(pT[:, :, tt * 128:(tt + 1) * 128], bv[:, 0:256].rearrange("p (a b) -> p a b", a=2)),
                          reads=[("ps", b)], writes=[("pT", tt)])
            xnC = [RP.alloc((D,), BF16) for _ in range(2)]
            RS.reset()
            junkC = [RS.alloc((D,), BF16) for _ in range(2)]
            xh = RS.alloc((D,), F32)
            halT = RS.alloc((KC, 128), BF16)
            Wpg = [RS.alloc((KC, 256), BF16) for _ in range(2)]
            Wpp = [RS.alloc((2, 256), BF16) for _ in range(2)]
            sg3 = [RS.alloc((256,), F32) for _ in range(2)]
            tm3 = [RS.alloc((256,), F32) for _ in range(2)]
            gB = RS.alloc((D,), F32)
            ost = [RS.alloc((D,), F32) for _ in range(2)]
            xn1s = xnC
            xn3s = xnC
            junk1 = junkC
            junk3 = junkC

            def c1_load_halo(th_):
                hrow_ = 1024 if th_ == 0 else 896
                T.add("sp", I_dma(xh, x1s[hrow_:hrow_ + 128, :]), writes=["xh"], dma_key="xh")

            def c1_load_tile(th_, tt):
                trow_ = th_ * 1024 + tt * 128
                T.add("sp", I_dma(X1[:, tt, :], x1s[trow_:trow_ + 128, :]), writes=[("X1", tt)], dma_key=("X1", tt))

            for th in range(2):
                hcol = 0 if th == 0 else 127
                if th == 0:
                    c1_load_halo(0)
                    for tt in range(8):
                        c1_load_tile(0, tt)
                rms_stats(xh, "xh", 0, junk1)
                for tt in range(4):
                    rms_stats(X1[:, tt, :], ("X1", tt), tt + 1, junk1)
                rms_rstd(0, 5)
                rms_apply(xh, "xh", 1, xn1s[0], ("xnC", 0), 0,
                          lambda half: halT[:, half * 8:(half + 1) * 8, :], ("halT",), eng="act")
                T.add("dve", I_copy(hfT[:, :, 1024:1025], halT[:, :, hcol:hcol + 1]), reads=[("halT", 0), ("halT", 1)],
                      writes=[("hfT", -1)])
                for tt in range(4, 8):
                    rms_stats(X1[:, tt, :], ("X1", tt), tt + 1, junk1)
                rms_rstd(5, 9)
                for tt in range(8):
                    rms_apply(X1[:, tt, :], ("X1", tt), 1, xn1s[(tt + 1) % 2], ("xnC", (tt + 1) % 2), tt + 1,
                              lambda half, tt=tt: hfT[:, half * 8:(half + 1) * 8, tt * 128:(tt + 1) * 128],
                              ("hfT", tt), eng="act")
                T.barrier()
                RS.reset()
                aT = RS.alloc((11, 1024), BF16)
                Wd = [RS.alloc((11, 512), BF16) for _ in range(2)]
                G = [RS.alloc((1026,), F32) for _ in range(2)]
                ACC = [RS.alloc((1024,), F32) for _ in range(2)]
                GEL = [RS.alloc((1024,), F32) for _ in range(2)]
                FB = [0, 1, 2, 3, 4, 5, 6]
                cC = [0]
                for qd in range(4):
                    for cl in range(11):
                        c = qd * 11 + cl
                        rg_, wg_ = load_w_cols(w_g, c * 128)
                        ru_, wu_ = load_w_cols(w_u, c * 128)
                        gi = cC[0] % 2
                        cC[0] += 1
                        bg = [next_bank(FB), next_bank(FB)]
                        for tb2 in range(2):
                            for kc in range(KC):
                                T.add("pe", I_mm(banks[bg[tb2]][:, :], wg_[:, kc, :],
                                                 hfT[:, kc, tb2 * 512:(tb2 + 1) * 512], kc == 0, kc == KC - 1),
                                      reads=[rg_], writes=[("ps", bg[tb2])])
                        for kc in range(KC):
                            T.add("pe", I_mm(banks[7][:, 0:1], wg_[:, kc, :], hfT[:, kc, 1024:1025], kc == 0, kc == KC - 1),
                                  reads=[rg_], writes=[("ps", 7)])
                        bu = [next_bank(FB), next_bank(FB)]
                        for tb2 in range(2):
                            for kc in range(KC):
                                T.add("pe", I_mm(banks[bu[tb2]][:, :], wu_[:, kc, :],
                                                 hfT[:, kc, tb2 * 512:(tb2 + 1) * 512], kc == 0, kc == KC - 1),
                                      reads=[ru_], writes=[("ps", bu[tb2])])
                        Gt = G[gi]
                        T.add("act", I_acopy(Gt[:, 1:513], banks[bg[0]][:, :]), reads=[("ps", bg[0])], writes=[("G", gi, 0)])
                        T.add("act", I_acopy(Gt[:, 513:1025], banks[bg[1]][:, :]), reads=[("ps", bg[1])], writes=[("G", gi, 1)])
                        if th == 0:
                            T.add("act", I_acopy(Gt[:, 1025:1026], banks[7][:, 0:1]), reads=[("ps", 7)], writes=[("G", gi, 2)])
                            T.add("dve", I_memset(Gt[:, 0:1], 0.0), writes=[("G", gi, 3)])
                        else:
                            T.add("act", I_acopy(Gt[:, 0:1], banks[7][:, 0:1]), reads=[("ps", 7)], writes=[("G", gi, 2)])
                            T.add("dve", I_memset(Gt[:, 1025:1026], 0.0), writes=[("G", gi, 3)])
                        gres = [("G", gi, k_) for k_ in range(4)]
                        A_ = ACC[gi]
                        T.add("dve", I_ts(A_, Gt[:, 1:1025], convp[:, 1, c:c + 1], convp[:, 3, c:c + 1], ALU.mult, ALU.add),
                              reads=gres + ["convp"], writes=[("ACC", gi)])
                        T.add("dve", I_stt(A_, Gt[:, 0:1024], convp[:, 0, c:c + 1], A_, ALU.mult, ALU.add),
                              reads=gres + [("ACC", gi)], writes=[("ACC", gi)])
                        T.add("dve", I_stt(A_, Gt[:, 2:1026], convp[:, 2, c:c + 1], A_, ALU.mult, ALU.add),
                              reads=gres + [("ACC", gi)], writes=[("ACC", gi)])
                        T.add("act", I_act(GEL[gi], A_, AF.Gelu_apprx_tanh), reads=[("ACC", gi)], writes=[("GEL", gi)])
                        for tb2 in range(2):
                            T.add("dve", I_tt(aT[:, cl, tb2 * 512:(tb2 + 1) * 512], GEL[gi][:, tb2 * 512:(tb2 + 1) * 512],
                                              banks[bu[tb2]][:, :], ALU.mult),
                                  reads=[("GEL", gi), ("ps", bu[tb2])], writes=[("aT", cl, tb2)])
                        if c < 8:
                            p_load(th, c)
                        if 1 <= c < 9:
                            p_transpose(th, c - 1)
                    for fb in range(4):
                        wi = cC[0] % 2
                        cC[0] += 1
                        T.add("pool", I_dma(Wd[wi], w_d[qd * 11 * 128:(qd + 1) * 11 * 128, fb * 512:(fb + 1) * 512]
                                            .rearrange("(c p) f -> p c f", p=128)),
                              writes=[("Wd", wi)], dma_key=("Wd", wi))
                        for t2 in range(2):
                            bd = [next_bank(FB) for _ in range(4)]
                            for cl in range(11):
                                for t4 in range(4):
                                    tt = t2 * 4 + t4
                                    T.add("pe", I_mm(banks[bd[t4]][:, :], aT[:, cl, tt * 128:(tt + 1) * 128], Wd[wi][:, cl, :],
                                                     cl == 0, cl == 10),
                                          reads=[("Wd", wi), ("aT", cl, tt // 4)], writes=[("ps", bd[t4])])
                            for t4 in range(4):
                                tt = t2 * 4 + t4
                                xsl = X1[:, tt, fb * 512:(fb + 1) * 512]
                                T.add("dve", I_tt(xsl, banks[bd[t4]][:, :], xsl, ALU.add),
                                      reads=[("ps", bd[t4]), ("X1", tt)], writes=[("X1", tt)])
                T.barrier()
                T.add("sp", I_dma(gB, gfin_d.partition_broadcast(128)), writes=["gB"], dma_key="gB")
                for tt in range(8):
                    rms_stats(X1[:, tt, :], ("X1", tt), tt, junk3)
                rms_rstd(0, 8)
                for tt in range(8):
                    rms_apply(X1[:, tt, :], ("X1", tt), 2, xn3s[tt % 2], ("xnC", tt % 2), tt,
                              lambda half, tt=tt: hfT[:, half * 8:(half + 1) * 8, tt * 128:(tt + 1) * 128],
                              ("hfT", tt), eng="act")
                PB_ = [0, 1, 2, 3, 4]
                c3 = [0]

                def c3_final(tt):
                    rms_stats(X1[:, tt, :], ("X1", tt), 16 + tt, junk3)
                    if tt % 4 != 3:
                        return
                    rms_rstd(16 + tt - 3, 16 + tt + 1)
                    if th == 0 and tt == 3:
                        c1_load_halo(1)
                    for t_ in range(tt - 3, tt + 1):
                        trow = th * 1024 + t_ * 128
                        oi = t_ % 2
                        T.add("dve", I_stt(ost[oi], X1[:, t_, :], stat[:, 2, 16 + t_:17 + t_], gB, ALU.mult, ALU.mult),
                              reads=[("X1", t_), ("rs", 16 + t_), "gB"], writes=[("ost", oi)])
                        T.add("sp", I_dma(out_d[trow:trow + 128, :], ost[oi]), reads=[("ost", oi)], writes=[("out", trow)],
                              dma_key=("ost", oi))
                        if th == 0:
                            c1_load_tile(1, t_)

                def c3_wload(fb):
                    wi = fb % 2
                    T.add("pool", I_dma(Wpg[wi], w_pg[:, fb * 256:(fb + 1) * 256].rearrange("(k p) c -> p k c", p=128)),
                          writes=[("Wpg", wi)], dma_key=("Wpg", wi))
                    T.add("pool", I_dma(Wpp[wi], w_pp[:, fb * 256:(fb + 1) * 256].rearrange("(k p) c -> p k c", p=128)),
                          writes=[("Wpp", wi)], dma_key=("Wpp", wi))

                order = [(fb, tt) for fb in range(6) for tt in range(8)]
                order += [(fb, tt) for fb in (6, 7) for tt in range(4)] + [(fb, tt) for fb in (6, 7) for tt in range(4, 8)]
                loaded = set()
                for (fb, tt) in order:
                    wi = fb % 2
                    if fb not in loaded:
                        c3_wload(fb)
                        loaded.add(fb)
                        if fb == 6:
                            c3_wload(7)
                            loaded.add(7)
                    if True:
                        ba = next_bank(PB_)
                        for kc in range(KC):
                            T.add("pe", I_mm(banks[ba][:, 0:256], hfT[:, kc, tt * 128:(tt + 1) * 128], Wpg[wi][:, kc, :],
                                             kc == 0, kc == KC - 1),
                                  reads=[("Wpg", wi), ("hfT", tt, 0), ("hfT", tt, 1)], writes=[("ps", ba)])
                        for kc in range(2):
                            T.add("pe", I_mm(banks[ba][:, 256:512], pT[:, kc, tt * 128:(tt + 1) * 128], Wpp[wi][:, kc, :],
                                             False, kc == 1),
                                  reads=[("Wpp", wi), ("pT", tt)], writes=[("ps", ba)])
                        i3 = c3[0] % 2
                        c3[0] += 1
                        T.add("act", I_act(sg3[i3], banks[ba][:, 0:256], AF.Sigmoid), reads=[("ps", ba)], writes=[("sg3", i3)])
                        T.add("dve", I_tt(tm3[i3], sg3[i3], banks[ba][:, 256:512], ALU.mult),
                              reads=[("sg3", i3), ("ps", ba)], writes=[("tm3", i3)])
                        xsl = X1[:, tt, fb * 256:(fb + 1) * 256]
                        T.add("dve", I_tt(xsl, xsl, tm3[i3], ALU.add), reads=[("tm3", i3), ("X1", tt)], writes=[("X1", tt)])
                        if fb == 7:
                            c3_final(tt)
                if th == 1:
                    T.barrier()

        T.barrier()
        T.finalize()

        csem = {e: es.enter_context(nc.semaphore("c_" + e)) for e in ["pe", "act", "dve", "pool"]}
        dsem = {}
        for i, k in enumerate(sorted(T.dma_cnt.keys(), key=str)):
            dsem[k] = es.enter_context(nc.semaphore("d%d" % i))
        with nc.Block() as block:
            @block.tensor
            def _(e):
                T.emit("pe", e, csem, dsem)

            @block.scalar
            def _(e):
                T.emit("act", e, csem, dsem)

            @block.vector
            def _(e):
                T.emit("dve", e, csem, dsem)

            @block.gpsimd
            def _(e):
                T.emit("pool", e, csem, dsem)

            @block.sync
            def _(e):
                T.emit("sp", e, csem, dsem)
    return nc, dbg


def _t5_bucket(rel):
    half = N_BUCKETS // 2
    max_exact = half // 2
    n = np.abs(rel)
    side = np.where(rel > 0, half, 0)
    nf = np.maximum(n, 1).astype(np.float32)
    large = max_exact + (np.log(nf / np.float32(max_exact)) / np.float32(math.log(MAX_DISTANCE / max_exact))
                         * np.float32(half - max_exact)).astype(np.int32)
    large = np.minimum(large, half - 1)
    return side + np.where(n < max_exact, n, large)


def _bias_index_tiles():
    k = np.arange(128)[:, None]
    q = np.arange(128)[None, :]
    a_idx = np.zeros((3, 128, 128), np.int64)
    for oi, o in enumerate((-1, 0, 1)):
        rel = o * 128 + k - q
        valid = np.abs(rel) <= 128
        a_idx[oi] = np.where(valid, _t5_bucket(rel), N_BUCKETS)
    b_idx = np.zeros((NB_TILES, 128, 128), np.int64)
    for gi in range(2):
        window, dil = B_PAT[gi]
        omin, omax = B_OFFS[gi]
        for o in range(omin, omax + 1):
            rel = o * 128 + k - q
            valid = (rel % dil == 0) & (np.abs(rel) <= (window // 2))
            b_idx[B_EBASE[gi] + o - omin] = np.where(valid, _t5_bucket(rel), N_BUCKETS)
    window, dil = B_PAT[2]
    rel_sub = k - q
    valid = np.abs(rel_sub) <= (window // (2 * dil))
    b_idx[B_EBASE[2]] = np.where(valid, _t5_bucket(rel_sub * dil), N_BUCKETS)
    return a_idx, b_idx


_NC_CACHE = {}


def _host_inputs(inputs):
    f = lambda a: np.ascontiguousarray(np.asarray(a, dtype=np.float32))
    table = f(inputs["rel_bias_table"])
    table_ext = np.concatenate([table, np.full((1, table.shape[1]), NEG, np.float32)], axis=0)
    a_idx, b_idx = _bias_index_tiles()
    biasA = np.zeros((2, 128, 12, 128), np.float32)
    for g in range(2):
        for h in range(4):
            for oi in range(3):
                biasA[g, :, oi * 4 + h, :] = table_ext[a_idx[oi], 4 * g + h]
    biasB = np.zeros((4, 128, NB_TILES, 128), np.float32)
    for j in range(4):
        for gi in range(3):
            if gi < 2:
                nt_ = B_OFFS[gi][1] - B_OFFS[gi][0] + 1
                for t in range(B_EBASE[gi], B_EBASE[gi] + nt_):
                    biasB[j, :, t, :] = table_ext[b_idx[t], 8 + gi * 4 + j]
            else:
                for t in range(B_EBASE[2], B_EBASE[2] + 4):
                    biasB[j, :, t, :] = table_ext[b_idx[B_EBASE[2]], 8 + gi * 4 + j]
    gains = np.stack([f(inputs["attn_norm"])[0], f(inputs["ffn_norm"])[0], f(inputs["ple_norm"])[0]], 0)
    gfm = np.ascontiguousarray(gains.reshape(3, 16, 128).transpose(2, 0, 1)).reshape(128, 48)
    cw = np.concatenate([f(inputs["conv_w"])[0], f(inputs["conv_b"])], 0)
    convp = np.ascontiguousarray(cw.reshape(4, NCC, 128).transpose(2, 0, 1)).reshape(128, 4 * NCC)
    sinkb = np.ascontiguousarray(np.broadcast_to(f(inputs["sink_a"])[0][None, :], (128, 8)))
    shared = {
        "w_in": f(inputs["w_in"])[0], "w_ba": f(inputs["w_branch_a"])[0], "w_bb": f(inputs["w_branch_b"])[0],
        "w_out": f(inputs["w_out"])[0], "w_g": f(inputs["w_ffn_gate"])[0], "w_u": f(inputs["w_ffn_up"])[0],
        "w_d": f(inputs["w_ffn_down"])[0], "w_pg": f(inputs["w_ple_gate"])[0], "w_pp": f(inputs["w_ple_proj"])[0],
        "gfm": gfm, "gfin": f(inputs["final_norm"]).reshape(1, D), "convp": convp, "sinkb": sinkb,
        "biasA": biasA.reshape(2, 128, 12 * 128), "biasB": biasB.reshape(4, 128, NB_TILES * 128),
        "ident": np.eye(128, dtype=np.float32).astype(ml_dtypes.bfloat16),
    }
    return shared


def kernel(**inputs):
    x = np.asarray(inputs["x"], dtype=np.float32)
    p = np.asarray(inputs["p"], dtype=np.float32)
    shared = _host_inputs(inputs)
    if "nc" not in _NC_CACHE:
        _NC_CACHE["nc"] = build_nc()[0]
    nc = _NC_CACHE["nc"]
    n = x.shape[0]
    in_maps = []
    for b in range(n):
        m = dict(shared)
        m["x"] = np.ascontiguousarray(x[b])
        m["p"] = np.ascontiguousarray(p[0, b])
        in_maps.append(m)
    res = run_bass_kernel_spmd(nc, in_maps, core_ids=list(range(n)))
    return np.stack([np.asarray(r["out"], dtype=np.float32) for r in res.results], 0)
```

```python
import math
from contextlib import ExitStack

import numpy as np
import ml_dtypes
import concourse.bass as bass
import concourse.mybir as mybir
from concourse.bass_utils import run_bass_kernel_spmd

F32 = mybir.dt.float32
BF16 = mybir.dt.bfloat16
AF = mybir.ActivationFunctionType
ALU = mybir.AluOpType

S = 2048
D = 2048
NT = 16
KC = 16
DFF = 5632
NCC = 44
HD = 128
SCALE = HD ** -0.5
EPS = 1e-6
NEG = -30000.0
N_BUCKETS = 32
MAX_DISTANCE = 1024

QA0, KA0, VA0 = 0, 1024, 1280
QB0, KB0, VB0 = 1536, 3072, 4608
GA0, GB0 = 6144, 8192

B_PAT = ((128, 1), (512, 4), (2048, 16))
B_OFFS = ((-1, 1), (-2, 2))
B_EBASE = (0, 3, 8)
NB_TILES = 12

ENGS = ["pe", "act", "dve", "pool", "sp"]


class Tracker:
    def __init__(self):
        self.ops = {e: [] for e in ENGS}
        self.res = {}
        self.dma_cnt = {}

    def add(self, eng, fn, reads=(), writes=(), dma_key=None):
        idx = len(self.ops[eng])
        ref = (eng, idx)
        deps = set()
        for r in reads:
            st = self.res.get(r)
            if st is not None and st[0] is not None:
                deps.add(st[0])
            if st is not None and isinstance(r, tuple) and r[0] == "ps":
                deps.update(v for k_, v in st[1].items() if k_ != eng)
        for w in writes:
            st = self.res.get(w)
            if st is not None:
                if st[0] is not None:
                    deps.add(st[0])
                deps.update(st[1].values())
                deps.update(st[2])
        isdma = dma_key is not None
        if eng == "pe":
            deps = {d for d in deps if not (d[0] == "pe")}
        deps.discard(ref)
        for r in reads:
            st = self.res.setdefault(r, [None, {}, []])
            if isdma:
                st[2].append(ref)
            else:
                st[1][eng] = ref
        for w in writes:
            self.res[w] = [ref, {}, []]
        op = {"fn": fn, "deps": deps, "key": dma_key, "inc": False}
        if isdma:
            c = self.dma_cnt.get(dma_key, 0) + 1
            self.dma_cnt[dma_key] = c
            op["dval"] = 16 * c
        self.ops[eng].append(op)
        return ref

    def barrier(self):
        last = {}
        for e in ENGS:
            for i in range(len(self.ops[e]) - 1, -1, -1):
                if self.ops[e][i]["fn"] is not None and self.ops[e][i]["key"] is None:
                    last[e] = (e, i)
                    break
        dmas = {}
        for e in ENGS:
            for i, op in enumerate(self.ops[e]):
                if op["key"] is not None:
                    dmas[op["key"]] = (e, i)
        for e in ENGS:
            deps = set(v for k, v in last.items() if k != e) | set(dmas.values())
            self.ops[e].append({"fn": None, "deps": deps, "key": None, "inc": False})

    def finalize(self):
        for e in ENGS:
            for op in self.ops[e]:
                for d in op["deps"]:
                    dop = self.ops[d[0]][d[1]]
                    if dop["key"] is None:
                        dop["inc"] = True
        for e in ENGS:
            c = 0
            for op in self.ops[e]:
                if op["key"] is None and op["inc"]:
                    c += 1
                op["val"] = c

    def emit(self, eng, engobj, csem, dsem):
        waited = {}
        n_wait = 0
        for op in self.ops[eng]:
            need = {}
            for d in op["deps"]:
                dop = self.ops[d[0]][d[1]]
                if dop["key"] is not None:
                    k = ("d", dop["key"])
                    v = dop["dval"]
                else:
                    k = ("c", d[0])
                    v = dop["val"]
                if need.get(k, 0) < v:
                    need[k] = v
            for k, v in need.items():
                if waited.get(k, 0) < v:
                    sem = dsem[k[1]] if k[0] == "d" else csem[k[1]]
                    engobj.wait_ge(sem, v)
                    waited[k] = v
                    n_wait += 1
            if op["fn"] is None:
                continue
            ins = op["fn"](engobj)
            if op["key"] is not None:
                ins.then_inc(dsem[op["key"]], 16)
            elif op["inc"]:
                ins.then_inc(csem[eng], 1)
        return n_wait


def I_act(out, in_, func, **kw):
    return lambda e: e.activation(out=out, in_=in_, func=func, **kw)


def I_mm(out, lhsT, rhs, start=True, stop=True):
    return lambda e: e.matmul(out, lhsT=lhsT, rhs=rhs, start=start, stop=stop, skip_group_check=True)


def I_tr(out, in_, ident):
    return lambda e: e.transpose(out=out, in_=in_, identity=ident)


def I_tt(out, in0, in1, op):
    return lambda e: e.tensor_tensor(out=out, in0=in0, in1=in1, op=op)


def I_ts(out, in0, s1, s2, op0, op1):
    return lambda e: e.tensor_scalar(out=out, in0=in0, scalar1=s1, scalar2=s2, op0=op0, op1=op1)


def I_stt(out, in0, scalar, in1, op0, op1):
    return lambda e: e.scalar_tensor_tensor(out=out, in0=in0, scalar=scalar, in1=in1, op0=op0, op1=op1)


def I_copy(out, in_):
    return lambda e: e.tensor_copy(out=out, in_=in_)


def I_acopy(out, in_):
    return lambda e: e.copy(out=out, in_=in_)


def I_amul(out, in_, mul):
    return lambda e: e.mul(out=out, in_=in_, mul=mul)


def I_recip(out, in_):
    return lambda e: e.reciprocal(out=out, in_=in_)


def I_dma(out, in_):
    return lambda e: e.dma_start(out=out, in_=in_)


def I_memset(ap, c):
    return lambda e: e.memset(ap, c)


class Region:
    def __init__(self, t, nbytes):
        self.t = t
        self.nbytes = nbytes
        self.off = 0

    def reset(self):
        self.off = 0

    def alloc(self, shape, dtype):
        n = 1
        for s_ in shape:
            n *= s_
        esz = 4 if dtype == F32 else 2
        nb = n * esz
        self.off = (self.off + 63) // 64 * 64
        assert self.off + nb <= self.nbytes, ("region overflow", self.off, nb, self.nbytes)
        ap = self.t[:, self.off // 2:(self.off + nb) // 2]
        self.off += nb
        if dtype == F32:
            ap = ap.bitcast(F32)
        if len(shape) == 2:
            ap = ap.rearrange("p (a b) -> p a b", a=shape[0])
        elif len(shape) == 3:
            ap = ap.rearrange("p (a b c) -> p a b c", a=shape[0], b=shape[1])
        return ap


def build_nc(stop_after=None):
    nc = bass.Bass("TRN2", target_bir_lowering=False)

    def din(name, shape, dt=F32):
        return nc.dram_tensor(name, list(shape), dt, kind="ExternalInput").ap()

    x_d = din("x", [S, D])
    p_d = din("p", [S, 256])
    w_in = din("w_in", [D, 10240])
    w_ba = din("w_ba", [1024, D])
    w_bb = din("w_bb", [512, D])
    w_out = din("w_out", [D, D])
    w_g = din("w_g", [D, DFF])
    w_u = din("w_u", [D, DFF])
    w_d = din("w_d", [DFF, D])
    w_pg = din("w_pg", [D, D])
    w_pp = din("w_pp", [256, D])
    gfm_d = din("gfm", [128, 3 * 16])
    gfin_d = din("gfin", [1, D])
    convp_d = din("convp", [128, 4 * NCC])
    sink_d = din("sinkb", [128, 8])
    biasA_d = din("biasA", [2, 128, 12 * 128])
    biasB_d = din("biasB", [4, 128, NB_TILES * 128])
    ident_d = din("ident", [128, 128], BF16)
    out_d = nc.dram_tensor("out", [S, D], F32, kind="ExternalOutput").ap()
    x1s = nc.dram_tensor("x1s", [S, D], F32, kind=("ExternalOutput" if stop_after == "B" else "Internal")).ap()
    dbg = {}

    T = Tracker()

    with ExitStack() as es:
        PB = 112 * 1024
        WB = 16 * 1024
        SB = 75 * 1024
        CB = 3 * 1024
        Pt = es.enter_context(nc.sbuf_tensor("Pt", [128, PB // 2], BF16))
        Wt = es.enter_context(nc.sbuf_tensor("Wt", [128, WB // 2], BF16))
        St = es.enter_context(nc.sbuf_tensor("St", [128, SB // 2], BF16))
        Ct = es.enter_context(nc.sbuf_tensor("Ct", [128, CB // 2], BF16))
        banks = [es.enter_context(nc.psum_tensor("ps%d" % i, [128, 512], F32)) for i in range(8)]
        RP = Region(Pt, PB)
        RS = Region(St, SB)
        RC = Region(Ct, CB)

        ident = RC.alloc((128,), BF16)
        ones = RC.alloc((128,), BF16)
        sel0 = RC.alloc((128,), BF16)
        epsb = RC.alloc((1,), F32)
        gfm = RC.alloc((3, 16), F32)
        convp = RC.alloc((4, NCC), F32)
        esink = RC.alloc((8,), F32)
        stat = RC.alloc((3, 24), F32)
        gstash = RC.alloc((NCC,), F32)

        T.add("sp", I_dma(ident, ident_d[:, :]), writes=["ident"], dma_key="c0")
        T.add("sp", I_dma(gfm, gfm_d[:, :].rearrange("p (a b) -> p a b", a=3)), writes=["gfm"], dma_key="c1")
        T.add("sp", I_dma(convp, convp_d[:, :].rearrange("p (a b) -> p a b", a=4)), writes=["convp"], dma_key="c2")
        T.add("sp", I_dma(esink, sink_d[:, :]), writes=["esink"], dma_key="c3")
        T.add("dve", I_memset(ones, 1.0), writes=["ones"])
        T.add("dve", I_memset(sel0, 0.0), writes=["sel0"])
        T.add("dve", I_memset(sel0[0:1, :], 1.0), reads=["sel0"], writes=["sel0"])
        T.add("dve", I_memset(epsb, EPS), writes=["epsb"])
        T.add("act", I_act(esink, esink, AF.Exp), reads=["esink"], writes=["esink"])

        wslots = [Wt[:, i * 2048:(i + 1) * 2048] for i in range(4)]
        wctr = [0]

        def wslot():
            i = wctr[0] % 4
            wctr[0] += 1
            return i, wslots[i]

        preloaded = {}

        def prefetch_w(src2d, c0, ncols=128, nk=KC):
            preloaded[(id(src2d), c0)] = load_w_cols(src2d, c0, ncols, nk)

        def load_w_cols(src2d, c0, ncols=128, nk=KC):
            if (id(src2d), c0) in preloaded:
                return preloaded.pop((id(src2d), c0))
            i, sl = wslot()
            v = sl[:, 0:nk * ncols].rearrange("p (k c) -> p k c", k=nk)
            T.add("pool", I_dma(v, src2d[:, c0:c0 + ncols].rearrange("(k p) c -> p k c", p=128)),
                  writes=[("w", i)], dma_key=("w", i))
            return ("w", i), v

        bankctr = [0]

        def next_bank(lst):
            b = lst[bankctr[0] % len(lst)]
            bankctr[0] += 1
            return b

        evctr = [0]

        def evac_eng():
            evctr[0] += 1
            return "act" if evctr[0] % 2 else "dve"

        def evac_copy(eng, out, in_, reads, writes):
            if eng == "act":
                T.add("act", I_acopy(out, in_), reads=reads, writes=writes)
            else:
                T.add("dve", I_copy(out, in_), reads=reads, writes=writes)

        def rms_stats(xt, xres, slot, junk):
            jb = junk[slot % 2]
            T.add("act", I_act(jb, xt, AF.Square, accum_out=stat[:, 0, slot:slot + 1]), reads=[xres],
                  writes=[("ss", slot), ("junk", id(junk), slot % 2)])

        def rms_rstd(s0, s1):
            T.add("act", I_act(stat[:, 1, s0:s1], stat[:, 0, s0:s1], AF.Sqrt, scale=1.0 / D, bias=epsb[:, 0:1]),
                  reads=[("ss", k_) for k_ in range(s0, s1)] + ["epsb"], writes=[("sq", k_) for k_ in range(s0, s1)])
            T.add("dve", I_recip(stat[:, 2, s0:s1], stat[:, 1, s0:s1]),
                  reads=[("sq", k_) for k_ in range(s0, s1)], writes=[("rs", k_) for k_ in range(s0, s1)])

        def rms_apply(xt, xres, gi, xn, xnres, slot, dst_fn, dstres, eng="act", tbanks=(6, 7)):
            rs = stat[:, 2, slot:slot + 1]
            if eng == "act":
                T.add("act", I_amul(xn, xt, rs), reads=[xres, ("rs", slot)], writes=[xnres])
            else:
                T.add("pool", I_tt(xn, xt, rs.to_broadcast([128, D]), ALU.mult),
                      reads=[xres, ("rs", slot)], writes=[xnres])
            for half in range(2):
                b = tbanks[half]
                bv = banks[b][:].bitcast(BF16)
                for j in range(8):
                    kc = half * 8 + j
                    T.add("pe", I_tr(bv[:, j * 128:(j + 1) * 128], xn[:, kc * 128:(kc + 1) * 128], ident),
                          reads=[xnres, "ident"], writes=[("ps", b)])
                g_b = gfm[:, gi, half * 8:(half + 1) * 8].unsqueeze(2).to_broadcast([128, 8, 128])
                T.add("dve", I_tt(dst_fn(half), bv[:, 0:1024].rearrange("p (a b) -> p a b", a=8), g_b, ALU.mult),
                      reads=[("ps", b), "gfm"], writes=[dstres + (half,)])

        hT = RP.alloc((KC, S), BF16)
        yaT = RP.alloc((8, S), BF16)
        ybT = RP.alloc((4, S), BF16)

        RA = Region(Pt, PB)
        RA.off = 64 * 1024
        NXT = 4
        xts = [RA.alloc((D,), F32) for _ in range(NXT)]
        xns = [RA.alloc((D,), BF16) for _ in range(2)]
        junkA = [RA.alloc((D,), BF16) for _ in range(2)]
        assert RA.off <= 112 * 1024
        a2_units = [("A", 0), ("A", 1), ("B", 0), ("B", 1), ("B", 2), ("B", 3)]

        def unit_cols(kind, u):
            if kind == "A":
                return [KA0 + u * 128, VA0 + u * 128] + [QA0 + (4 * u + h) * 128 for h in range(4)]
            cols = []
            for gi in range(3):
                hc = (gi * 4 + u) * 128
                cols += [KB0 + hc, VB0 + hc, QB0 + hc]
            return cols

        if stop_after != "A1":
            for c0 in unit_cols(*a2_units[0])[:4]:
                prefetch_w(w_in, c0)
        def a1_apply(tt):
            bsel = tt % NXT
            rms_rstd(tt, tt + 1)
            rms_apply(xts[bsel], ("xt", bsel), 0, xns[tt % 2], ("xn", tt % 2), tt,
                      lambda half, tt=tt: hT[:, half * 8:(half + 1) * 8, tt * 128:(tt + 1) * 128], ("hT", tt), eng="act")

        for tt in range(NT):
            bsel = tt % NXT
            T.add("sp", I_dma(xts[bsel], x_d[tt * 128:(tt + 1) * 128, :]), writes=[("xt", bsel)], dma_key=("xt", bsel))
            rms_stats(xts[bsel], ("xt", bsel), tt, junkA)
            if tt >= 1:
                a1_apply(tt - 1)
        a1_apply(NT - 1)
        if stop_after == "A1":
            T.barrier()
            dbg["hT"] = nc.dram_tensor("dbg_hT", [128, KC * S], BF16, kind="ExternalOutput").ap()
            T.add("sp", I_dma(dbg["hT"][:, :].rearrange("p (a b) -> p a b", a=KC), hT), dma_key="dbg")

        if stop_after not in ("A1",):
            RS.reset()
            QTf = RS.alloc((4 * S,), BF16)
            QT = QTf.rearrange("p (h t) -> p h t", h=4)
            QTa = QTf.rearrange("p (q h c) -> p q h c", q=NT, h=4)
            U2sb = RS.alloc((S,), BF16)
            D2sb = RS.alloc((S,), BF16)
            KT = RS.alloc((3, S), BF16)
            VTs = [RS.alloc((S,), BF16)]
            Vb = RS.alloc((3, NT, 128), BF16)
            EB = RS.alloc((12, 128), F32)
            NRING = 4
            Es = [RS.alloc((512,), F32) for _ in range(NRING)]
            Ps = [RS.alloc((512,), BF16) for _ in range(NRING)]
            recs = [RS.alloc((128,), F32) for _ in range(2)]
            PROJ_BANKS = [0, 1, 2, 5]
            S_BANKS = [0, 1, 2, 7]
            UD_BANKS = [3, 4]
            vtc = [0]

            def project(c0, dst, dstres, dst_fn=None, deint=False):
                wres, wv = load_w_cols(w_in, c0)
                for tb in range(4):
                    b = next_bank(PROJ_BANKS)
                    for kc in range(KC):
                        T.add("pe", I_mm(banks[b][:, :], wv[:, kc, :], hT[:, kc, tb * 512:(tb + 1) * 512],
                                         start=(kc == 0), stop=(kc == KC - 1)),
                              reads=[wres] + [("hT", 4 * tb + q_, hf_) for q_ in range(4) for hf_ in range(2)],
                              writes=[("ps", b)])
                    if deint:
                        evac_copy(evac_eng(), dst.rearrange("p (r i) -> p r i", r=16)[:, :, 32 * tb:32 * tb + 32],
                                  banks[b][:, :].rearrange("p (i r) -> p r i", r=16), [("ps", b)], [dstres + (tb,)])
                    elif dst_fn is None:
                        evac_copy(evac_eng(), dst[:, tb * 512:(tb + 1) * 512], banks[b][:, :], [("ps", b)], [dstres + (tb,)])
                    else:
                        evac_copy(evac_eng(), dst_fn(tb), banks[b][:, :].rearrange("p (a b) -> p a b", a=4),
                                  [("ps", b)], [dstres + (tb,)])

            def project_v(c0, hv, deint=False):
                i = 0
                project(c0, VTs[i], ("VT", i), deint=deint)
                for half in range(2):
                    b = 6 + half
                    bv = banks[b][:].bitcast(BF16)
                    for j in range(8):
                        kb = half * 8 + j
                        T.add("pe", I_tr(bv[:, j * 128:(j + 1) * 128], VTs[i][:, kb * 128:(kb + 1) * 128], ident),
                              reads=([("VT", i, t_) for t_ in range(4)] if deint else [("VT", i, kb // 4)]) + ["ident"],
                              writes=[("ps", b)])
                    evac_copy(evac_eng(), Vb[:, hv, half * 8:(half + 1) * 8, :],
                              bv[:, 0:1024].rearrange("p (a b) -> p a b", a=8), [("ps", b)], [("V", hv)])

            udc = [0]
            stc = [0]
            mulc = [0]

            def mul_eng():
                mulc[0] += 1
                return "pool" if mulc[0] % 3 == 0 else "dve"

            def attention(groups, inject=False):
                steps = []
                for gidx, (sources, sink_col, dstT, dchunk) in enumerate(groups):
                    for qb in range(NT):
                        first = True
                        lst = []
                        for (qsel, ksel, vsel, ebase, omin, omax) in sources:
                            kbs = [kb for kb in range(qb + omin, qb + omax + 1) if 0 <= kb < NT]
                            for c in range(0, len(kbs), 4):
                                ch = kbs[c:c + 4]
                                lst.append([gidx, qb, qsel, ksel, vsel, ebase + (ch[0] - qb - omin), ch, False, False])
                        lst[0][7] = True
                        lst[-1][8] = True
                        steps.extend(lst)
                LAG = 3
                n = len(steps)
                info = {}
                for s_ in range(n + LAG):
                    if s_ < n:
                        gidx, qb, qsel, ksel, vsel, t0, ch, isfirst, islast = steps[s_]
                        k = stc[0] % NRING
                        stc[0] += 1
                        sb = S_BANKS[k]
                        w = len(ch) * 128
                        for i, kb in enumerate(ch):
                            T.add("pe", I_mm(banks[sb][:, i * 128:(i + 1) * 128], KT[:, ksel, kb * 128:(kb + 1) * 128],
                                             QT[:, qsel, qb * 128:(qb + 1) * 128]),
                                  reads=[("KT", ksel, kb // 4), ("QT", qsel, qb // 4)], writes=[("ps", sb)])
                        T.add("act", I_act(Es[k][:, 0:w], banks[sb][:, 0:w], AF.Exp, scale=SCALE),
                              reads=[("ps", sb)], writes=[("E", k)])
                        T.add(mul_eng(), I_tt(Ps[k][:, 0:w], Es[k][:, 0:w],
                                           EB[:, t0:t0 + len(ch), :].rearrange("p a b -> p (a b)"), ALU.mult),
                              reads=[("E", k), "EB"], writes=[("P", k)])
                        info[s_] = k
                    s2 = s_ - LAG
                    if s2 >= 0:
                        gidx, qb, qsel, ksel, vsel, t0, ch, isfirst, islast = steps[s2]
                        k = info[s2]
                        if isfirst:
                            udc[0] += 1
                        ub = UD_BANKS[udc[0] % 2]
                        if isfirst and inject:
                            T.add("pe", I_mm(banks[ub][:, 0:128], ident, U2sb[:, qb * 128:(qb + 1) * 128], True, False),
                                  reads=["ident", "U2sb"], writes=[("ps", ub)])
                            T.add("pe", I_mm(banks[ub][:, 128:256], sel0, D2sb[:, qb * 128:(qb + 1) * 128], False, False),
                                  reads=["sel0", "D2sb"], writes=[("ps", ub)])
                        for i, kb in enumerate(ch):
                            T.add("pe", I_mm(banks[ub][:, 0:128], Vb[:, vsel, kb, :], Ps[k][:, i * 128:(i + 1) * 128],
                                             start=(isfirst and i == 0 and not inject), stop=False),
                                  reads=[("V", vsel), ("P", k)], writes=[("ps", ub)])
                            T.add("pe", I_mm(banks[ub][:, 128:256], ones, Ps[k][:, i * 128:(i + 1) * 128],
                                             start=False, stop=(islast and i == len(ch) - 1)),
                                  reads=["ones", ("P", k)], writes=[("ps", ub)])
                        if islast:
                            sources, sink_col, dstT, dchunk = groups[gidx]
                            r = udc[0] % 2
                            if sink_col is not None:
                                T.add("dve", (lambda e, o=recs[r], i_=banks[ub][:, 128:256], s1=esink[:, sink_col:sink_col + 1]:
                                              e.tensor_scalar_add(out=o, in0=i_, scalar1=s1)),
                                      reads=[("ps", ub), "esink"], writes=[("rec", r)])
                                T.add("dve", I_recip(recs[r], recs[r]), reads=[("rec", r)], writes=[("rec", r)])
                            else:
                                T.add("act", I_act(recs[r], banks[ub][:, 128:256], AF.Ln), reads=[("ps", ub)], writes=[("rec", r)])
                                T.add("act", I_act(recs[r], recs[r], AF.Exp, scale=-1.0), reads=[("rec", r)], writes=[("rec", r)])
                            T.add("dve", I_tt(dstT[:, dchunk, qb * 128:(qb + 1) * 128], banks[ub][:, 0:128], recs[r], ALU.mult),
                                  reads=[("ps", ub), ("rec", r)], writes=[("yT", id(dstT), dchunk, qb)])

            def attention_g2():
                KTd = KT[:, 2, :].rearrange("p (r i) -> p r i", r=16)
                QTd = QT[:, 2, :].rearrange("p (r i) -> p r i", r=16)
                U2v = U2sb.rearrange("p (i r) -> p r i", r=16)
                D2v = D2sb.rearrange("p (i r) -> p r i", r=16)
                kq_reads = [("KT", 2, t_) for t_ in range(4)] + [("QT", 2, t_) for t_ in range(4)]
                LAG = 3
                info = {}
                for s_ in range(4 + LAG):
                    if s_ < 4:
                        k = stc[0] % NRING
                        stc[0] += 1
                        sb = S_BANKS[k]
                        for c in range(4):
                            r = s_ * 4 + c
                            T.add("pe", I_mm(banks[sb][:, c * 128:(c + 1) * 128], KTd[:, r, :], QTd[:, r, :]),
                                  reads=kq_reads, writes=[("ps", sb)])
                        T.add("act", I_act(Es[k][:, :], banks[sb][:, :], AF.Exp, scale=SCALE),
                              reads=[("ps", sb)], writes=[("E", k)])
                        T.add(mul_eng(), I_tt(Ps[k][:, :].rearrange("p (a b) -> p a b", a=4),
                                           Es[k][:, :].rearrange("p (a b) -> p a b", a=4),
                                           EB[:, B_EBASE[2]:B_EBASE[2] + 4, :], ALU.mult),
                              reads=[("E", k), "EB"], writes=[("P", k)])
                        info[s_] = k
                    s2 = s_ - LAG
                    if s2 >= 0:
                        k = info[s2]
                        for half in range(2):
                            udc[0] += 1
                            ub = UD_BANKS[udc[0] % 2]
                            for c2 in range(2):
                                c = half * 2 + c2
                                r = s2 * 4 + c
                                T.add("pe", I_mm(banks[ub][:, c2 * 256:c2 * 256 + 128], Vb[:, 2, r, :],
                                                 Ps[k][:, c * 128:(c + 1) * 128], c2 == 0, False),
                                      reads=[("V", 2), ("P", k)], writes=[("ps", ub)])
                                T.add("pe", I_mm(banks[ub][:, c2 * 256 + 128:c2 * 256 + 256], ones,
                                                 Ps[k][:, c * 128:(c + 1) * 128], False, c2 == 1),
                                      reads=["ones", ("P", k)], writes=[("ps", ub)])
                            r0 = s2 * 4 + half * 2
                            bview = banks[ub][:, :].rearrange("p (a b) -> p a b", a=2)
                            ee = "act" if half == 0 else "dve"
                            evac_copy(ee, U2v[:, r0:r0 + 2, :], bview[:, :, 0:128], [("ps", ub)], [("U2sb", r0)])
                            evac_copy(ee, D2v[:, r0:r0 + 2, :], bview[:, :, 128:256], [("ps", ub)], [("D2sb", r0)])
                T.add("dve", I_memset(recs[0][:, 0:1], 0.0),
                      reads=[("U2sb", r0_) for r0_ in range(0, 16, 2)] + [("D2sb", r0_) for r0_ in range(0, 16, 2)],
                      writes=["U2sb", "D2sb"])

            def attention_gqa(u):
                steps = []
                for qb in range(NT):
                    kbs = [kb for kb in (qb - 1, qb, qb + 1) if 0 <= kb < NT]
                    for kb in kbs:
                        steps.append((qb, kb, kb == kbs[0], kb == kbs[-1]))
                LAG = 3
                n = len(steps)
                info = {}
                for s_ in range(n + LAG):
                    if s_ < n:
                        qb, kb, isfirst, islast = steps[s_]
                        k = stc[0] % NRING
                        stc[0] += 1
                        sb = S_BANKS[k]
                        oi = kb - qb + 1
                        T.add("pe", I_mm(banks[sb][:, :], KT[:, 0, kb * 128:(kb + 1) * 128],
                                         QTa[:, qb, :, :].rearrange("p h c -> p (h c)")),
                              reads=[("KT", 0, kb // 4)] + [("QT", h, qb // 4) for h in range(4)], writes=[("ps", sb)])
                        T.add("act", I_act(Es[k][:, :], banks[sb][:, :], AF.Exp, scale=SCALE),
                              reads=[("ps", sb)], writes=[("E", k)])
                        T.add(mul_eng(), I_tt(Ps[k][:, :], Es[k][:, :],
                                           EB[:, oi * 4:(oi + 1) * 4, :].rearrange("p a b -> p (a b)"), ALU.mult),
                              reads=[("E", k), "EB"], writes=[("P", k)])
                        info[s_] = k
                    s2 = s_ - LAG
                    if s2 >= 0:
                        qb, kb, isfirst, islast = steps[s2]
                        k = info[s2]
                        if isfirst:
                            udc[0] += 1
                        ub = (3, 4)[udc[0] % 2]
                        db = (5, 6)[udc[0] % 2]
                        T.add("pe", I_mm(banks[ub][:, :], Vb[:, 0, kb, :], Ps[k][:, :], isfirst, islast),
                              reads=[("V", 0), ("P", k)], writes=[("ps", ub)])
                        T.add("pe", I_mm(banks[db][:, :], ones, Ps[k][:, :], isfirst, islast),
                              reads=["ones", ("P", k)], writes=[("ps", db)])
                        if islast:
                            r = udc[0] % 2
                            import os
                            if os.environ.get("K_OLDFIN"):
                                for h in range(4):
                                    T.add("dve", (lambda e, o=rec4[r][:, h * 128:(h + 1) * 128], i_=banks[db][:, h * 128:(h + 1) * 128],
                                                  s1=esink[:, 4 * u + h:4 * u + h + 1]: e.tensor_scalar_add(out=o, in0=i_, scalar1=s1)),
                                          reads=[("ps", db), "esink"], writes=[("rec4", r, h)])
                                T.add("dve", I_recip(rec4[r], rec4[r]),
                                      reads=[("rec4", r, h) for h in range(4)], writes=[("rec4", r, h) for h in range(4)])
                            else:
                              for h in range(4):
                                T.add("act", I_act(rec4[r][:, h * 128:(h + 1) * 128], banks[db][:, h * 128:(h + 1) * 128],
                                                   AF.Ln, bias=esink[:, 4 * u + h:4 * u + h + 1]),
                                      reads=[("ps", db), "esink"], writes=[("rec4", r, h)])
                              T.add("act", I_act(rec4[r], rec4[r], AF.Exp, scale=-1.0),
                                  reads=[("rec4", r, h) for h in range(4)], writes=[("rec4", r, h) for h in range(4)])
                            T.add("dve", I_tt(yaT[:, 4 * u:4 * u + 4, qb * 128:(qb + 1) * 128],
                                              banks[ub][:, :].rearrange("p (a b) -> p a b", a=4),
                                              rec4[r].rearrange("p (a b) -> p a b", a=4), ALU.mult),
                                  reads=[("ps", ub)] + [("rec4", r, h) for h in range(4)], writes=[("yaT", u, qb)])

            rec4 = [RS.alloc((512,), F32) for _ in range(2)]
            for ui, (kind, u) in enumerate(a2_units):
                if kind == "A":
                    ntile = 12
                    T.add("sp", I_dma(EB[:, 0:ntile, :], biasA_d[u].rearrange("p (a b) -> p a b", a=ntile)),
                          writes=["EB"], dma_key="EB")
                else:
                    ntile = NB_TILES
                    T.add("sp", I_dma(EB[:, 0:ntile, :], biasB_d[u].rearrange("p (a b) -> p a b", a=ntile)),
                          writes=["EB"], dma_key="EB")
                T.add("act", I_act(EB[:, 0:ntile, :], EB[:, 0:ntile, :], AF.Exp), reads=["EB"], writes=["EB"])
                if kind == "A":
                    project(KA0 + u * 128, KT[:, 0, :], ("KT", 0))
                    project_v(VA0 + u * 128, 0)
                    for h in range(4):
                        project(QA0 + (4 * u + h) * 128, None, ("QT", h),
                                dst_fn=lambda tb, h=h: QTa[:, tb * 4:(tb + 1) * 4, h, :])
                else:
                    for gi in range(3):
                        hc = (gi * 4 + u) * 128
                        project(KB0 + hc, KT[:, gi, :], ("KT", gi), deint=(gi == 2))
                        project_v(VB0 + hc, gi, deint=(gi == 2))
                        project(QB0 + hc, QT[:, gi, :], ("QT", gi), deint=(gi == 2))
                if ui + 1 < len(a2_units):
                    for c0 in unit_cols(*a2_units[ui + 1])[:4]:
                        prefetch_w(w_in, c0)
                if kind == "A":
                    attention_gqa(u)
                else:
                    attention_g2()
                    srcs = [(gi, gi, gi, B_EBASE[gi], B_OFFS[gi][0], B_OFFS[gi][1]) for gi in range(2)]
                    attention([(srcs, None, ybT, u)], inject=True)
            T.barrier()
            if stop_after == "A2":
                dbg["yaT"] = nc.dram_tensor("dbg_yaT", [128, 8 * S], BF16, kind="ExternalOutput").ap()
                dbg["ybT"] = nc.dram_tensor("dbg_ybT", [128, 4 * S], BF16, kind="ExternalOutput").ap()
                T.add("sp", I_dma(dbg["yaT"][:, :].rearrange("p (a b) -> p a b", a=8), yaT), dma_key="dbg")
                T.add("sp", I_dma(dbg["ybT"][:, :].rearrange("p (a b) -> p a b", a=4), ybT), dma_key="dbg2")

        if stop_after not in ("A1", "A2"):
            RS.reset()
            mergedT = RS.alloc((KC, 1024), BF16)
            WoA = RS.alloc((KC, 512), BF16)
            WoB = Wt[:, 0:KC * 512].rearrange("p (k c) -> p k c", k=KC)
            WOS = [(WoA, [("WoA",)]), (WoB, [("w", i) for i in range(4)])]
            sg = [RS.alloc((512,), F32) for _ in range(4)]
            xs = [RS.alloc((512,), F32) for _ in range(3)]
            os_ = [RS.alloc((512,), F32) for _ in range(3)]
            MB = [0, 1, 2, 3, 4, 5, 6, 7]
            cB = [0]
            for th in range(2):
                for fc in range(KC):
                    rga, wga = load_w_cols(w_in, GA0 + fc * 128)
                    rgb, wgb = load_w_cols(w_in, GB0 + fc * 128)
                    rba, wba = load_w_cols(w_ba, fc * 128, nk=8)
                    rbb, wbb = load_w_cols(w_bb, fc * 128, nk=4)
                    bk = [[next_bank(MB) for _ in range(2)] for _ in range(4)]
                    for wi_, (wres_, wv_, src_, nk_) in enumerate(((rga, wga, hT, KC), (rgb, wgb, hT, KC),
                                                                   (rba, wba, yaT, 8), (rbb, wbb, ybT, 4))):
                        for tb2 in range(2):
                            t0 = th * 1024 + tb2 * 512
                            for kc in range(nk_):
                                T.add("pe", I_mm(banks[bk[wi_][tb2]][:, :], wv_[:, kc, :], src_[:, kc, t0:t0 + 512],
                                                 kc == 0, kc == nk_ - 1),
                                      reads=[wres_], writes=[("ps", bk[wi_][tb2])])
                    for tb2 in range(2):
                        bga, bgb, bza, bzb = (bk[w_][tb2] for w_ in range(4))
                        i0 = (cB[0] % 2) * 2
                        cB[0] += 1
                        T.add("act", I_act(sg[i0], banks[bga][:, :], AF.Sigmoid), reads=[("ps", bga)], writes=[("sg", i0)])
                        T.add("act", I_act(sg[i0 + 1], banks[bgb][:, :], AF.Sigmoid), reads=[("ps", bgb)], writes=[("sg", i0 + 1)])
                        T.add("dve", I_tt(sg[i0], sg[i0], banks[bza][:, :], ALU.mult),
                              reads=[("sg", i0), ("ps", bza)], writes=[("sg", i0)])
                        T.add("dve", I_tt(sg[i0 + 1], sg[i0 + 1], banks[bzb][:, :], ALU.mult),
                              reads=[("sg", i0 + 1), ("ps", bzb)], writes=[("sg", i0 + 1)])
                        T.add("dve", I_tt(mergedT[:, fc, tb2 * 512:(tb2 + 1) * 512], sg[i0], sg[i0 + 1], ALU.add),
                              reads=[("sg", i0), ("sg", i0 + 1)], writes=[("mg", fc, tb2)])
                its = [(fb, tt) for fb in range(4) for tt in range(8)]

                def xload(it):
                    fb, tt = its[it]
                    trow = th * 1024 + tt * 128
                    xi = it % 3
                    T.add("sp", I_dma(xs[xi], x_d[trow:trow + 128, fb * 512:(fb + 1) * 512]),
                          writes=[("xs", xi)], dma_key=("xs", xi))

                xload(0)
                xload(1)
                for it, (fb, tt) in enumerate(its):
                    Wo_, wres_ = WOS[fb % 2]
                    if tt == 0:
                        T.add("pool", I_dma(Wo_, w_out[:, fb * 512:(fb + 1) * 512].rearrange("(k p) c -> p k c", p=128)),
                              writes=wres_, dma_key=("Wo", fb % 2))
                    trow = th * 1024 + tt * 128
                    b = next_bank(MB)
                    xi = it % 3
                    if it + 2 < len(its):
                        xload(it + 2)
                    for kc in range(KC):
                        T.add("pe", I_mm(banks[b][:, :], mergedT[:, kc, tt * 128:(tt + 1) * 128], Wo_[:, kc, :],
                                         kc == 0, kc == KC - 1),
                              reads=wres_ + [("mg", kc, tt // 4)], writes=[("ps", b)])
                    T.add("dve", I_tt(os_[xi], banks[b][:, :], xs[xi], ALU.add),
                          reads=[("ps", b), ("xs", xi)], writes=[("os", xi)])
                    T.add("sp", I_dma(x1s[trow:trow + 128, fb * 512:(fb + 1) * 512], os_[xi]),
                          reads=[("os", xi)], writes=[("x1s", trow, fb)], dma_key=("os", xi))
            T.barrier()

        if stop_after not in ("A1", "A2", "B"):
            RP.reset()
            X1 = RP.alloc((8, D), F32)
            hfT = RP.alloc((KC, 1032), BF16)
            pT = RP.alloc((2, 1024), BF16)
            pf = [RP.alloc((256,), F32) for _ in range(2)]
            pb = [RP.alloc((256,), BF16) for _ in range(2)]

            def p_load(th, tt):
                trow = th * 1024 + tt * 128
                pi = tt % 2
                T.add("sp", I_dma(pf[pi], p_d[trow:trow + 128, :]), writes=[("pf", pi)], dma_key=("pf", pi))
                T.add("act", I_acopy(pb[pi], pf[pi]), reads=[("pf", pi)], writes=[("pb", pi)])

            def p_transpose(th, tt):
                if True:
                    pi = tt % 2
                    b = 5
                    bv = banks[b][:].bitcast(BF16)
                    for j in range(2):
                        T.add("pe", I_tr(bv[:, j * 128:(j + 1) * 128], pb[pi][:, j * 128:(j + 1) * 128], ident),
                              reads=[("pb", pi), "ident"], writes=[("ps", b)])
                    T.add("act", I_acopy(pT[:, :, tt * 128:(tt + 1) * 128], bv[:, 0:256].rearrange("p (a b) -> p a b", a=2)),
                          reads=[("ps", b)], writes=[("pT", tt)])
            xnC = [RP.alloc((D,), BF16) for _ in range(2)]
            RS.reset()
            junkC = [RS.alloc((D,), BF16) for _ in range(2)]
            xh = RS.alloc((D,), F32)
            halT = RS.alloc((KC, 128), BF16)
            Wpg = [RS.alloc((KC, 256), BF16) for _ in range(2)]
            Wpp = [RS.alloc((2, 256), BF16) for _ in range(2)]
            sg3 = [RS.alloc((256,), F32) for _ in range(2)]
            tm3 = [RS.alloc((256,), F32) for _ in range(2)]
            gB = RS.alloc((D,), F32)
            ost = [RS.alloc((D,), F32) for _ in range(2)]
            xn1s = xnC
            xn3s = xnC
            junk1 = junkC
            junk3 = junkC

            def c1_load_halo(th_):
                hrow_ = 1024 if th_ == 0 else 896
                T.add("sp", I_dma(xh, x1s[hrow_:hrow_ + 128, :]), writes=["xh"], dma_key="xh")

            def c1_load_tile(th_, tt):
                trow_ = th_ * 1024 + tt * 128
                T.add("sp", I_dma(X1[:, tt, :], x1s[trow_:trow_ + 128, :]), writes=[("X1", tt)], dma_key=("X1", tt))

            for th in range(2):
                hcol = 0 if th == 0 else 127
                if th == 0:
                    c1_load_halo(0)
                    for tt in range(8):
                        c1_load_tile(0, tt)
                if th == 0:
                    rms_stats(xh, "xh", 0, junk1)
                for tt in range(4):
                    rms_stats(X1[:, tt, :], ("X1", tt), tt + 1, junk1)
                rms_rstd(0 if th == 0 else 1, 5)
                if th == 0:
                    rms_apply(xh, "xh", 1, xn1s[0], ("xnC", 0), 0,
                              lambda half: halT[:, half * 8:(half + 1) * 8, :], ("halT",), eng="act")
                    T.add("dve", I_copy(hfT[:, :, 1024:1025], halT[:, :, hcol:hcol + 1]), reads=[("halT", 0), ("halT", 1)],
                          writes=[("hfT", -1)])
                for tt in range(4, 8):
                    rms_stats(X1[:, tt, :], ("X1", tt), tt + 1, junk1)
                rms_rstd(5, 9)
                for tt in range(8):
                    rms_apply(X1[:, tt, :], ("X1", tt), 1, xn1s[(tt + 1) % 2], ("xnC", (tt + 1) % 2), tt + 1,
                              lambda half, tt=tt: hfT[:, half * 8:(half + 1) * 8, tt * 128:(tt + 1) * 128],
                              ("hfT", tt), eng="act")
                T.barrier()
                RS.reset()
                aT = RS.alloc((11, 1024), BF16)
                Wd = [RS.alloc((11, 512), BF16) for _ in range(2)]
                G = [RS.alloc((1026,), F32) for _ in range(2)]
                ACC = [RS.alloc((1024,), F32) for _ in range(2)]
                GEL = [RS.alloc((1024,), F32) for _ in range(2)]
                FB = [0, 1, 2, 3, 4, 5, 6]
                cC = [0]
                for qd in range(4):
                    for cl in range(11):
                        c = qd * 11 + cl
                        rg_, wg_ = load_w_cols(w_g, c * 128)
                        ru_, wu_ = load_w_cols(w_u, c * 128)
                        gi = cC[0] % 2
                        cC[0] += 1
                        bg = [next_bank(FB), next_bank(FB)]
                        for tb2 in range(2):
                            for kc in range(KC):
                                T.add("pe", I_mm(banks[bg[tb2]][:, :], wg_[:, kc, :],
                                                 hfT[:, kc, tb2 * 512:(tb2 + 1) * 512], kc == 0, kc == KC - 1),
                                      reads=[rg_], writes=[("ps", bg[tb2])])
                        if th == 0:
                            for kc in range(KC):
                                T.add("pe", I_mm(banks[7][:, 0:1], wg_[:, kc, :], hfT[:, kc, 1024:1025], kc == 0, kc == KC - 1),
                                      reads=[rg_], writes=[("ps", 7)])
                        bu = [next_bank(FB), next_bank(FB)]
                        for tb2 in range(2):
                            for kc in range(KC):
                                T.add("pe", I_mm(banks[bu[tb2]][:, :], wu_[:, kc, :],
                                                 hfT[:, kc, tb2 * 512:(tb2 + 1) * 512], kc == 0, kc == KC - 1),
                                      reads=[ru_], writes=[("ps", bu[tb2])])
                        Gt = G[gi]
                        T.add("act", I_acopy(Gt[:, 1:513], banks[bg[0]][:, :]), reads=[("ps", bg[0])], writes=[("G", gi, 0)])
                        T.add("act", I_acopy(Gt[:, 513:1025], banks[bg[1]][:, :]), reads=[("ps", bg[1])], writes=[("G", gi, 1)])
                        if th == 0:
                            T.add("act", I_acopy(Gt[:, 1025:1026], banks[7][:, 0:1]), reads=[("ps", 7)], writes=[("G", gi, 2)])
                            T.add("dve", I_memset(Gt[:, 0:1], 0.0), writes=[("G", gi, 3)])
                            T.add("dve", I_copy(gstash[:, c:c + 1], Gt[:, 1024:1025]), reads=[("G", gi, 1)], writes=[("gstash", c)])
                        else:
                            T.add("dve", I_copy(Gt[:, 0:1], gstash[:, c:c + 1]), reads=[("gstash", c)], writes=[("G", gi, 2)])
                            T.add("dve", I_memset(Gt[:, 1025:1026], 0.0), writes=[("G", gi, 3)])
                        gres = [("G", gi, k_) for k_ in range(4)]
                        A_ = ACC[gi]
                        T.add("dve", I_ts(A_, Gt[:, 1:1025], convp[:, 1, c:c + 1], convp[:, 3, c:c + 1], ALU.mult, ALU.add),
                              reads=gres + ["convp"], writes=[("ACC", gi)])
                        T.add("dve", I_stt(A_, Gt[:, 0:1024], convp[:, 0, c:c + 1], A_, ALU.mult, ALU.add),
                              reads=gres + [("ACC", gi)], writes=[("ACC", gi)])
                        T.add("dve", I_stt(A_, Gt[:, 2:1026], convp[:, 2, c:c + 1], A_, ALU.mult, ALU.add),
                              reads=gres + [("ACC", gi)], writes=[("ACC", gi)])
                        T.add("act", I_act(GEL[gi], A_, AF.Gelu_apprx_tanh), reads=[("ACC", gi)], writes=[("GEL", gi)])
                        for tb2 in range(2):
                            T.add("dve", I_tt(aT[:, cl, tb2 * 512:(tb2 + 1) * 512], GEL[gi][:, tb2 * 512:(tb2 + 1) * 512],
                                              banks[bu[tb2]][:, :], ALU.mult),
                                  reads=[("GEL", gi), ("ps", bu[tb2])], writes=[("aT", cl, tb2)])
                        if c < 8:
                            p_load(th, c)
                        if 1 <= c < 9:
                            p_transpose(th, c - 1)
                    for fb in range(4):
                        wi = cC[0] % 2
                        cC[0] += 1
                        T.add("pool", I_dma(Wd[wi], w_d[qd * 11 * 128:(qd + 1) * 11 * 128, fb * 512:(fb + 1) * 512]
                                            .rearrange("(c p) f -> p c f", p=128)),
                              writes=[("Wd", wi)], dma_key=("Wd", wi))
                        for t2 in range(2):
                            bd = [next_bank(FB) for _ in range(4)]
                            for cl in range(11):
                                for t4 in range(4):
                                    tt = t2 * 4 + t4
                                    T.add("pe", I_mm(banks[bd[t4]][:, :], aT[:, cl, tt * 128:(tt + 1) * 128], Wd[wi][:, cl, :],
                                                     cl == 0, cl == 10),
                                          reads=[("Wd", wi), ("aT", cl, tt // 4)], writes=[("ps", bd[t4])])
                            for t4 in range(4):
                                tt = t2 * 4 + t4
                                xsl = X1[:, tt, fb * 512:(fb + 1) * 512]
                                T.add("dve", I_tt(xsl, banks[bd[t4]][:, :], xsl, ALU.add),
                                      reads=[("ps", bd[t4]), ("X1", tt)], writes=[("X1", tt)])
                T.barrier()
                T.add("sp", I_dma(gB, gfin_d.partition_broadcast(128)), writes=["gB"], dma_key="gB")
                for tt in range(8):
                    rms_stats(X1[:, tt, :], ("X1", tt), tt, junk3)
                rms_rstd(0, 8)
                for tt in range(8):
                    rms_apply(X1[:, tt, :], ("X1", tt), 2, xn3s[tt % 2], ("xnC", tt % 2), tt,
                              lambda half, tt=tt: hfT[:, half * 8:(half + 1) * 8, tt * 128:(tt + 1) * 128],
                              ("hfT", tt), eng="act")
                PB_ = [0, 1, 2, 3, 4]
                c3 = [0]

                def c3_final(tt):
                    rms_stats(X1[:, tt, :], ("X1", tt), 16 + tt, junk3)
                    if tt % 4 != 3:
                        return
                    rms_rstd(16 + tt - 3, 16 + tt + 1)
                    for t_ in range(tt - 3, tt + 1):
                        trow = th * 1024 + t_ * 128
                        oi = t_ % 2
                        T.add("dve", I_stt(ost[oi], X1[:, t_, :], stat[:, 2, 16 + t_:17 + t_], gB, ALU.mult, ALU.mult),
                              reads=[("X1", t_), ("rs", 16 + t_), "gB"], writes=[("ost", oi)])
                        T.add("sp", I_dma(out_d[trow:trow + 128, :], ost[oi]), reads=[("ost", oi)], writes=[("out", trow)],
                              dma_key=("ost", oi))
                        if th == 0:
                            c1_load_tile(1, t_)

                def c3_wload(fb):
                    wi = fb % 2
                    T.add("pool", I_dma(Wpg[wi], w_pg[:, fb * 256:(fb + 1) * 256].rearrange("(k p) c -> p k c", p=128)),
                          writes=[("Wpg", wi)], dma_key=("Wpg", wi))
                    T.add("pool", I_dma(Wpp[wi], w_pp[:, fb * 256:(fb + 1) * 256].rearrange("(k p) c -> p k c", p=128)),
                          writes=[("Wpp", wi)], dma_key=("Wpp", wi))

                order = [(fb, tt) for fb in range(6) for tt in range(8)]
                order += [(fb, tt) for fb in (6, 7) for tt in range(4)] + [(fb, tt) for fb in (6, 7) for tt in range(4, 8)]
                loaded = set()
                for (fb, tt) in order:
                    wi = fb % 2
                    if fb not in loaded:
                        c3_wload(fb)
                        loaded.add(fb)
                        if fb == 6:
                            c3_wload(7)
                            loaded.add(7)
                    if True:
                        ba = next_bank(PB_)
                        for kc in range(KC):
                            T.add("pe", I_mm(banks[ba][:, 0:256], hfT[:, kc, tt * 128:(tt + 1) * 128], Wpg[wi][:, kc, :],
                                             kc == 0, kc == KC - 1),
                                  reads=[("Wpg", wi), ("hfT", tt, 0), ("hfT", tt, 1)], writes=[("ps", ba)])
                        for kc in range(2):
                            T.add("pe", I_mm(banks[ba][:, 256:512], pT[:, kc, tt * 128:(tt + 1) * 128], Wpp[wi][:, kc, :],
                                             False, kc == 1),
                                  reads=[("Wpp", wi), ("pT", tt)], writes=[("ps", ba)])
                        i3 = c3[0] % 2
                        c3[0] += 1
                        T.add("act", I_act(sg3[i3], banks[ba][:, 0:256], AF.Sigmoid), reads=[("ps", ba)], writes=[("sg3", i3)])
                        T.add("dve", I_tt(tm3[i3], sg3[i3], banks[ba][:, 256:512], ALU.mult),
                              reads=[("sg3", i3), ("ps", ba)], writes=[("tm3", i3)])
                        xsl = X1[:, tt, fb * 256:(fb + 1) * 256]
                        T.add("dve", I_tt(xsl, xsl, tm3[i3], ALU.add), reads=[("tm3", i3), ("X1", tt)], writes=[("X1", tt)])
                        if fb == 7:
                            c3_final(tt)
                if th == 1:
                    T.barrier()

        T.barrier()
        T.finalize()

        csem = {e: es.enter_context(nc.semaphore("c_" + e)) for e in ["pe", "act", "dve", "pool"]}
        dsem = {}
        for i, k in enumerate(sorted(T.dma_cnt.keys(), key=str)):
            dsem[k] = es.enter_context(nc.semaphore("d%d" % i))
        with nc.Block() as block:
            @block.tensor
            def _(e):
                T.emit("pe", e, csem, dsem)

            @block.scalar
            def _(e):
                T.emit("act", e, csem, dsem)

            @block.vector
            def _(e):
                T.emit("dve", e, csem, dsem)

            @block.gpsimd
            def _(e):
                T.emit("pool", e, csem, dsem)

            @block.sync
            def _(e):
                T.emit("sp", e, csem, dsem)
    return nc, dbg


def _t5_bucket(rel):
    half = N_BUCKETS // 2
    max_exact = half // 2
    n = np.abs(rel)
    side = np.where(rel > 0, half, 0)
    nf = np.maximum(n, 1).astype(np.float32)
    large = max_exact + (np.log(nf / np.float32(max_exact)) / np.float32(math.log(MAX_DISTANCE / max_exact))
                         * np.float32(half - max_exact)).astype(np.int32)
    large = np.minimum(large, half - 1)
    return side + np.where(n < max_exact, n, large)


def _bias_index_tiles():
    k = np.arange(128)[:, None]
    q = np.arange(128)[None, :]
    a_idx = np.zeros((3, 128, 128), np.int64)
    for oi, o in enumerate((-1, 0, 1)):
        rel = o * 128 + k - q
        valid = np.abs(rel) <= 128
        a_idx[oi] = np.where(valid, _t5_bucket(rel), N_BUCKETS)
    b_idx = np.zeros((NB_TILES, 128, 128), np.int64)
    for gi in range(2):
        window, dil = B_PAT[gi]
        omin, omax = B_OFFS[gi]
        for o in range(omin, omax + 1):
            rel = o * 128 + k - q
            valid = (rel % dil == 0) & (np.abs(rel) <= (window // 2))
            b_idx[B_EBASE[gi] + o - omin] = np.where(valid, _t5_bucket(rel), N_BUCKETS)
    window, dil = B_PAT[2]
    rel_sub = k - q
    valid = np.abs(rel_sub) <= (window // (2 * dil))
    b_idx[B_EBASE[2]] = np.where(valid, _t5_bucket(rel_sub * dil), N_BUCKETS)
    return a_idx, b_idx


_NC_CACHE = {}


def _host_inputs(inputs):
    f = lambda a: np.ascontiguousarray(np.asarray(a, dtype=np.float32))
    table = f(inputs["rel_bias_table"])
    table_ext = np.concatenate([table, np.full((1, table.shape[1]), NEG, np.float32)], axis=0)
    a_idx, b_idx = _bias_index_tiles()
    biasA = np.zeros((2, 128, 12, 128), np.float32)
    for g in range(2):
        for h in range(4):
            for oi in range(3):
                biasA[g, :, oi * 4 + h, :] = table_ext[a_idx[oi], 4 * g + h]
    biasB = np.zeros((4, 128, NB_TILES, 128), np.float32)
    for j in range(4):
        for gi in range(3):
            if gi < 2:
                nt_ = B_OFFS[gi][1] - B_OFFS[gi][0] + 1
                for t in range(B_EBASE[gi], B_EBASE[gi] + nt_):
                    biasB[j, :, t, :] = table_ext[b_idx[t], 8 + gi * 4 + j]
            else:
                for t in range(B_EBASE[2], B_EBASE[2] + 4):
                    biasB[j, :, t, :] = table_ext[b_idx[B_EBASE[2]], 8 + gi * 4 + j]
    gains = np.stack([f(inputs["attn_norm"])[0], f(inputs["ffn_norm"])[0], f(inputs["ple_norm"])[0]], 0)
    gfm = np.ascontiguousarray(gains.reshape(3, 16, 128).transpose(2, 0, 1)).reshape(128, 48)
    cw = np.concatenate([f(inputs["conv_w"])[0], f(inputs["conv_b"])], 0)
    convp = np.ascontiguousarray(cw.reshape(4, NCC, 128).transpose(2, 0, 1)).reshape(128, 4 * NCC)
    sinkb = np.ascontiguousarray(np.broadcast_to(f(inputs["sink_a"])[0][None, :], (128, 8)))
    shared = {
        "w_in": f(inputs["w_in"])[0], "w_ba": f(inputs["w_branch_a"])[0], "w_bb": f(inputs["w_branch_b"])[0],
        "w_out": f(inputs["w_out"])[0], "w_g": f(inputs["w_ffn_gate"])[0], "w_u": f(inputs["w_ffn_up"])[0],
        "w_d": f(inputs["w_ffn_down"])[0], "w_pg": f(inputs["w_ple_gate"])[0], "w_pp": f(inputs["w_ple_proj"])[0],
        "gfm": gfm, "gfin": f(inputs["final_norm"]).reshape(1, D), "convp": convp, "sinkb": sinkb,
        "biasA": biasA.reshape(2, 128, 12 * 128), "biasB": biasB.reshape(4, 128, NB_TILES * 128),
        "ident": np.eye(128, dtype=np.float32).astype(ml_dtypes.bfloat16),
    }
    return shared


def kernel(**inputs):
    x = np.asarray(inputs["x"], dtype=np.float32)
    p = np.asarray(inputs["p"], dtype=np.float32)
    shared = _host_inputs(inputs)
    if "nc" not in _NC_CACHE:
        _NC_CACHE["nc"] = build_nc()[0]
    nc = _NC_CACHE["nc"]
    n = x.shape[0]
    in_maps = []
    for b in range(n):
        m = dict(shared)
        m["x"] = np.ascontiguousarray(x[b])
        m["p"] = np.ascontiguousarray(p[0, b])
        in_maps.append(m)
    res = run_bass_kernel_spmd(nc, in_maps, core_ids=list(range(n)))
    return np.stack([np.asarray(r["out"], dtype=np.float32) for r in res.results], 0)
```

```python
import math
from contextlib import ExitStack

import numpy as np
import ml_dtypes
import concourse.bass as bass
import concourse.mybir as mybir
from concourse.bass_utils import run_bass_kernel_spmd

F32 = mybir.dt.float32
BF16 = mybir.dt.bfloat16
AF = mybir.ActivationFunctionType
ALU = mybir.AluOpType

S = 2048
D = 2048
NT = 16
KC = 16
DFF = 5632
NCC = 44
HD = 128
SCALE = HD ** -0.5
EPS = 1e-6
NEG = -30000.0
N_BUCKETS = 32
MAX_DISTANCE = 1024

QA0, KA0, VA0 = 0, 1024, 1280
QB0, KB0, VB0 = 1536, 3072, 4608
GA0, GB0 = 6144, 8192

B_PAT = ((128, 1), (512, 4), (2048, 16))
B_OFFS = ((-1, 1), (-2, 2))
B_EBASE = (0, 3, 8)
NB_TILES = 12

ENGS = ["pe", "act", "dve", "pool", "sp"]


class Tracker:
    def __init__(self):
        self.ops = {e: [] for e in ENGS}
        self.res = {}
        self.dma_cnt = {}

    def add(self, eng, fn, reads=(), writes=(), dma_key=None):
        idx = len(self.ops[eng])
        ref = (eng, idx)
        deps = set()
        for r in reads:
            st = self.res.get(r)
            if st is not None and st[0] is not None:
                deps.add(st[0])
            if st is not None and isinstance(r, tuple) and r[0] == "ps":
                deps.update(v for k_, v in st[1].items() if k_ != eng)
        for w in writes:
            st = self.res.get(w)
            if st is not None:
                if st[0] is not None:
                    deps.add(st[0])
                deps.update(st[1].values())
                deps.update(st[2])
        isdma = dma_key is not None
        if eng == "pe":
            deps = {d for d in deps if not (d[0] == "pe")}
        deps.discard(ref)
        for r in reads:
            st = self.res.setdefault(r, [None, {}, []])
            if isdma:
                st[2].append(ref)
            else:
                st[1][eng] = ref
        for w in writes:
            self.res[w] = [ref, {}, []]
        op = {"fn": fn, "deps": deps, "key": dma_key, "inc": False}
        if isdma:
            c = self.dma_cnt.get(dma_key, 0) + 1
            self.dma_cnt[dma_key] = c
            op["dval"] = 16 * c
        self.ops[eng].append(op)
        return ref

    def barrier(self):
        last = {}
        for e in ENGS:
            for i in range(len(self.ops[e]) - 1, -1, -1):
                if self.ops[e][i]["fn"] is not None and self.ops[e][i]["key"] is None:
                    last[e] = (e, i)
                    break
        dmas = {}
        for e in ENGS:
            for i, op in enumerate(self.ops[e]):
                if op["key"] is not None:
                    dmas[op["key"]] = (e, i)
        for e in ENGS:
            deps = set(v for k, v in last.items() if k != e) | set(dmas.values())
            self.ops[e].append({"fn": None, "deps": deps, "key": None, "inc": False})

    def finalize(self):
        for e in ENGS:
            for op in self.ops[e]:
                for d in op["deps"]:
                    dop = self.ops[d[0]][d[1]]
                    if dop["key"] is None:
                        dop["inc"] = True
        for e in ENGS:
            c = 0
            for op in self.ops[e]:
                if op["key"] is None and op["inc"]:
                    c += 1
                op["val"] = c

    def emit(self, eng, engobj, csem, dsem):
        waited = {}
        n_wait = 0
        for op in self.ops[eng]:
            need = {}
            for d in op["deps"]:
                dop = self.ops[d[0]][d[1]]
                if dop["key"] is not None:
                    k = ("d", dop["key"])
                    v = dop["dval"]
                else:
                    k = ("c", d[0])
                    v = dop["val"]
                if need.get(k, 0) < v:
                    need[k] = v
            for k, v in need.items():
                if waited.get(k, 0) < v:
                    sem = dsem[k[1]] if k[0] == "d" else csem[k[1]]
                    engobj.wait_ge(sem, v)
                    waited[k] = v
                    n_wait += 1
            if op["fn"] is None:
                continue
            ins = op["fn"](engobj)
            if op["key"] is not None:
                ins.then_inc(dsem[op["key"]], 16)
            elif op["inc"]:
                ins.then_inc(csem[eng], 1)
        return n_wait


def I_act(out, in_, func, **kw):
    return lambda e: e.activation(out=out, in_=in_, func=func, **kw)


def I_mm(out, lhsT, rhs, start=True, stop=True):
    return lambda e: e.matmul(out, lhsT=lhsT, rhs=rhs, start=start, stop=stop, skip_group_check=True)


def I_tr(out, in_, ident):
    return lambda e: e.transpose(out=out, in_=in_, identity=ident)


def I_tt(out, in0, in1, op):
    return lambda e: e.tensor_tensor(out=out, in0=in0, in1=in1, op=op)


def I_ts(out, in0, s1, s2, op0, op1):
    return lambda e: e.tensor_scalar(out=out, in0=in0, scalar1=s1, scalar2=s2, op0=op0, op1=op1)


def I_stt(out, in0, scalar, in1, op0, op1):
    return lambda e: e.scalar_tensor_tensor(out=out, in0=in0, scalar=scalar, in1=in1, op0=op0, op1=op1)


def I_copy(out, in_):
    return lambda e: e.tensor_copy(out=out, in_=in_)


def I_acopy(out, in_):
    return lambda e: e.copy(out=out, in_=in_)


def I_amul(out, in_, mul):
    return lambda e: e.mul(out=out, in_=in_, mul=mul)


def I_recip(out, in_):
    return lambda e: e.reciprocal(out=out, in_=in_)


def I_dma(out, in_):
    return lambda e: e.dma_start(out=out, in_=in_)


def I_memset(ap, c):
    return lambda e: e.memset(ap, c)


class Region:
    def __init__(self, t, nbytes):
        self.t = t
        self.nbytes = nbytes
        self.off = 0

    def reset(self):
        self.off = 0

    def alloc(self, shape, dtype):
        n = 1
        for s_ in shape:
            n *= s_
        esz = 4 if dtype == F32 else 2
        nb = n * esz
        self.off = (self.off + 63) // 64 * 64
        assert self.off + nb <= self.nbytes, ("region overflow", self.off, nb, self.nbytes)
        ap = self.t[:, self.off // 2:(self.off + nb) // 2]
        self.off += nb
        if dtype == F32:
            ap = ap.bitcast(F32)
        if len(shape) == 2:
            ap = ap.rearrange("p (a b) -> p a b", a=shape[0])
        elif len(shape) == 3:
            ap = ap.rearrange("p (a b c) -> p a b c", a=shape[0], b=shape[1])
        return ap


def build_nc(stop_after=None):
    nc = bass.Bass("TRN2", target_bir_lowering=False)

    def din(name, shape, dt=F32):
        return nc.dram_tensor(name, list(shape), dt, kind="ExternalInput").ap()

    x_d = din("x", [S, D])
    p_d = din("p", [S, 256])
    w_in = din("w_in", [D, 10240])
    w_ba = din("w_ba", [1024, D])
    w_bb = din("w_bb", [512, D])
    w_out = din("w_out", [D, D])
    w_g = din("w_g", [D, DFF])
    w_u = din("w_u", [D, DFF])
    w_d = din("w_d", [DFF, D])
    w_pg = din("w_pg", [D, D])
    w_pp = din("w_pp", [256, D])
    gfm_d = din("gfm", [128, 3 * 16])
    gfin_d = din("gfin", [1, D])
    convp_d = din("convp", [128, 4 * NCC])
    sink_d = din("sinkb", [128, 8])
    biasA_d = din("biasA", [2, 128, 12 * 128])
    biasB_d = din("biasB", [4, 128, NB_TILES * 128])
    ident_d = din("ident", [128, 128], BF16)
    out_d = nc.dram_tensor("out", [S, D], F32, kind="ExternalOutput").ap()
    x1s = nc.dram_tensor("x1s", [S, D], F32, kind=("ExternalOutput" if stop_after == "B" else "Internal")).ap()
    dbg = {}

    T = Tracker()

    with ExitStack() as es:
        PB = 112 * 1024
        WB = 16 * 1024
        SB = 75 * 1024
        CB = 3 * 1024
        Pt = es.enter_context(nc.sbuf_tensor("Pt", [128, PB // 2], BF16))
        Wt = es.enter_context(nc.sbuf_tensor("Wt", [128, WB // 2], BF16))
        St = es.enter_context(nc.sbuf_tensor("St", [128, SB // 2], BF16))
        Ct = es.enter_context(nc.sbuf_tensor("Ct", [128, CB // 2], BF16))
        banks = [es.enter_context(nc.psum_tensor("ps%d" % i, [128, 512], F32)) for i in range(8)]
        RP = Region(Pt, PB)
        RS = Region(St, SB)
        RC = Region(Ct, CB)

        ident = RC.alloc((128,), BF16)
        ones = RC.alloc((128,), BF16)
        sel0 = RC.alloc((128,), BF16)
        epsb = RC.alloc((1,), F32)
        gfm = RC.alloc((3, 16), F32)
        convp = RC.alloc((4, NCC), F32)
        esink = RC.alloc((8,), F32)
        stat = RC.alloc((3, 24), F32)
        gstash = RC.alloc((NCC,), F32)

        T.add("sp", I_dma(ident, ident_d[:, :]), writes=["ident"], dma_key="c0")
        T.add("sp", I_dma(gfm, gfm_d[:, :].rearrange("p (a b) -> p a b", a=3)), writes=["gfm"], dma_key="c1")
        T.add("sp", I_dma(convp, convp_d[:, :].rearrange("p (a b) -> p a b", a=4)), writes=["convp"], dma_key="c2")
        T.add("sp", I_dma(esink, sink_d[:, :]), writes=["esink"], dma_key="c3")
        T.add("dve", I_memset(ones, 1.0), writes=["ones"])
        T.add("dve", I_memset(sel0, 0.0), writes=["sel0"])
        T.add("dve", I_memset(sel0[0:1, :], 1.0), reads=["sel0"], writes=["sel0"])
        T.add("dve", I_memset(epsb, EPS), writes=["epsb"])
        T.add("act", I_act(esink, esink, AF.Exp), reads=["esink"], writes=["esink"])

        wslots = [Wt[:, i * 2048:(i + 1) * 2048] for i in range(4)]
        wctr = [0]

        def wslot():
            i = wctr[0] % 4
            wctr[0] += 1
            return i, wslots[i]

        preloaded = {}

        def prefetch_w(src2d, c0, ncols=128, nk=KC):
            preloaded[(id(src2d), c0)] = load_w_cols(src2d, c0, ncols, nk)

        def load_w_cols(src2d, c0, ncols=128, nk=KC):
            if (id(src2d), c0) in preloaded:
                return preloaded.pop((id(src2d), c0))
            i, sl = wslot()
            v = sl[:, 0:nk * ncols].rearrange("p (k c) -> p k c", k=nk)
            T.add("pool", I_dma(v, src2d[:, c0:c0 + ncols].rearrange("(k p) c -> p k c", p=128)),
                  writes=[("w", i)], dma_key=("w", i))
            return ("w", i), v

        bankctr = [0]

        def next_bank(lst):
            b = lst[bankctr[0] % len(lst)]
            bankctr[0] += 1
            return b

        evctr = [0]

        def evac_eng():
            evctr[0] += 1
            return "act" if evctr[0] % 2 else "dve"

        def evac_copy(eng, out, in_, reads, writes):
            if eng == "act":
                T.add("act", I_acopy(out, in_), reads=reads, writes=writes)
            else:
                T.add("dve", I_copy(out, in_), reads=reads, writes=writes)

        def rms_stats(xt, xres, slot, junk):
            jb = junk[slot % 2]
            T.add("act", I_act(jb, xt, AF.Square, accum_out=stat[:, 0, slot:slot + 1]), reads=[xres],
                  writes=[("ss", slot), ("junk", id(junk), slot % 2)])

        def rms_rstd(s0, s1):
            T.add("act", I_act(stat[:, 1, s0:s1], stat[:, 0, s0:s1], AF.Sqrt, scale=1.0 / D, bias=epsb[:, 0:1]),
                  reads=[("ss", k_) for k_ in range(s0, s1)] + ["epsb"], writes=[("sq", k_) for k_ in range(s0, s1)])
            T.add("dve", I_recip(stat[:, 2, s0:s1], stat[:, 1, s0:s1]),
                  reads=[("sq", k_) for k_ in range(s0, s1)], writes=[("rs", k_) for k_ in range(s0, s1)])

        def rms_apply(xt, xres, gi, xn, xnres, slot, dst_fn, dstres, eng="act", tbanks=(6, 7)):
            rs = stat[:, 2, slot:slot + 1]
            if eng == "act":
                T.add("act", I_amul(xn, xt, rs), reads=[xres, ("rs", slot)], writes=[xnres])
            else:
                T.add("pool", I_tt(xn, xt, rs.to_broadcast([128, D]), ALU.mult),
                      reads=[xres, ("rs", slot)], writes=[xnres])
            for half in range(2):
                b = tbanks[half]
                bv = banks[b][:].bitcast(BF16)
                for j in range(8):
                    kc = half * 8 + j
                    T.add("pe", I_tr(bv[:, j * 128:(j + 1) * 128], xn[:, kc * 128:(kc + 1) * 128], ident),
                          reads=[xnres, "ident"], writes=[("ps", b)])
                g_b = gfm[:, gi, half * 8:(half + 1) * 8].unsqueeze(2).to_broadcast([128, 8, 128])
                T.add("dve", I_tt(dst_fn(half), bv[:, 0:1024].rearrange("p (a b) -> p a b", a=8), g_b, ALU.mult),
                      reads=[("ps", b), "gfm"], writes=[dstres + (half,)])

        hT = RP.alloc((KC, S), BF16)
        yaT = RP.alloc((8, S), BF16)
        ybT = RP.alloc((4, S), BF16)

        RA = Region(Pt, PB)
        RA.off = 64 * 1024
        NXT = 4
        xts = [RA.alloc((D,), F32) for _ in range(NXT)]
        xns = [RA.alloc((D,), BF16) for _ in range(2)]
        junkA = [RA.alloc((D,), BF16) for _ in range(2)]
        assert RA.off <= 112 * 1024
        a2_units = [("A", 0), ("A", 1), ("B", 0), ("B", 1), ("B", 2), ("B", 3)]

        def unit_cols(kind, u):
            if kind == "A":
                return [KA0 + u * 128, VA0 + u * 128] + [QA0 + (4 * u + h) * 128 for h in range(4)]
            cols = []
            for gi in range(3):
                hc = (gi * 4 + u) * 128
                cols += [KB0 + hc, VB0 + hc, QB0 + hc]
            return cols

        if stop_after != "A1":
            for c0 in unit_cols(*a2_units[0])[:4]:
                prefetch_w(w_in, c0)
        def a1_apply(tt):
            bsel = tt % NXT
            rms_rstd(tt, tt + 1)
            rms_apply(xts[bsel], ("xt", bsel), 0, xns[tt % 2], ("xn", tt % 2), tt,
                      lambda half, tt=tt: hT[:, half * 8:(half + 1) * 8, tt * 128:(tt + 1) * 128], ("hT", tt), eng="act")

        for tt in range(NT):
            bsel = tt % NXT
            T.add("sp", I_dma(xts[bsel], x_d[tt * 128:(tt + 1) * 128, :]), writes=[("xt", bsel)], dma_key=("xt", bsel))
            rms_stats(xts[bsel], ("xt", bsel), tt, junkA)
            if tt >= 1:
                a1_apply(tt - 1)
        a1_apply(NT - 1)
        if stop_after == "A1":
            T.barrier()
            dbg["hT"] = nc.dram_tensor("dbg_hT", [128, KC * S], BF16, kind="ExternalOutput").ap()
            T.add("sp", I_dma(dbg["hT"][:, :].rearrange("p (a b) -> p a b", a=KC), hT), dma_key="dbg")

        if stop_after not in ("A1",):
            RS.reset()
            QTf = RS.alloc((4 * S,), BF16)
            QT = QTf.rearrange("p (h t) -> p h t", h=4)
            QTa = QTf.rearrange("p (q h c) -> p q h c", q=NT, h=4)
            U2sb = RS.alloc((S,), BF16)
            D2sb = RS.alloc((S,), BF16)
            KT = RS.alloc((3, S), BF16)
            VTs = [RS.alloc((S,), BF16)]
            Vb = RS.alloc((3, NT, 128), BF16)
            EB = RS.alloc((12, 128), F32)
            NRING = 4
            Es = [RS.alloc((512,), F32) for _ in range(NRING)]
            Ps = [RS.alloc((512,), BF16) for _ in range(NRING)]
            recs = [RS.alloc((128,), F32) for _ in range(2)]
            PROJ_BANKS = [0, 1, 2, 5]
            S_BANKS = [0, 1, 2, 7]
            UD_BANKS = [3, 4]
            vtc = [0]

            def project(c0, dst, dstres, dst_fn=None, deint=False):
                wres, wv = load_w_cols(w_in, c0)
                for tb in range(4):
                    b = next_bank(PROJ_BANKS)
                    for kc in range(KC):
                        T.add("pe", I_mm(banks[b][:, :], wv[:, kc, :], hT[:, kc, tb * 512:(tb + 1) * 512],
                                         start=(kc == 0), stop=(kc == KC - 1)),
                              reads=[wres] + [("hT", 4 * tb + q_, hf_) for q_ in range(4) for hf_ in range(2)],
                              writes=[("ps", b)])
                    if deint:
                        evac_copy(evac_eng(), dst.rearrange("p (r i) -> p r i", r=16)[:, :, 32 * tb:32 * tb + 32],
                                  banks[b][:, :].rearrange("p (i r) -> p r i", r=16), [("ps", b)], [dstres + (tb,)])
                    elif dst_fn is None:
                        evac_copy(evac_eng(), dst[:, tb * 512:(tb + 1) * 512], banks[b][:, :], [("ps", b)], [dstres + (tb,)])
                    else:
                        evac_copy(evac_eng(), dst_fn(tb), banks[b][:, :].rearrange("p (a b) -> p a b", a=4),
                                  [("ps", b)], [dstres + (tb,)])

            def project_v(c0, hv, deint=False):
                i = 0
                project(c0, VTs[i], ("VT", i), deint=deint)
                for half in range(2):
                    b = 6 + half
                    bv = banks[b][:].bitcast(BF16)
                    for j in range(8):
                        kb = half * 8 + j
                        T.add("pe", I_tr(bv[:, j * 128:(j + 1) * 128], VTs[i][:, kb * 128:(kb + 1) * 128], ident),
                              reads=([("VT", i, t_) for t_ in range(4)] if deint else [("VT", i, kb // 4)]) + ["ident"],
                              writes=[("ps", b)])
                    evac_copy(evac_eng(), Vb[:, hv, half * 8:(half + 1) * 8, :],
                              bv[:, 0:1024].rearrange("p (a b) -> p a b", a=8), [("ps", b)], [("V", hv)])

            udc = [0]
            stc = [0]
            mulc = [0]

            def mul_eng():
                mulc[0] += 1
                return "pool" if mulc[0] % 3 == 0 else "dve"

            def attention(groups, inject=False):
                steps = []
                for gidx, (sources, sink_col, dstT, dchunk) in enumerate(groups):
                    for qb in range(NT):
                        first = True
                        lst = []
                        for (qsel, ksel, vsel, ebase, omin, omax) in sources:
                            kbs = [kb for kb in range(qb + omin, qb + omax + 1) if 0 <= kb < NT]
                            for c in range(0, len(kbs), 4):
                                ch = kbs[c:c + 4]
                                lst.append([gidx, qb, qsel, ksel, vsel, ebase + (ch[0] - qb - omin), ch, False, False])
                        lst[0][7] = True
                        lst[-1][8] = True
                        steps.extend(lst)
                LAG = 3
                n = len(steps)
                info = {}
                for s_ in range(n + LAG):
                    if s_ < n:
                        gidx, qb, qsel, ksel, vsel, t0, ch, isfirst, islast = steps[s_]
                        k = stc[0] % NRING
                        stc[0] += 1
                        sb = S_BANKS[k]
                        w = len(ch) * 128
                        for i, kb in enumerate(ch):
                            T.add("pe", I_mm(banks[sb][:, i * 128:(i + 1) * 128], KT[:, ksel, kb * 128:(kb + 1) * 128],
                                             QT[:, qsel, qb * 128:(qb + 1) * 128]),
                                  reads=[("KT", ksel, kb // 4), ("QT", qsel, qb // 4)], writes=[("ps", sb)])
                        T.add("act", I_act(Es[k][:, 0:w], banks[sb][:, 0:w], AF.Exp, scale=SCALE),
                              reads=[("ps", sb)], writes=[("E", k)])
                        T.add(mul_eng(), I_tt(Ps[k][:, 0:w], Es[k][:, 0:w],
                                           EB[:, t0:t0 + len(ch), :].rearrange("p a b -> p (a b)"), ALU.mult),
                              reads=[("E", k), "EB"], writes=[("P", k)])
                        info[s_] = k
                    s2 = s_ - LAG
                    if s2 >= 0:
                        gidx, qb, qsel, ksel, vsel, t0, ch, isfirst, islast = steps[s2]
                        k = info[s2]
                        if isfirst:
                            udc[0] += 1
                        ub = UD_BANKS[udc[0] % 2]
                        if isfirst and inject:
                            T.add("pe", I_mm(banks[ub][:, 0:128], ident, U2sb[:, qb * 128:(qb + 1) * 128], True, False),
                                  reads=["ident", "U2sb"], writes=[("ps", ub)])
                            T.add("pe", I_mm(banks[ub][:, 128:256], sel0, D2sb[:, qb * 128:(qb + 1) * 128], False, False),
                                  reads=["sel0", "D2sb"], writes=[("ps", ub)])
                        for i, kb in enumerate(ch):
                            T.add("pe", I_mm(banks[ub][:, 0:128], Vb[:, vsel, kb, :], Ps[k][:, i * 128:(i + 1) * 128],
                                             start=(isfirst and i == 0 and not inject), stop=False),
                                  reads=[("V", vsel), ("P", k)], writes=[("ps", ub)])
                            T.add("pe", I_mm(banks[ub][:, 128:256], ones, Ps[k][:, i * 128:(i + 1) * 128],
                                             start=False, stop=(islast and i == len(ch) - 1)),
                                  reads=["ones", ("P", k)], writes=[("ps", ub)])
                        if islast:
                            sources, sink_col, dstT, dchunk = groups[gidx]
                            r = udc[0] % 2
                            if sink_col is not None:
                                T.add("dve", (lambda e, o=recs[r], i_=banks[ub][:, 128:256], s1=esink[:, sink_col:sink_col + 1]:
                                              e.tensor_scalar_add(out=o, in0=i_, scalar1=s1)),
                                      reads=[("ps", ub), "esink"], writes=[("rec", r)])
                                T.add("dve", I_recip(recs[r], recs[r]), reads=[("rec", r)], writes=[("rec", r)])
                            else:
                                T.add("act", I_act(recs[r], banks[ub][:, 128:256], AF.Ln), reads=[("ps", ub)], writes=[("rec", r)])
                                T.add("act", I_act(recs[r], recs[r], AF.Exp, scale=-1.0), reads=[("rec", r)], writes=[("rec", r)])
                            T.add("dve", I_tt(dstT[:, dchunk, qb * 128:(qb + 1) * 128], banks[ub][:, 0:128], recs[r], ALU.mult),
                                  reads=[("ps", ub), ("rec", r)], writes=[("yT", id(dstT), dchunk, qb)])

            def attention_g2():
                KTd = KT[:, 2, :].rearrange("p (r i) -> p r i", r=16)
                QTd = QT[:, 2, :].rearrange("p (r i) -> p r i", r=16)
                U2v = U2sb.rearrange("p (i r) -> p r i", r=16)
                D2v = D2sb.rearrange("p (i r) -> p r i", r=16)
                kq_reads = [("KT", 2, t_) for t_ in range(4)] + [("QT", 2, t_) for t_ in range(4)]
                LAG = 3
                info = {}
                for s_ in range(4 + LAG):
                    if s_ < 4:
                        k = stc[0] % NRING
                        stc[0] += 1
                        sb = S_BANKS[k]
                        for c in range(4):
                            r = s_ * 4 + c
                            T.add("pe", I_mm(banks[sb][:, c * 128:(c + 1) * 128], KTd[:, r, :], QTd[:, r, :]),
                                  reads=kq_reads, writes=[("ps", sb)])
                        T.add("act", I_act(Es[k][:, :], banks[sb][:, :], AF.Exp, scale=SCALE),
                              reads=[("ps", sb)], writes=[("E", k)])
                        T.add(mul_eng(), I_tt(Ps[k][:, :].rearrange("p (a b) -> p a b", a=4),
                                           Es[k][:, :].rearrange("p (a b) -> p a b", a=4),
                                           EB[:, B_EBASE[2]:B_EBASE[2] + 4, :], ALU.mult),
                              reads=[("E", k), "EB"], writes=[("P", k)])
                        info[s_] = k
                    s2 = s_ - LAG
                    if s2 >= 0:
                        k = info[s2]
                        for half in range(2):
                            udc[0] += 1
                            ub = UD_BANKS[udc[0] % 2]
                            for c2 in range(2):
                                c = half * 2 + c2
                                r = s2 * 4 + c
                                T.add("pe", I_mm(banks[ub][:, c2 * 256:c2 * 256 + 128], Vb[:, 2, r, :],
                                                 Ps[k][:, c * 128:(c + 1) * 128], c2 == 0, False),
                                      reads=[("V", 2), ("P", k)], writes=[("ps", ub)])
                                T.add("pe", I_mm(banks[ub][:, c2 * 256 + 128:c2 * 256 + 256], ones,
                                                 Ps[k][:, c * 128:(c + 1) * 128], False, c2 == 1),
                                      reads=["ones", ("P", k)], writes=[("ps", ub)])
                            r0 = s2 * 4 + half * 2
                            bview = banks[ub][:, :].rearrange("p (a b) -> p a b", a=2)
                            ee = "act" if half == 0 else "dve"
                            evac_copy(ee, U2v[:, r0:r0 + 2, :], bview[:, :, 0:128], [("ps", ub)], [("U2sb", r0)])
                            evac_copy(ee, D2v[:, r0:r0 + 2, :], bview[:, :, 128:256], [("ps", ub)], [("D2sb", r0)])
                T.add("dve", I_memset(recs[0][:, 0:1], 0.0),
                      reads=[("U2sb", r0_) for r0_ in range(0, 16, 2)] + [("D2sb", r0_) for r0_ in range(0, 16, 2)],
                      writes=["U2sb", "D2sb"])

            def attention_gqa(u):
                steps = []
                for qb in range(NT):
                    kbs = [kb for kb in (qb - 1, qb, qb + 1) if 0 <= kb < NT]
                    for kb in kbs:
                        steps.append((qb, kb, kb == kbs[0], kb == kbs[-1]))
                LAG = 3
                n = len(steps)
                info = {}
                for s_ in range(n + LAG):
                    if s_ < n:
                        qb, kb, isfirst, islast = steps[s_]
                        k = stc[0] % NRING
                        stc[0] += 1
                        sb = S_BANKS[k]
                        oi = kb - qb + 1
                        T.add("pe", I_mm(banks[sb][:, :], KT[:, 0, kb * 128:(kb + 1) * 128],
                                         QTa[:, qb, :, :].rearrange("p h c -> p (h c)")),
                              reads=[("KT", 0, kb // 4)] + [("QT", h, qb // 4) for h in range(4)], writes=[("ps", sb)])
                        T.add("act", I_act(Es[k][:, :], banks[sb][:, :], AF.Exp, scale=SCALE),
                              reads=[("ps", sb)], writes=[("E", k)])
                        T.add(mul_eng(), I_tt(Ps[k][:, :], Es[k][:, :],
                                           EB[:, oi * 4:(oi + 1) * 4, :].rearrange("p a b -> p (a b)"), ALU.mult),
                              reads=[("E", k), "EB"], writes=[("P", k)])
                        info[s_] = k
                    s2 = s_ - LAG
                    if s2 >= 0:
                        qb, kb, isfirst, islast = steps[s2]
                        k = info[s2]
                        if isfirst:
                            udc[0] += 1
                        ub = (3, 4)[udc[0] % 2]
                        db = (5, 6)[udc[0] % 2]
                        T.add("pe", I_mm(banks[ub][:, :], Vb[:, 0, kb, :], Ps[k][:, :], isfirst, islast),
                              reads=[("V", 0), ("P", k)], writes=[("ps", ub)])
                        T.add("pe", I_mm(banks[db][:, :], ones, Ps[k][:, :], isfirst, islast),
                              reads=["ones", ("P", k)], writes=[("ps", db)])
                        if islast:
                            r = udc[0] % 2
                            import os
                            if os.environ.get("K_OLDFIN"):
                                for h in range(4):
                                    T.add("dve", (lambda e, o=rec4[r][:, h * 128:(h + 1) * 128], i_=banks[db][:, h * 128:(h + 1) * 128],
                                                  s1=esink[:, 4 * u + h:4 * u + h + 1]: e.tensor_scalar_add(out=o, in0=i_, scalar1=s1)),
                                          reads=[("ps", db), "esink"], writes=[("rec4", r, h)])
                                T.add("dve", I_recip(rec4[r], rec4[r]),
                                      reads=[("rec4", r, h) for h in range(4)], writes=[("rec4", r, h) for h in range(4)])
                            else:
                              for h in range(4):
                                T.add("act", I_act(rec4[r][:, h * 128:(h + 1) * 128], banks[db][:, h * 128:(h + 1) * 128],
                                                   AF.Ln, bias=esink[:, 4 * u + h:4 * u + h + 1]),
                                      reads=[("ps", db), "esink"], writes=[("rec4", r, h)])
                              T.add("act", I_act(rec4[r], rec4[r], AF.Exp, scale=-1.0),
                                  reads=[("rec4", r, h) for h in range(4)], writes=[("rec4", r, h) for h in range(4)])
                            T.add("dve", I_tt(yaT[:, 4 * u:4 * u + 4, qb * 128:(qb + 1) * 128],
                                              banks[ub][:, :].rearrange("p (a b) -> p a b", a=4),
                                              rec4[r].rearrange("p (a b) -> p a b", a=4), ALU.mult),
                                  reads=[("ps", ub)] + [("rec4", r, h) for h in range(4)], writes=[("yaT", u, qb)])

            rec4 = [RS.alloc((512,), F32) for _ in range(2)]
            for ui, (kind, u) in enumerate(a2_units):
                if kind == "A":
                    ntile = 12
                    T.add("sp", I_dma(EB[:, 0:ntile, :], biasA_d[u].rearrange("p (a b) -> p a b", a=ntile)),
                          writes=["EB"], dma_key="EB")
                else:
                    ntile = NB_TILES
                    T.add("sp", I_dma(EB[:, 0:ntile, :], biasB_d[u].rearrange("p (a b) -> p a b", a=ntile)),
                          writes=["EB"], dma_key="EB")
                T.add("act", I_act(EB[:, 0:ntile, :], EB[:, 0:ntile, :], AF.Exp), reads=["EB"], writes=["EB"])
                if kind == "A":
                    project(KA0 + u * 128, KT[:, 0, :], ("KT", 0))
                    project_v(VA0 + u * 128, 0)
                    for h in range(4):
                        project(QA0 + (4 * u + h) * 128, None, ("QT", h),
                                dst_fn=lambda tb, h=h: QTa[:, tb * 4:(tb + 1) * 4, h, :])
                else:
                    for gi in range(3):
                        hc = (gi * 4 + u) * 128
                        project(KB0 + hc, KT[:, gi, :], ("KT", gi), deint=(gi == 2))
                        project_v(VB0 + hc, gi, deint=(gi == 2))
                        project(QB0 + hc, QT[:, gi, :], ("QT", gi), deint=(gi == 2))
                if ui + 1 < len(a2_units):
                    for c0 in unit_cols(*a2_units[ui + 1])[:4]:
                        prefetch_w(w_in, c0)
                if kind == "A":
                    attention_gqa(u)
                else:
                    attention_g2()
                    srcs = [(gi, gi, gi, B_EBASE[gi], B_OFFS[gi][0], B_OFFS[gi][1]) for gi in range(2)]
                    attention([(srcs, None, ybT, u)], inject=True)
            if stop_after != "A2":
                prefetch_w(w_in, GA0)
                prefetch_w(w_in, GB0)
                prefetch_w(w_ba, 0, nk=8)
                prefetch_w(w_bb, 0, nk=4)
            T.barrier()
            if stop_after == "A2":
                dbg["yaT"] = nc.dram_tensor("dbg_yaT", [128, 8 * S], BF16, kind="ExternalOutput").ap()
                dbg["ybT"] = nc.dram_tensor("dbg_ybT", [128, 4 * S], BF16, kind="ExternalOutput").ap()
                T.add("sp", I_dma(dbg["yaT"][:, :].rearrange("p (a b) -> p a b", a=8), yaT), dma_key="dbg")
                T.add("sp", I_dma(dbg["ybT"][:, :].rearrange("p (a b) -> p a b", a=4), ybT), dma_key="dbg2")

        if stop_after not in ("A1", "A2"):
            RS.reset()
            mergedT = RS.alloc((KC, 1024), BF16)
            WoA = RS.alloc((KC, 512), BF16)
            WoB = Wt[:, 0:KC * 512].rearrange("p (k c) -> p k c", k=KC)
            WOS = [(WoA, [("WoA",)]), (WoB, [("w", i) for i in range(4)])]
            sg = [RS.alloc((512,), F32) for _ in range(4)]
            xs = [RS.alloc((512,), F32) for _ in range(3)]
            os_ = [RS.alloc((512,), F32) for _ in range(3)]
            MB = [0, 1, 2, 3, 4, 5, 6, 7]
            cB = [0]
            for th in range(2):
                for fc in range(KC):
                    rga, wga = load_w_cols(w_in, GA0 + fc * 128)
                    rgb, wgb = load_w_cols(w_in, GB0 + fc * 128)
                    rba, wba = load_w_cols(w_ba, fc * 128, nk=8)
                    rbb, wbb = load_w_cols(w_bb, fc * 128, nk=4)
                    bk = [[next_bank(MB) for _ in range(2)] for _ in range(4)]
                    for wi_, (wres_, wv_, src_, nk_) in enumerate(((rga, wga, hT, KC), (rgb, wgb, hT, KC),
                                                                   (rba, wba, yaT, 8), (rbb, wbb, ybT, 4))):
                        for tb2 in range(2):
                            t0 = th * 1024 + tb2 * 512
                            for kc in range(nk_):
                                T.add("pe", I_mm(banks[bk[wi_][tb2]][:, :], wv_[:, kc, :], src_[:, kc, t0:t0 + 512],
                                                 kc == 0, kc == nk_ - 1),
                                      reads=[wres_], writes=[("ps", bk[wi_][tb2])])
                    for tb2 in range(2):
                        bga, bgb, bza, bzb = (bk[w_][tb2] for w_ in range(4))
                        i0 = (cB[0] % 2) * 2
                        cB[0] += 1
                        T.add("act", I_act(sg[i0], banks[bga][:, :], AF.Sigmoid), reads=[("ps", bga)], writes=[("sg", i0)])
                        T.add("act", I_act(sg[i0 + 1], banks[bgb][:, :], AF.Sigmoid), reads=[("ps", bgb)], writes=[("sg", i0 + 1)])
                        T.add("dve", I_tt(sg[i0], sg[i0], banks[bza][:, :], ALU.mult),
                              reads=[("sg", i0), ("ps", bza)], writes=[("sg", i0)])
                        T.add("dve", I_tt(sg[i0 + 1], sg[i0 + 1], banks[bzb][:, :], ALU.mult),
                              reads=[("sg", i0 + 1), ("ps", bzb)], writes=[("sg", i0 + 1)])
                        T.add("dve", I_tt(mergedT[:, fc, tb2 * 512:(tb2 + 1) * 512], sg[i0], sg[i0 + 1], ALU.add),
                              reads=[("sg", i0), ("sg", i0 + 1)], writes=[("mg", fc, tb2)])
                its = [(fb, tt) for fb in range(4) for tt in range(8)]

                def xload(it):
                    fb, tt = its[it]
                    trow = th * 1024 + tt * 128
                    xi = it % 3
                    T.add("sp", I_dma(xs[xi], x_d[trow:trow + 128, fb * 512:(fb + 1) * 512]),
                          writes=[("xs", xi)], dma_key=("xs", xi))

                xload(0)
                xload(1)
                for it, (fb, tt) in enumerate(its):
                    Wo_, wres_ = WOS[fb % 2]
                    if tt == 0:
                        T.add("pool", I_dma(Wo_, w_out[:, fb * 512:(fb + 1) * 512].rearrange("(k p) c -> p k c", p=128)),
                              writes=wres_, dma_key=("Wo", fb % 2))
                    trow = th * 1024 + tt * 128
                    b = next_bank(MB)
                    xi = it % 3
                    if it + 2 < len(its):
                        xload(it + 2)
                    for kc in range(KC):
                        T.add("pe", I_mm(banks[b][:, :], mergedT[:, kc, tt * 128:(tt + 1) * 128], Wo_[:, kc, :],
                                         kc == 0, kc == KC - 1),
                              reads=wres_ + [("mg", kc, tt // 4)], writes=[("ps", b)])
                    T.add("dve", I_tt(os_[xi], banks[b][:, :], xs[xi], ALU.add),
                          reads=[("ps", b), ("xs", xi)], writes=[("os", xi)])
                    T.add("sp", I_dma(x1s[trow:trow + 128, fb * 512:(fb + 1) * 512], os_[xi]),
                          reads=[("os", xi)], writes=[("x1s", trow, fb)], dma_key=("os", xi))
            T.barrier()

        if stop_after not in ("A1", "A2", "B"):
            RP.reset()
            X1 = RP.alloc((8, D), F32)
            hfT = RP.alloc((KC, 1032), BF16)
            pT = RP.alloc((2, 1024), BF16)
            pf = [RP.alloc((256,), F32) for _ in range(2)]
            pb = [RP.alloc((256,), BF16) for _ in range(2)]

            def p_load(th, tt):
                trow = th * 1024 + tt * 128
                pi = tt % 2
                T.add("sp", I_dma(pf[pi], p_d[trow:trow + 128, :]), writes=[("pf", pi)], dma_key=("pf", pi))
                T.add("act", I_acopy(pb[pi], pf[pi]), reads=[("pf", pi)], writes=[("pb", pi)])

            def p_transpose(th, tt):
                if True:
                    pi = tt % 2
                    b = 5
                    bv = banks[b][:].bitcast(BF16)
                    for j in range(2):
                        T.add("pe", I_tr(bv[:, j * 128:(j + 1) * 128], pb[pi][:, j * 128:(j + 1) * 128], ident),
                              reads=[("pb", pi), "ident"], writes=[("ps", b)])
                    T.add("act", I_acopy(pT[:, :, tt * 128:(tt + 1) * 128], bv[:, 0:256].rearrange("p (a b) -> p a b", a=2)),
                          reads=[("ps", b)], writes=[("pT", tt)])
            xnC = [RP.alloc((D,), BF16) for _ in range(2)]
            RS.reset()
            junkC = [RS.alloc((D,), BF16) for _ in range(2)]
            xh = RS.alloc((D,), F32)
            halT = RS.alloc((KC, 128), BF16)
            Wpg = [RS.alloc((KC, 256), BF16) for _ in range(2)]
            Wpp = [RS.alloc((2, 256), BF16) for _ in range(2)]
            sg3 = [RS.alloc((256,), F32) for _ in range(2)]
            tm3 = [RS.alloc((256,), F32) for _ in range(2)]
            gB = RS.alloc((D,), F32)
            ost = [RS.alloc((D,), F32) for _ in range(2)]
            xn1s = xnC
            xn3s = xnC
            junk1 = junkC
            junk3 = junkC

            def c1_load_halo(th_):
                hrow_ = 1024 if th_ == 0 else 896
                T.add("sp", I_dma(xh, x1s[hrow_:hrow_ + 128, :]), writes=["xh"], dma_key="xh")

            def c1_load_tile(th_, tt):
                trow_ = th_ * 1024 + tt * 128
                T.add("sp", I_dma(X1[:, tt, :], x1s[trow_:trow_ + 128, :]), writes=[("X1", tt)], dma_key=("X1", tt))

            for th in range(2):
                hcol = 0 if th == 0 else 127
                if th == 0:
                    c1_load_halo(0)
                    for tt in range(8):
                        c1_load_tile(0, tt)
                if th == 0:
                    rms_stats(xh, "xh", 0, junk1)
                for tt in range(4):
                    rms_stats(X1[:, tt, :], ("X1", tt), tt + 1, junk1)
                rms_rstd(0 if th == 0 else 1, 5)
                if th == 0:
                    rms_apply(xh, "xh", 1, xn1s[0], ("xnC", 0), 0,
                              lambda half: halT[:, half * 8:(half + 1) * 8, :], ("halT",), eng="act")
                    T.add("dve", I_copy(hfT[:, :, 1024:1025], halT[:, :, hcol:hcol + 1]), reads=[("halT", 0), ("halT", 1)],
                          writes=[("hfT", -1)])
                for tt in range(4, 8):
                    rms_stats(X1[:, tt, :], ("X1", tt), tt + 1, junk1)
                rms_rstd(5, 9)
                for tt in range(8):
                    rms_apply(X1[:, tt, :], ("X1", tt), 1, xn1s[(tt + 1) % 2], ("xnC", (tt + 1) % 2), tt + 1,
                              lambda half, tt=tt: hfT[:, half * 8:(half + 1) * 8, tt * 128:(tt + 1) * 128],
                              ("hfT", tt), eng="act")
                for c_ in range(2):
                    prefetch_w(w_g, c_ * 128)
                    prefetch_w(w_u, c_ * 128)
                T.barrier()
                RS.reset()
                aT = RS.alloc((11, 1024), BF16)
                Wd = [RS.alloc((11, 512), BF16) for _ in range(2)]
                G = [RS.alloc((1026,), F32) for _ in range(2)]
                ACC = [RS.alloc((1024,), F32) for _ in range(2)]
                GEL = [RS.alloc((1024,), F32) for _ in range(2)]
                FB = [0, 1, 2, 3, 4, 5, 6]
                cC = [0]
                for qd in range(4):
                    for cl in range(11):
                        c = qd * 11 + cl
                        rg_, wg_ = load_w_cols(w_g, c * 128)
                        ru_, wu_ = load_w_cols(w_u, c * 128)
                        gi = cC[0] % 2
                        cC[0] += 1
                        bg = [next_bank(FB), next_bank(FB)]
                        for tb2 in range(2):
                            for kc in range(KC):
                                T.add("pe", I_mm(banks[bg[tb2]][:, :], wg_[:, kc, :],
                                                 hfT[:, kc, tb2 * 512:(tb2 + 1) * 512], kc == 0, kc == KC - 1),
                                      reads=[rg_], writes=[("ps", bg[tb2])])
                        if th == 0:
                            for kc in range(KC):
                                T.add("pe", I_mm(banks[7][:, 0:1], wg_[:, kc, :], hfT[:, kc, 1024:1025], kc == 0, kc == KC - 1),
                                      reads=[rg_], writes=[("ps", 7)])
                        bu = [next_bank(FB), next_bank(FB)]
                        for tb2 in range(2):
                            for kc in range(KC):
                                T.add("pe", I_mm(banks[bu[tb2]][:, :], wu_[:, kc, :],
                                                 hfT[:, kc, tb2 * 512:(tb2 + 1) * 512], kc == 0, kc == KC - 1),
                                      reads=[ru_], writes=[("ps", bu[tb2])])
                        Gt = G[gi]
                        T.add("act", I_acopy(Gt[:, 1:513], banks[bg[0]][:, :]), reads=[("ps", bg[0])], writes=[("G", gi, 0)])
                        T.add("act", I_acopy(Gt[:, 513:1025], banks[bg[1]][:, :]), reads=[("ps", bg[1])], writes=[("G", gi, 1)])
                        if th == 0:
                            T.add("act", I_acopy(Gt[:, 1025:1026], banks[7][:, 0:1]), reads=[("ps", 7)], writes=[("G", gi, 2)])
                            T.add("dve", I_memset(Gt[:, 0:1], 0.0), writes=[("G", gi, 3)])
                            T.add("dve", I_copy(gstash[:, c:c + 1], Gt[:, 1024:1025]), reads=[("G", gi, 1)], writes=[("gstash", c)])
                        else:
                            T.add("dve", I_copy(Gt[:, 0:1], gstash[:, c:c + 1]), reads=[("gstash", c)], writes=[("G", gi, 2)])
                            T.add("dve", I_memset(Gt[:, 1025:1026], 0.0), writes=[("G", gi, 3)])
                        gres = [("G", gi, k_) for k_ in range(4)]
                        A_ = ACC[gi]
                        T.add("dve", I_ts(A_, Gt[:, 1:1025], convp[:, 1, c:c + 1], convp[:, 3, c:c + 1], ALU.mult, ALU.add),
                              reads=gres + ["convp"], writes=[("ACC", gi)])
                        T.add("dve", I_stt(A_, Gt[:, 0:1024], convp[:, 0, c:c + 1], A_, ALU.mult, ALU.add),
                              reads=gres + [("ACC", gi)], writes=[("ACC", gi)])
                        T.add("dve", I_stt(A_, Gt[:, 2:1026], convp[:, 2, c:c + 1], A_, ALU.mult, ALU.add),
                              reads=gres + [("ACC", gi)], writes=[("ACC", gi)])
                        T.add("act", I_act(GEL[gi], A_, AF.Gelu_apprx_tanh), reads=[("ACC", gi)], writes=[("GEL", gi)])
                        for tb2 in range(2):
                            T.add("dve", I_tt(aT[:, cl, tb2 * 512:(tb2 + 1) * 512], GEL[gi][:, tb2 * 512:(tb2 + 1) * 512],
                                              banks[bu[tb2]][:, :], ALU.mult),
                                  reads=[("GEL", gi), ("ps", bu[tb2])], writes=[("aT", cl, tb2)])
                        if c < 8:
                            p_load(th, c)
                        if 1 <= c < 9:
                            p_transpose(th, c - 1)
                    for fb in range(4):
                        wi = cC[0] % 2
                        cC[0] += 1
                        T.add("pool", I_dma(Wd[wi], w_d[qd * 11 * 128:(qd + 1) * 11 * 128, fb * 512:(fb + 1) * 512]
                                            .rearrange("(c p) f -> p c f", p=128)),
                              writes=[("Wd", wi)], dma_key=("Wd", wi))
                        for t2 in range(2):
                            bd = [next_bank(FB) for _ in range(4)]
                            for cl in range(11):
                                for t4 in range(4):
                                    tt = t2 * 4 + t4
                                    T.add("pe", I_mm(banks[bd[t4]][:, :], aT[:, cl, tt * 128:(tt + 1) * 128], Wd[wi][:, cl, :],
                                                     cl == 0, cl == 10),
                                          reads=[("Wd", wi), ("aT", cl, tt // 4)], writes=[("ps", bd[t4])])
                            for t4 in range(4):
                                tt = t2 * 4 + t4
                                xsl = X1[:, tt, fb * 512:(fb + 1) * 512]
                                T.add("dve", I_tt(xsl, banks[bd[t4]][:, :], xsl, ALU.add),
                                      reads=[("ps", bd[t4]), ("X1", tt)], writes=[("X1", tt)])
                T.barrier()
                T.add("sp", I_dma(gB, gfin_d.partition_broadcast(128)), writes=["gB"], dma_key="gB")
                def c3_apply(tt):
                    rms_apply(X1[:, tt, :], ("X1", tt), 2, xn3s[tt % 2], ("xnC", tt % 2), tt,
                              lambda half, tt=tt: hfT[:, half * 8:(half + 1) * 8, tt * 128:(tt + 1) * 128],
                              ("hfT", tt), eng="act")

                for tt in range(4):
                    rms_stats(X1[:, tt, :], ("X1", tt), tt, junk3)
                rms_rstd(0, 4)
                for tt in range(4):
                    c3_apply(tt)
                    rms_stats(X1[:, 4 + tt, :], ("X1", 4 + tt), 4 + tt, junk3)
                rms_rstd(4, 8)
                for tt in range(4, 8):
                    c3_apply(tt)
                PB_ = [0, 1, 2, 3, 4]
                c3 = [0]

                def c3_final(tt):
                    rms_stats(X1[:, tt, :], ("X1", tt), 16 + tt, junk3)
                    if tt % 4 != 3:
                        return
                    rms_rstd(16 + tt - 3, 16 + tt + 1)
                    for t_ in range(tt - 3, tt + 1):
                        trow = th * 1024 + t_ * 128
                        oi = t_ % 2
                        T.add("dve", I_stt(ost[oi], X1[:, t_, :], stat[:, 2, 16 + t_:17 + t_], gB, ALU.mult, ALU.mult),
                              reads=[("X1", t_), ("rs", 16 + t_), "gB"], writes=[("ost", oi)])
                        T.add("sp", I_dma(out_d[trow:trow + 128, :], ost[oi]), reads=[("ost", oi)], writes=[("out", trow)],
                              dma_key=("ost", oi))
                        if th == 0:
                            c1_load_tile(1, t_)

                def c3_wload(fb):
                    wi = fb % 2
                    T.add("pool", I_dma(Wpg[wi], w_pg[:, fb * 256:(fb + 1) * 256].rearrange("(k p) c -> p k c", p=128)),
                          writes=[("Wpg", wi)], dma_key=("Wpg", wi))
                    T.add("pool", I_dma(Wpp[wi], w_pp[:, fb * 256:(fb + 1) * 256].rearrange("(k p) c -> p k c", p=128)),
                          writes=[("Wpp", wi)], dma_key=("Wpp", wi))

                order = [(fb, tt) for fb in range(6) for tt in range(8)]
                order += [(fb, tt) for fb in (6, 7) for tt in range(4)] + [(fb, tt) for fb in (6, 7) for tt in range(4, 8)]
                loaded = set()
                for (fb, tt) in order:
                    wi = fb % 2
                    if fb not in loaded:
                        c3_wload(fb)
                        loaded.add(fb)
                        if fb == 6:
                            c3_wload(7)
                            loaded.add(7)
                    if True:
                        ba = next_bank(PB_)
                        for kc in range(KC):
                            T.add("pe", I_mm(banks[ba][:, 0:256], hfT[:, kc, tt * 128:(tt + 1) * 128], Wpg[wi][:, kc, :],
                                             kc == 0, kc == KC - 1),
                                  reads=[("Wpg", wi), ("hfT", tt, 0), ("hfT", tt, 1)], writes=[("ps", ba)])
                        for kc in range(2):
                            T.add("pe", I_mm(banks[ba][:, 256:512], pT[:, kc, tt * 128:(tt + 1) * 128], Wpp[wi][:, kc, :],
                                             False, kc == 1),
                                  reads=[("Wpp", wi), ("pT", tt)], writes=[("ps", ba)])
                        i3 = c3[0] % 2
                        c3[0] += 1
                        T.add("act", I_act(sg3[i3], banks[ba][:, 0:256], AF.Sigmoid), reads=[("ps", ba)], writes=[("sg3", i3)])
                        T.add("dve", I_tt(tm3[i3], sg3[i3], banks[ba][:, 256:512], ALU.mult),
                              reads=[("sg3", i3), ("ps", ba)], writes=[("tm3", i3)])
                        xsl = X1[:, tt, fb * 256:(fb + 1) * 256]
                        T.add("dve", I_tt(xsl, xsl, tm3[i3], ALU.add), reads=[("tm3", i3), ("X1", tt)], writes=[("X1", tt)])
                        if fb == 7:
                            c3_final(tt)
                if th == 1:
                    T.barrier()

        T.barrier()
        T.finalize()

        csem = {e: es.enter_context(nc.semaphore("c_" + e)) for e in ["pe", "act", "dve", "pool"]}
        dsem = {}
        for i, k in enumerate(sorted(T.dma_cnt.keys(), key=str)):
            dsem[k] = es.enter_context(nc.semaphore("d%d" % i))
        with nc.Block() as block:
            @block.tensor
            def _(e):
                T.emit("pe", e, csem, dsem)

            @block.scalar
            def _(e):
                T.emit("act", e, csem, dsem)

            @block.vector
            def _(e):
                T.emit("dve", e, csem, dsem)

            @block.gpsimd
            def _(e):
                T.emit("pool", e, csem, dsem)

            @block.sync
            def _(e):
                T.emit("sp", e, csem, dsem)
    return nc, dbg


def _t5_bucket(rel):
    half = N_BUCKETS // 2
    max_exact = half // 2
    n = np.abs(rel)
    side = np.where(rel > 0, half, 0)
    nf = np.maximum(n, 1).astype(np.float32)
    large = max_exact + (np.log(nf / np.float32(max_exact)) / np.float32(math.log(MAX_DISTANCE / max_exact))
                         * np.float32(half - max_exact)).astype(np.int32)
    large = np.minimum(large, half - 1)
    return side + np.where(n < max_exact, n, large)


def _bias_index_tiles():
    k = np.arange(128)[:, None]
    q = np.arange(128)[None, :]
    a_idx = np.zeros((3, 128, 128), np.int64)
    for oi, o in enumerate((-1, 0, 1)):
        rel = o * 128 + k - q
        valid = np.abs(rel) <= 128
        a_idx[oi] = np.where(valid, _t5_bucket(rel), N_BUCKETS)
    b_idx = np.zeros((NB_TILES, 128, 128), np.int64)
    for gi in range(2):
        window, dil = B_PAT[gi]
        omin, omax = B_OFFS[gi]
        for o in range(omin, omax + 1):
            rel = o * 128 + k - q
            valid = (rel % dil == 0) & (np.abs(rel) <= (window // 2))
            b_idx[B_EBASE[gi] + o - omin] = np.where(valid, _t5_bucket(rel), N_BUCKETS)
    window, dil = B_PAT[2]
    rel_sub = k - q
    valid = np.abs(rel_sub) <= (window // (2 * dil))
    b_idx[B_EBASE[2]] = np.where(valid, _t5_bucket(rel_sub * dil), N_BUCKETS)
    return a_idx, b_idx


_NC_CACHE = {}


def _host_inputs(inputs):
    f = lambda a: np.ascontiguousarray(np.asarray(a, dtype=np.float32))
    table = f(inputs["rel_bias_table"])
    table_ext = np.concatenate([table, np.full((1, table.shape[1]), NEG, np.float32)], axis=0)
    a_idx, b_idx = _bias_index_tiles()
    biasA = np.zeros((2, 128, 12, 128), np.float32)
    for g in range(2):
        for h in range(4):
            for oi in range(3):
                biasA[g, :, oi * 4 + h, :] = table_ext[a_idx[oi], 4 * g + h]
    biasB = np.zeros((4, 128, NB_TILES, 128), np.float32)
    for j in range(4):
        for gi in range(3):
            if gi < 2:
                nt_ = B_OFFS[gi][1] - B_OFFS[gi][0] + 1
                for t in range(B_EBASE[gi], B_EBASE[gi] + nt_):
                    biasB[j, :, t, :] = table_ext[b_idx[t], 8 + gi * 4 + j]
            else:
                for t in range(B_EBASE[2], B_EBASE[2] + 4):
                    biasB[j, :, t, :] = table_ext[b_idx[B_EBASE[2]], 8 + gi * 4 + j]
    gains = np.stack([f(inputs["attn_norm"])[0], f(inputs["ffn_norm"])[0], f(inputs["ple_norm"])[0]], 0)
    gfm = np.ascontiguousarray(gains.reshape(3, 16, 128).transpose(2, 0, 1)).reshape(128, 48)
    cw = np.concatenate([f(inputs["conv_w"])[0], f(inputs["conv_b"])], 0)
    convp = np.ascontiguousarray(cw.reshape(4, NCC, 128).transpose(2, 0, 1)).reshape(128, 4 * NCC)
    sinkb = np.ascontiguousarray(np.broadcast_to(f(inputs["sink_a"])[0][None, :], (128, 8)))
    shared = {
        "w_in": f(inputs["w_in"])[0], "w_ba": f(inputs["w_branch_a"])[0], "w_bb": f(inputs["w_branch_b"])[0],
        "w_out": f(inputs["w_out"])[0], "w_g": f(inputs["w_ffn_gate"])[0], "w_u": f(inputs["w_ffn_up"])[0],
        "w_d": f(inputs["w_ffn_down"])[0], "w_pg": f(inputs["w_ple_gate"])[0], "w_pp": f(inputs["w_ple_proj"])[0],
        "gfm": gfm, "gfin": f(inputs["final_norm"]).reshape(1, D), "convp": convp, "sinkb": sinkb,
        "biasA": biasA.reshape(2, 128, 12 * 128), "biasB": biasB.reshape(4, 128, NB_TILES * 128),
        "ident": np.eye(128, dtype=np.float32).astype(ml_dtypes.bfloat16),
    }
    return shared


def kernel(**inputs):
    x = np.asarray(inputs["x"], dtype=np.float32)
    p = np.asarray(inputs["p"], dtype=np.float32)
    shared = _host_inputs(inputs)
    if "nc" not in _NC_CACHE:
        _NC_CACHE["nc"] = build_nc()[0]
    nc = _NC_CACHE["nc"]
    n = x.shape[0]
    in_maps = []
    for b in range(n):
        m = dict(shared)
        m["x"] = np.ascontiguousarray(x[b])
        m["p"] = np.ascontiguousarray(p[0, b])
        in_maps.append(m)
    res = run_bass_kernel_spmd(nc, in_maps, core_ids=list(range(n)))
    return np.stack([np.asarray(r["out"], dtype=np.float32) for r in res.results], 0)
```
